# Optimizing a Trainium2 kernel written in Bass

```python
import math
import jax, jax.numpy as jnp
from jax import lax
import numpy as np

D_MODEL = 2048
BATCH = 2
SEQ = 4096
DEPTH = 1

N_META = 16
D_ATTN = D_MODEL // 2
D_LRU = D_MODEL // 2
N_HEADS = 8
HEAD_DIM = D_ATTN // (2 * N_HEADS)
V_DIM = 2 * HEAD_DIM
N_LRU_BLOCKS = 8
LRU_BLOCK = D_LRU // N_LRU_BLOCKS
CONV_WIDTH = 4
LRU_C = 8.0
N_BUCKETS = 32
MAX_DISTANCE = 128
Q_BLOCK = 128
NORM_EPS = 1e-6
SUBLN_EPS = 1e-5
D_IN = 4 * D_ATTN + 2 * D_LRU
NEG_INF = -1e30

kernel_name = "hymba_diffattn_rglru_block"


def rmsnorm(x, g, eps):
    xf = x.astype(jnp.float32)
    y = xf * lax.rsqrt(jnp.mean(xf * xf, axis=-1, keepdims=True) + eps)
    return (y * g.astype(jnp.float32)).astype(x.dtype)


def lambda_init_fn(layer_idx):
    return 0.8 - 0.6 * math.exp(-0.3 * layer_idx)


def t5_causal_bucket(dist):
    max_exact = N_BUCKETS // 2
    d = jnp.maximum(dist, 0)
    large = max_exact + (jnp.log(jnp.maximum(d, 1).astype(jnp.float32) / max_exact)
                         / math.log(MAX_DISTANCE / max_exact)
                         * (N_BUCKETS - max_exact)).astype(jnp.int32)
    large = jnp.minimum(large, N_BUCKETS - 1)
    return jnp.where(d < max_exact, d, large)


def diff_attention(q, k, v, dist_bias, lam):
    B, T = q.shape[0], q.shape[1]
    kpos = jnp.arange(T, dtype=jnp.int32)
    scale = HEAD_DIM ** -0.5
    k1, k2 = k[..., 0, :], k[..., 1, :]

    def attend(qb, qpos):
        rel = qpos[:, None] - kpos[None, :]
        bias = jnp.transpose(dist_bias[jnp.clip(rel, 0, T - 1)], (2, 0, 1))
        causal = rel >= 0

        def probs(qh, kh):
            s = jnp.einsum('blhd,bthd->bhlt', qh, kh).astype(jnp.float32) * scale + bias
            s = jnp.where(causal, s, NEG_INF)
            return jax.nn.softmax(s, axis=-1)

        p = probs(qb[..., 0, :], k1) - lam * probs(qb[..., 1, :], k2)
        return jnp.einsum('bhlt,bthe->blhe', p.astype(v.dtype), v)

    out_meta = attend(q[:, :N_META], jnp.arange(N_META, dtype=jnp.int32))
    s_real = T - N_META
    nb = s_real // Q_BLOCK
    qr = jnp.moveaxis(q[:, N_META:].reshape(B, nb, Q_BLOCK, N_HEADS, 2, HEAD_DIM), 1, 0)
    pos = (N_META + jnp.arange(s_real, dtype=jnp.int32)).reshape(nb, Q_BLOCK)
    out_real = lax.map(lambda a: attend(a[0], a[1]), (qr, pos))
    out_real = jnp.moveaxis(out_real, 0, 1).reshape(B, s_real, N_HEADS, V_DIM)
    return jnp.concatenate([out_meta, out_real], axis=1)


def rg_lru_branch(u, conv_w, conv_b, w_a, b_a, w_x, b_x, lru_lambda):
    B, T, C = u.shape
    u = lax.conv_general_dilated(u, conv_w[:, None, :], window_strides=(1,),
                                 padding=[(CONV_WIDTH - 1, 0)],
                                 dimension_numbers=('NWC', 'WIO', 'NWC'),
                                 feature_group_count=C) + conv_b
    ub = u.reshape(B, T, N_LRU_BLOCKS, LRU_BLOCK)
    r = jax.nn.sigmoid((jnp.einsum('btnc,ncd->btnd', ub, w_a).reshape(B, T, C) + b_a).astype(jnp.float32))
    i = jax.nn.sigmoid((jnp.einsum('btnc,ncd->btnd', ub, w_x).reshape(B, T, C) + b_x).astype(jnp.float32))
    log_a = -LRU_C * r * jax.nn.softplus(-lru_lambda.astype(jnp.float32))
    a = jnp.exp(log_a)
    mult = jnp.sqrt(-jnp.expm1(2.0 * log_a))
    mult = jnp.where(jnp.arange(T)[None, :, None] == 0, 1.0, mult)
    b = mult * i * u.astype(jnp.float32)

    def combine(c1, c2):
        a1, b1 = c1
        a2, b2 = c2
        return a1 * a2, a2 * b1 + b2

    _, h = lax.associative_scan(combine, (a, b), axis=1)
    return h.astype(u.dtype)


def setup_inputs(seed: int = 0) -> dict:
    key = jax.random.key(seed)
    ks = jax.random.split(key, 20)
    f32 = jnp.float32
    a0 = jax.random.uniform(ks[10], (DEPTH, D_LRU), f32, 0.9, 0.999)
    s0 = a0 ** (1.0 / LRU_C)
    return {
        "x": jax.random.normal(ks[0], (BATCH, SEQ, D_MODEL), f32),
        "meta_tokens": jax.random.normal(ks[1], (N_META, D_MODEL), f32),
        "rel_bias": 0.2 * jax.random.normal(ks[2], (N_BUCKETS, N_HEADS), f32),
        "norm_g": 1.0 + 0.02 * jax.random.normal(ks[3], (DEPTH, D_MODEL), f32),
        "w_in": jax.random.normal(ks[4], (DEPTH, D_MODEL, D_IN), f32) * D_MODEL ** -0.5,
        "conv_w": jax.random.normal(ks[5], (DEPTH, CONV_WIDTH, D_LRU), f32) * CONV_WIDTH ** -0.5,
        "conv_b": 0.01 * jax.random.normal(ks[6], (DEPTH, D_LRU), f32),
        "w_a": jax.random.normal(ks[7], (DEPTH, N_LRU_BLOCKS, LRU_BLOCK, LRU_BLOCK), f32) * LRU_BLOCK ** -0.5,
        "b_a": 0.01 * jax.random.normal(ks[8], (DEPTH, D_LRU), f32),
        "w_x": jax.random.normal(ks[9], (DEPTH, N_LRU_BLOCKS, LRU_BLOCK, LRU_BLOCK), f32) * LRU_BLOCK ** -0.5,
        "b_x": 0.01 * jax.random.normal(ks[11], (DEPTH, D_LRU), f32),
        "lru_lambda": jnp.log(s0 / (1.0 - s0)),
        "lam_q1": 0.1 * jax.random.normal(ks[12], (DEPTH, HEAD_DIM), f32),
        "lam_k1": 0.1 * jax.random.normal(ks[13], (DEPTH, HEAD_DIM), f32),
        "lam_q2": 0.1 * jax.random.normal(ks[14], (DEPTH, HEAD_DIM), f32),
        "lam_k2": 0.1 * jax.random.normal(ks[15], (DEPTH, HEAD_DIM), f32),
        "subln_g": 1.0 + 0.02 * jax.random.normal(ks[16], (DEPTH, V_DIM), f32),
        "w_out": jax.random.normal(ks[17], (DEPTH, D_ATTN + D_LRU, D_MODEL), f32) * (D_ATTN + D_LRU) ** -0.5,
        "final_g": 1.0 + 0.02 * jax.random.normal(ks[18], (D_MODEL,), f32),
    }


def reference(x, meta_tokens, rel_bias, norm_g, w_in, conv_w, conv_b, w_a, b_a, w_x, b_x,
              lru_lambda, lam_q1, lam_k1, lam_q2, lam_k2, subln_g, w_out, final_g):
    B = x.shape[0]
    meta = jnp.broadcast_to(meta_tokens[None].astype(x.dtype), (B, N_META, D_MODEL))
    x = jnp.concatenate([meta, x], axis=1)
    T = x.shape[1]
    dist_bias = rel_bias.astype(jnp.float32)[t5_causal_bucket(jnp.arange(T, dtype=jnp.int32))]
    splits = [D_ATTN, 2 * D_ATTN, 3 * D_ATTN, 4 * D_ATTN, 4 * D_ATTN + D_LRU]

    for l in range(DEPTH):
        lam_init = lambda_init_fn(l)
        h = rmsnorm(x, norm_g[l], NORM_EPS)
        proj = jnp.einsum('btd,de->bte', h, w_in[l])
        q, k, v, g_att, u, g_lru = jnp.split(proj, splits, axis=-1)
        q = q.reshape(B, T, N_HEADS, 2, HEAD_DIM)
        k = k.reshape(B, T, N_HEADS, 2, HEAD_DIM)
        v = v.reshape(B, T, N_HEADS, V_DIM)
        lam = (jnp.exp(jnp.sum(lam_q1[l].astype(jnp.float32) * lam_k1[l].astype(jnp.float32)))
               - jnp.exp(jnp.sum(lam_q2[l].astype(jnp.float32) * lam_k2[l].astype(jnp.float32)))
               + lam_init)
        att = diff_attention(q, k, v, dist_bias, lam)
        att = (rmsnorm(att, subln_g[l], SUBLN_EPS) * (1.0 - lam_init)).reshape(B, T, D_ATTN)
        att = att * jax.nn.silu(g_att)
        rec = rg_lru_branch(u, conv_w[l], conv_b[l], w_a[l], b_a[l], w_x[l], b_x[l], lru_lambda[l])
        rec = rec * jax.nn.silu(g_lru)
        mixed = jnp.concatenate([att, rec], axis=-1)
        x = x + jnp.einsum('bte,ed->btd', mixed, w_out[l])

    return rmsnorm(x, final_g, NORM_EPS)[:, N_META:]
```

```python
import math
from contextlib import ExitStack
import numpy as np
import concourse.bass as bass
import concourse.mybir as mybir
from concourse.bass_utils import run_bass_kernel_spmd

F32 = mybir.dt.float32
BF16 = mybir.dt.bfloat16
ALU = mybir.AluOpType
AF = mybir.ActivationFunctionType

D = 2048
NM = 16
S = 4096
T = S + NM
NT = 9
ENGS = ("pe", "act", "dve", "pool", "sp")
LAM_INIT = 0.8 - 0.6 * math.exp(0.0)
MASKV = -30000.0
DEBUG = False


class Prog:
    def __init__(self):
        self.ops = {e: [] for e in ENGS}
        self.cnt = {}
        self.waited = {e: {} for e in ENGS}
        self.lastw = {}
        self.readers = {}

    def _emit(self, eng, fn, reads, writes, sem, inc, ninst):
        deps = []
        for k in reads:
            if k in self.lastw:
                deps.append(self.lastw[k])
        for k in writes:
            if k in self.lastw:
                deps.append(self.lastw[k])
            deps.extend(self.readers.get(k, ()))
        waits = {}
        for (s, v) in deps:
            if s == "pe" and eng == "pe":
                continue
            if self.waited[eng].get(s, 0) >= v:
                continue
            if waits.get(s, 0) < v:
                waits[s] = v
        for s, v in waits.items():
            self.waited[eng][s] = v
        self.cnt[sem] = self.cnt.get(sem, 0) + inc * ninst
        tk = (sem, self.cnt[sem])
        self.ops[eng].append((sorted(waits.items()), fn, sem, inc))
        for k in writes:
            self.lastw[k] = tk
            self.readers[k] = []
        for k in reads:
            self.readers.setdefault(k, []).append(tk)
        return tk

    def op(self, eng, fn, reads=(), writes=()):
        return self._emit(eng, fn, reads, writes, eng, 1, 1)

    def dma(self, fn, sem, n, reads=(), writes=()):
        return self._emit("sp", fn, reads, writes, sem, 16, n)

    def dma_pool(self, fn, sem, n, reads=(), writes=()):
        return self._emit("pool", fn, reads, writes, sem, 16, n)

    def cc(self, fn, sem, reads=(), writes=()):
        return self._emit("pool", fn, reads, writes, sem, 1, 1)

    def barrier(self, skip_prefix=None):
        allk = [(s_, v_) for s_, v_ in self.cnt.items() if not (skip_prefix and s_.startswith(skip_prefix))]
        for e in ENGS:
            waits = []
            for s, v in allk:
                if self.waited[e].get(s, 0) < v and not (s == e == "pe"):
                    waits.append((s, v))
                    self.waited[e][s] = v
            if waits:
                self.ops[e].append((sorted(waits), None, None, 0))


def build_program():
    nc = bass.Bass("TRN2", target_bir_lowering=False)
    P = Prog()

    def din(name, shape, dt=F32):
        return nc.dram_tensor(name, list(shape), dt, kind="ExternalInput").ap()

    xT = din("xT", [D, T])
    w_in = din("w_in", [D, 1536])
    w_out = din("w_out", [D, 512])
    x_res = din("x_res", [S, 512])
    fg_in = din("fg", [128, 512])
    smalls_in = din("smalls", [128, 40])
    lamv_in = din("lamv", [128, 256])
    wax_in = din("wax", [128, 512])
    biast_in = din("biast", [128, 2048])
    ident_in = din("ident", [128, 128])
    out = nc.dram_tensor("out", [S, 512], F32, kind="ExternalOutput").ap()
    agin = [nc.dram_tensor("agin%d" % i, [256, 512], BF16).ap() for i in range(8)]
    agout = [nc.dram_tensor("agout%d" % i, [1024, 512], BF16).ap() for i in range(8)]
    aginl = [nc.dram_tensor("aginl%d" % i, [256, 512], BF16).ap() for i in range(8)]
    agoutl = [nc.dram_tensor("agoutl%d" % i, [1024, 512], BF16).ap() for i in range(8)]
    sqin = [nc.dram_tensor("sqin%d" % i, [128, 4], F32).ap() for i in range(8)]
    sqout = [nc.dram_tensor("sqout%d" % i, [512, 4], F32).ap() for i in range(8)]
    dbg = {}
    if DEBUG:
        dbg["qt"] = nc.dram_tensor("dbg_qt", [128, 2 * S], BF16, kind="ExternalOutput").ap()
        dbg["kt"] = nc.dram_tensor("dbg_kt", [128, 2 * T], BF16, kind="ExternalOutput").ap()
        dbg["vv"] = nc.dram_tensor("dbg_vv", [128, 33 * 256], BF16, kind="ExternalOutput").ap()
        dbg["sga"] = nc.dram_tensor("dbg_sga", [128, 2 * S], BF16, kind="ExternalOutput").ap()
        dbg["ag0"] = nc.dram_tensor("dbg_ag0", [1024, 512], BF16, kind="ExternalOutput").ap()

    ARENA_BYTES = 207 * 1024
    arena = nc.alloc_sbuf_tensor("arena", [128, ARENA_BYTES // 2], BF16)
    ptr = {"p": 0, 1: 0, 2: 0}

    def carve(phase, shape, dt):
        nel = 1
        for s_ in shape:
            nel *= s_
        nb = nel * (4 if dt == F32 else 2)
        nb_al = (nb + 63) // 64 * 64
        if phase == "p":
            off = ptr["p"]
            ptr["p"] += nb_al
            ptr[1] = ptr[2] = ptr["p"]
        else:
            off = ptr[phase]
            ptr[phase] += nb_al
        assert off + nb_al <= ARENA_BYTES, ("SBUF overflow", phase, off + nb_al)
        v = arena[:, off // 2: off // 2 + nb // 2]
        if dt == F32:
            v = v.bitcast(F32)
        if len(shape) == 2:
            return v.rearrange("p (a b) -> p a b", a=shape[0])
        if len(shape) == 3:
            return v.rearrange("p (a b c) -> p a b c", a=shape[0], b=shape[1])
        return v

    QT = carve("p", [2, S], BF16)
    KT = carve("p", [2, T], BF16)
    VV = carve("p", [33, 256], BF16)
    SGA = carve("p", [2, S], BF16)
    IDB = carve("p", [128], BF16)
    ONES = carve("p", [128], BF16)
    ONESM = carve("p", [128], BF16)
    ONESD = carve("p", [128], BF16)
    BIAS = carve("p", [2, 4, 512], BF16)
    SM = carve("p", [64], F32)
    WAX = carve("p", [4, 128], BF16)
    HST = carve("p", [2], F32)
    WP = carve(1, [16, 1536], BF16)
    XS = [carve(1, [4, 512], F32) for _ in range(2)]
    HB = [carve(1, [16, 512], BF16) for _ in range(2)]
    XSQ = carve(1, [4, 512], BF16)
    RSTD = [carve(1, [512], F32) for _ in range(2)]
    U = carve(1, [2, 520], F32)
    TT = [carve(1, [512], F32) for _ in range(6)]
    UCBF = carve(1, [512], BF16)
    SGL = carve(1, [2, 512], BF16)
    VT = carve(1, [512], BF16)
    LO = [carve(1, [2, 512], BF16) for _ in range(2)]
    TT6 = carve(1, [512], F32)
    UCBF2 = carve(1, [512], BF16)
    LAMV = XSQ[:, 0, :].bitcast(F32)
    WO = carve(2, [16, 512], BF16)
    MIXG = [carve(2, [16, 512], BF16) for _ in range(2)]
    XRES = carve(2, [4, 512], F32)
    YB = [carve(2, [4, 512], F32) for _ in range(2)]
    FG = carve(2, [512], F32)
    NPT = 6
    PT = [carve(2, [2, 256], BF16) for _ in range(NPT)]
    RL = carve(2, [512], F32)
    ON = carve(2, [512], F32)
    DIFF = carve(2, [256], F32)
    T1 = carve(2, [256], F32)
    R2 = carve(2, [256], F32)
    SQ = carve(2, [256], BF16)
    MIX = [carve(2, [2, 512], BF16) for _ in range(2)]
    SSQ = carve(2, [4], F32)
    SQG = carve(2, [4, 4], F32)
    TOT = carve(2, [4], F32)
    RS = carve(2, [4], F32)
    JUNK = carve(2, [512], F32)
    WOST32 = carve(2, [16, 512], F32)
    GS = [carve(2, [512], BF16) for _ in range(2)]
    QBD = [carve(2, [512], BF16) for _ in range(2)]

    psb = [nc.alloc_psum_tensor("psb%d" % i, [128, 512], F32) for i in range(4)]
    pss = [nc.alloc_psum_tensor("pss%d" % i, [128, 1024], F32) for i in range(2)]
    PSK = ["psb0", "psb1", "psb2", "psb3", "pss0", "pss1"]

    c31col = [33, 34]

    P.dma(lambda e: [e.dma_start(out=SM[:, 0:40], in_=smalls_in)], "d_sm", 1, writes=["SM_in"])
    P.dma(lambda e: [e.dma_start(out=LAMV[:, :], in_=lamv_in)], "d_lamv", 1, writes=["LAMV", "XSQ"])
    P.dma(lambda e: [e.dma_start(out=TT[0][:, 0:128], in_=ident_in)], "d_id", 1, writes=["T0"])
    P.op("dve", lambda e: e.tensor_copy(out=IDB[:, :], in_=TT[0][:, 0:128]), reads=["T0"], writes=["IDB"])
    P.op("dve", lambda e: e.memset(ONES[:, :], 1.0), writes=["ONES"])
    P.op("dve", lambda e: e.memset(ONESM[:, :], 1.0 / 128.0), writes=["ONESM"])
    P.op("dve", lambda e: e.memset(ONESD[:, :], 1.0 / 2048.0), writes=["ONESD"])
    P.op("dve", lambda e: e.memset(U[:, :, :], 0.0), writes=["U0", "U1"])
    P.op("dve", lambda e: e.memset(HST[:, :], 0.0), writes=["HST0", "HST1"])
    P.dma(lambda e: [e.dma_start(out=TT[1][:, :], in_=wax_in)], "d_wax", 1, writes=["T1"])
    P.op("dve", lambda e: e.tensor_copy(out=WAX[:, :, :].rearrange("p a b -> p (a b)"), in_=TT[1][:, :]),
         reads=["T1"], writes=["WAX"])
    P.dma(lambda e: [e.dma_start(out=TT[2 + q4][:, :], in_=biast_in[:, q4 * 512:(q4 + 1) * 512]) for q4 in range(4)],
          "d_bias", 4, writes=["T2", "T3", "T4", "T5"])
    for q4 in range(4):
        for half in range(2):
            P.op("dve", lambda e, q4=q4, half=half: e.tensor_copy(
                out=BIAS[:, q4 // 2, (q4 % 2) * 2:(q4 % 2) * 2 + 2, half * 256:(half + 1) * 256],
                in_=TT[2 + q4][:, :].rearrange("p (a b) -> p a b", a=2)),
                reads=["T%d" % (2 + q4)], writes=["BIAS"])
    P.op("act", lambda e: e.activation(out=SM[:, 40:42], in_=SM[:, 30:32], func=AF.Exp, scale=-1.0),
         reads=["SM_in"], writes=["SM_a"])
    P.op("act", lambda e: e.activation(out=SM[:, 40:42], in_=SM[:, 40:42], func=AF.Ln, bias=1.0),
         reads=["SM_a"], writes=["SM_a"])
    P.op("dve", lambda e: e.tensor_scalar(out=SM[:, 42:44], in0=SM[:, 40:42], scalar1=-16.0, scalar2=None, op0=ALU.mult),
         reads=["SM_a"], writes=["SM_c2"])
    P.op("dve", lambda e: e.tensor_scalar(out=SM[:, 40:42], in0=SM[:, 40:42], scalar1=-8.0, scalar2=None, op0=ALU.mult),
         reads=["SM_a", "SM_c2"], writes=["SM_a"])
    P.op("dve", lambda e: e.tensor_scalar(out=SM[:, 44:48], in0=SM[:, 26:30], scalar1=-1.0, scalar2=None, op0=ALU.mult),
         reads=["SM_in"], writes=["SM_nb"])
    P.op("dve", lambda e: e.tensor_tensor(out=LAMV[:, 0:64], in0=LAMV[:, 0:64], in1=LAMV[:, 64:128], op=ALU.mult),
         reads=["LAMV"], writes=["LAMV"])
    P.op("dve", lambda e: e.tensor_tensor(out=LAMV[:, 128:192], in0=LAMV[:, 128:192], in1=LAMV[:, 192:256], op=ALU.mult),
         reads=["LAMV"], writes=["LAMV"])
    P.op("dve", lambda e: e.tensor_reduce(out=SM[:, 50:51], in_=LAMV[:, 0:64], axis=mybir.AxisListType.X, op=ALU.add),
         reads=["LAMV"], writes=["SM_l"])
    P.op("dve", lambda e: e.tensor_reduce(out=SM[:, 51:52], in_=LAMV[:, 128:192], axis=mybir.AxisListType.X, op=ALU.add),
         reads=["LAMV", "SM_l"], writes=["SM_l", "XSQ"])
    P.op("act", lambda e: e.activation(out=SM[:, 52:54], in_=SM[:, 50:52], func=AF.Exp), reads=["SM_l"], writes=["SM_l2"])
    P.op("dve", lambda e: e.scalar_tensor_tensor(out=SM[:, 48:49], in0=SM[:, 53:54], scalar=-LAM_INIT, in1=SM[:, 52:53],
                                                 op0=ALU.add, op1=ALU.subtract), reads=["SM_l2"], writes=["SM_nl"])
    P.op("dve", lambda e: e.tensor_scalar(out=SM[:, 49:50], in0=SM[:, 32:33], scalar1=1.0 - LAM_INIT, scalar2=None, op0=ALU.mult),
         reads=["SM_in"], writes=["SM_gs"])
    SMK = ["SM_in", "SM_a", "SM_c2", "SM_nb", "SM_nl", "SM_gs"]

    xT_v = xT.rearrange("(c p) t -> p c t", p=128)

    def tile_tok(ti):
        return (0, NM) if ti == 0 else (NM + 512 * (ti - 1), 512)

    def load_piece(g):
        if g >= 4 * NT:
            return
        ti, k = divmod(g, 4)
        tok0, n = tile_tok(ti)
        sl = g % 2
        P.dma(lambda e: [e.dma_start(out=XS[sl][:, :, 0:n], in_=xT_v[:, 4 * k:4 * k + 4, tok0:tok0 + n])],
              "d_xs%d" % sl, 1, writes=["XS%d" % sl])

    def stats_piece(ti, k):
        g = 4 * ti + k
        tok0, n = tile_tok(ti)
        sl = g % 2
        hb = ti % 2
        for q in range(4):
            dch = 4 * k + q
            if q % 2 == 0:
                P.op("dve", lambda e, q=q, dch=dch: e.tensor_scalar(
                    out=HB[hb][:, dch, 0:n], in0=XS[sl][:, q, 0:n], scalar1=SM[:, dch:dch + 1], scalar2=None, op0=ALU.mult),
                    reads=["XS%d" % sl, "SM_in"], writes=["HB%d_%d" % (hb, dch)])
            else:
                P.op("act", lambda e, q=q, dch=dch: e.activation(
                    out=HB[hb][:, dch, 0:n], in_=XS[sl][:, q, 0:n], func=AF.Copy, scale=SM[:, dch:dch + 1]),
                    reads=["XS%d" % sl, "SM_in"], writes=["HB%d_%d" % (hb, dch)])
        P.op("pool", lambda e: e.tensor_tensor(out=XSQ[:, :, 0:n], in0=XS[sl][:, :, 0:n], in1=XS[sl][:, :, 0:n], op=ALU.mult),
             reads=["XS%d" % sl], writes=["XSQ"])
        load_piece(g + 2)

        def mm(e):
            r = None
            for q in range(4):
                r = e.matmul(psb[3][:, 0:n], lhsT=ONESD[:, :], rhs=XSQ[:, q, 0:n],
                             start=(k == 0 and q == 0), stop=(k == 3 and q == 3))
            return r
        P.op("pe", mm, reads=["XSQ", "ONESD"], writes=["psb3"])
        if k == 3:
            rb = ti % 2
            P.op("act", lambda e: e.activation(out=RSTD[rb][:, 0:n], in_=psb[3][:, 0:n], func=AF.Ln, bias=1e-6),
                 reads=["psb3"], writes=["RSTD%d" % rb])
            P.op("act", lambda e: e.activation(out=RSTD[rb][:, 0:n], in_=RSTD[rb][:, 0:n], func=AF.Exp, scale=-0.5),
                 reads=["RSTD%d" % rb], writes=["RSTD%d" % rb])

    w_in_v = w_in.rearrange("(c p) n -> p c n", p=128)
    W_ORDER = [8, 9, 0, 1, 2, 3, 4, 5, 6, 7, 10, 11]

    def load_w(cc):
        P.dma_pool(lambda e: [e.dma_start(out=WP[:, :, cc * 128:(cc + 1) * 128], in_=w_in_v[:, :, cc * 128:(cc + 1) * 128])],
                   "d_wp%d" % cc, 1, writes=["WP%d" % cc])

    load_piece(0)
    load_piece(1)
    for cc in W_ORDER[0:4]:
        load_w(cc)
    for k in range(4):
        stats_piece(0, k)
    for cc in W_ORDER[4:]:
        load_w(cc)

    bank_rr = [0]
    lo_cnt = [0]

    def phase1_tile(ti):
        tok0, n = tile_tok(ti)
        hb = ti % 2
        rb = ti % 2
        meta = (ti == 0)
        ci = ti - 1
        r0 = 512 * ci
        order = [8, 9, 2, 3, 4, 5] if meta else [8, 9, 0, 1, 2, 3, 4, 5, 6, 7, 10, 11]
        pend_tr = []
        HBK = ["HB%d_%d" % (hb, k) for k in range(16)]
        nstat = [0]

        def in_chunk(idx, cc):
            if pend_tr:
                pend_tr.pop(0)()
            bk = bank_rr[0] % 3
            bank_rr[0] += 1
            ps = psb[bk]
            psk = "psb%d" % bk

            def mm(e, cc=cc, ps=ps):
                r = None
                for c in range(16):
                    r = e.matmul(ps[:, 0:n], lhsT=WP[:, c, cc * 128:(cc + 1) * 128], rhs=HB[hb][:, c, 0:n],
                                 start=(c == 0), stop=(c == 15))
                return r
            P.op("pe", mm, reads=["WP%d" % cc] + HBK, writes=[psk])
            rk = "RSTD%d" % rb
            if cc in (0, 1):
                hh = cc
                P.op("dve", lambda e, ps=ps, hh=hh: e.scalar_tensor_tensor(
                    out=QT[:, hh, r0:r0 + n], in0=ps[:, 0:n], scalar=0.125, in1=RSTD[rb][:, 0:n], op0=ALU.mult, op1=ALU.mult),
                    reads=[psk, rk], writes=["QT"])
            elif cc in (2, 3):
                hh = cc - 2
                P.op("dve", lambda e, ps=ps, hh=hh: e.tensor_tensor(
                    out=KT[:, hh, tok0:tok0 + n], in0=ps[:, 0:n], in1=RSTD[rb][:, 0:n], op=ALU.mult),
                    reads=[psk, rk], writes=["KT"])
            elif cc in (4, 5):
                hh = cc - 4
                P.op("dve", lambda e, ps=ps: e.tensor_tensor(out=VT[:, 0:n], in0=ps[:, 0:n], in1=RSTD[rb][:, 0:n], op=ALU.mult),
                     reads=[psk, rk], writes=["VT"])
                nblk = 1 if meta else 4
                blk0 = 0 if meta else 1 + 4 * ci
                bw = NM if meta else 128
                ptv = pss[1][:, 0:256].bitcast(BF16)

                def tr(e, nblk=nblk, bw=bw, ptv=ptv):
                    r = None
                    for jb in range(nblk):
                        r = e.transpose(out=ptv[0:bw, jb * 128:(jb + 1) * 128], in_=VT[:, jb * bw:(jb + 1) * bw], identity=IDB[:, :])
                    return r
                def deferred(hh=hh, nblk=nblk, bw=bw, blk0=blk0, ptv=ptv, tr=tr):
                    P.op("pe", tr, reads=["VT", "IDB"], writes=["pss1"])
                    P.op("act", lambda e: e.activation(
                        out=VV[0:bw, blk0:blk0 + nblk, hh * 128:(hh + 1) * 128],
                        in_=ptv[0:bw, 0:nblk * 128].rearrange("p (a b) -> p a b", a=nblk), func=AF.Copy),
                        reads=["pss1"], writes=["VV"])
                pend_tr.append(deferred)
            elif cc in (6, 7, 10, 11):
                tg, te = TT[4], TT[5]
                P.op("dve", lambda e, ps=ps: e.tensor_tensor(out=tg[:, 0:n], in0=ps[:, 0:n], in1=RSTD[rb][:, 0:n], op=ALU.mult),
                     reads=[psk, rk], writes=["T4"])
                P.op("act", lambda e: e.activation(out=te[:, 0:n], in_=tg[:, 0:n], func=AF.Exp, scale=-1.0),
                     reads=["T4"], writes=["T5"])
                P.op("act", lambda e: e.activation(out=te[:, 0:n], in_=te[:, 0:n], func=AF.Ln, bias=1.0),
                     reads=["T5"], writes=["T5"])
                P.op("act", lambda e: e.activation(out=te[:, 0:n], in_=te[:, 0:n], func=AF.Exp, scale=-1.0),
                     reads=["T5"], writes=["T5"])
                if cc in (6, 7):
                    hh = cc - 6
                    P.op("dve", lambda e, hh=hh: e.tensor_tensor(out=SGA[:, hh, r0:r0 + n], in0=tg[:, 0:n], in1=te[:, 0:n], op=ALU.mult),
                         reads=["T4", "T5"], writes=["SGA"])
                else:
                    blk = cc - 10
                    P.op("dve", lambda e, blk=blk: e.tensor_tensor(out=SGL[:, blk, 0:n], in0=tg[:, 0:n], in1=te[:, 0:n], op=ALU.mult),
                         reads=["T4", "T5"], writes=["SGL%d" % blk])
            elif cc in (8, 9):
                blk = cc - 8
                P.op("dve", lambda e, ps=ps, blk=blk: e.tensor_tensor(
                    out=U[:, blk, 3:3 + n], in0=ps[:, 0:n], in1=RSTD[rb][:, 0:n], op=ALU.mult),
                    reads=[psk, rk], writes=["U%d" % blk])
            if ti + 1 < NT and idx >= 1 and idx % 2 == 1 and nstat[0] < 4:
                stats_piece(ti + 1, nstat[0])
                nstat[0] += 1

        hooks = make_hooks_for(ti)
        for idx, cc in enumerate(order):
            in_chunk(idx, cc)
            for h in hooks.pop(idx, []):
                h()
        while pend_tr:
            pend_tr.pop(0)()
        while ti + 1 < NT and nstat[0] < 4:
            stats_piece(ti + 1, nstat[0])
            nstat[0] += 1

        for idx in sorted(hooks.keys()):
            for h in hooks[idx]:
                h()

    def make_hooks_for(ti):
        hooks = {}
        if ti >= 1:
            hooks[1] = [lambda: lru_B1(ti - 1, 0)]
            hooks[3] = [lambda: lru_B2(ti - 1, 0)]
            hooks[4] = [lambda: lru_B1(ti - 1, 1)]
            hooks[6] = [lambda: lru_B2(ti - 1, 1)]
            hooks[9] = [lambda: lru_gather(ti - 1)]
        hooks[7] = [lambda: lru_A(ti, 0), lambda: lru_A(ti, 1)]
        return hooks

    UCB = [TT[0], TT6]
    UCK = ["T0", "T6"]
    UCBFS = [UCBF, UCBF2]
    UCBFK = ["UCBF0", "UCBF1"]

    def lru_A(ti, blk):
        tok0, n = tile_tok(ti)
        uk = "U%d" % blk
        uc, uck = UCB[blk], UCK[blk]
        cw = 16 + 4 * blk
        P.op("dve", lambda e: e.tensor_scalar(
            out=uc[:, 0:n], in0=U[:, blk, 3:3 + n], scalar1=SM[:, cw + 3:cw + 4], scalar2=SM[:, 24 + blk:25 + blk],
            op0=ALU.mult, op1=ALU.add), reads=[uk, "SM_in"], writes=[uck])
        for kk in (2, 1, 0):
            P.op("dve", lambda e, kk=kk: e.scalar_tensor_tensor(
                out=uc[:, 0:n], in0=U[:, blk, kk:kk + n], scalar=SM[:, cw + kk:cw + kk + 1], in1=uc[:, 0:n],
                op0=ALU.mult, op1=ALU.add), reads=[uk, "SM_in", uck], writes=[uck])
        P.op("pool", lambda e: e.tensor_copy(out=U[:, blk, 0:3], in_=U[:, blk, n:n + 3]), reads=[uk], writes=[uk])
        P.op("act", lambda e: e.activation(out=UCBFS[blk][:, 0:n], in_=uc[:, 0:n], func=AF.Copy),
             reads=[uck], writes=[UCBFK[blk]])

    def lru_B1(ti, blk):
        tok0, n = tile_tok(ti)
        tr_, ti_, ta, tm = TT[1], TT[2], TT[3], TT[4]

        def gmm(e):
            e.matmul(pss[0][:, 0:n], lhsT=WAX[:, 2 * blk, :], rhs=UCBFS[blk][:, 0:n], start=True, stop=True)
            return e.matmul(pss[0][:, 512:512 + n], lhsT=WAX[:, 2 * blk + 1, :], rhs=UCBFS[blk][:, 0:n], start=True, stop=True)
        P.op("pe", gmm, reads=["WAX", UCBFK[blk]], writes=["pss0"])
        for gi, (tdst, tkey, off) in enumerate(((tr_, "T1", 0), (ti_, "T2", 512))):
            P.op("act", lambda e, tdst=tdst, off=off, gi=gi: e.activation(
                out=tdst[:, 0:n], in_=pss[0][:, off:off + n], func=AF.Exp, scale=-1.0,
                bias=SM[:, 44 + 2 * gi + blk:45 + 2 * gi + blk]), reads=["pss0", "SM_nb"], writes=[tkey])
            P.op("act", lambda e, tdst=tdst: e.activation(out=tdst[:, 0:n], in_=tdst[:, 0:n], func=AF.Ln, bias=1.0),
                 reads=[tkey], writes=[tkey])
            P.op("act", lambda e, tdst=tdst: e.activation(out=tdst[:, 0:n], in_=tdst[:, 0:n], func=AF.Exp, scale=-1.0),
                 reads=[tkey], writes=[tkey])
        P.op("act", lambda e: e.activation(out=ta[:, 0:n], in_=tr_[:, 0:n], func=AF.Exp, scale=SM[:, 40 + blk:41 + blk]),
             reads=["T1", "SM_a"], writes=["T3"])
        P.op("act", lambda e: e.activation(out=tm[:, 0:n], in_=tr_[:, 0:n], func=AF.Exp, scale=SM[:, 42 + blk:43 + blk]),
             reads=["T1", "SM_c2"], writes=["T4"])
        P.op("act", lambda e: e.activation(out=tm[:, 0:n], in_=tm[:, 0:n], func=AF.Ln, scale=-1.0, bias=1.0),
             reads=["T4"], writes=["T4"])
        P.op("act", lambda e: e.activation(out=tm[:, 0:n], in_=tm[:, 0:n], func=AF.Exp, scale=0.5),
             reads=["T4"], writes=["T4"])

    def lru_B2(ti, blk):
        tok0, n = tile_tok(ti)
        meta = (ti == 0)
        ci = ti - 1
        uc, uck = UCB[blk], UCK[blk]
        tr_, ti_, ta, tm = TT[1], TT[2], TT[3], TT[4]
        if meta:
            P.op("dve", lambda e: e.memset(tm[:, 0:1], 1.0), reads=["T4"], writes=["T4"])
        P.op("dve", lambda e: e.tensor_tensor(out=ti_[:, 0:n], in0=ti_[:, 0:n], in1=tm[:, 0:n], op=ALU.mult),
             reads=["T2", "T4"], writes=["T2"])
        P.op("dve", lambda e: e.tensor_tensor(out=ti_[:, 0:n], in0=ti_[:, 0:n], in1=uc[:, 0:n], op=ALU.mult),
             reads=["T2", uck], writes=["T2"])
        P.op("dve", lambda e: e.tensor_tensor_scan(
            out=tr_[:, 0:n], data0=ta[:, 0:n], data1=ti_[:, 0:n], initial=HST[:, blk:blk + 1], op0=ALU.mult, op1=ALU.add),
            reads=["T3", "T2", "HST%d" % blk, "T1"], writes=["T1"])
        P.op("dve", lambda e: e.tensor_copy(out=HST[:, blk:blk + 1], in_=tr_[:, n - 1:n]),
             reads=["T1"], writes=["HST%d" % blk])
        if not meta:
            lb = ci % 2
            P.op("dve", lambda e: e.tensor_tensor(out=LO[lb][:, blk, :], in0=tr_[:, 0:n], in1=SGL[:, blk, 0:n], op=ALU.mult),
                 reads=["T1", "SGL%d" % blk], writes=["LO%d" % lb])
            if blk == 1:
                P.dma(lambda e: [e.dma_start(out=aginl[ci].rearrange("(b p) t -> p b t", p=128), in_=LO[lb][:, :, :])],
                      "d_lo%d" % lb, 1, reads=["LO%d" % lb], writes=["AGINL%d" % ci])

    def lru_gather(ti):
        ci = ti - 1
        if ci < 0:
            return
        P.cc(lambda e: e.collective_compute("AllGather", ALU.bypass, replica_groups=[[0, 1, 2, 3], [4, 5, 6, 7]],
                                            ins=[aginl[ci]], outs=[agoutl[ci]]),
             "cc_l", reads=["AGINL%d" % ci], writes=["AGOUTL%d" % ci])

    for ti in range(NT):
        phase1_tile(ti)
    lru_B1(NT - 1, 0)
    lru_B2(NT - 1, 0)
    lru_B1(NT - 1, 1)
    lru_B2(NT - 1, 1)
    lru_gather(NT - 1)

    if DEBUG:
        P.dma(lambda e: [e.dma_start(out=dbg["qt"], in_=QT[:, :, :].rearrange("p a b -> p (a b)")),
                         e.dma_start(out=dbg["kt"], in_=KT[:, :, :].rearrange("p a b -> p (a b)")),
                         e.dma_start(out=dbg["vv"], in_=VV[:, :, :].rearrange("p a b -> p (a b)")),
                         e.dma_start(out=dbg["sga"], in_=SGA[:, :, :].rearrange("p a b -> p (a b)"))],
              "d_dbg", 4, reads=["QT", "KT", "VV", "SGA"], writes=["DBG"])

    P.barrier(skip_prefix="cc_")
    w_out_v = w_out.rearrange("(c p) n -> p c n", p=128)
    P.dma(lambda e: [e.dma_start(out=WOST32[:, 4 * q_:4 * q_ + 4, :], in_=w_out_v[:, 4 * q_:4 * q_ + 4, :]) for q_ in range(4)],
          "d_wo", 4, writes=["WOST32"])
    WO_CAST = [False]

    def cast_wo():
        if WO_CAST[0]:
            return
        WO_CAST[0] = True
        for q_ in range(4):
            P.op("dve", lambda e, q_=q_: e.tensor_copy(out=WO[:, 4 * q_:4 * q_ + 4, :], in_=WOST32[:, 4 * q_:4 * q_ + 4, :]),
                 reads=["WOST32"], writes=["WO"])
    P.dma(lambda e: [e.dma_start(out=FG[:, :], in_=fg_in)], "d_fg", 1, writes=["FG"])
    P.op("dve", lambda e: e.memset(QBD[0][:, :], 0.0), writes=["QBD0"])
    P.op("dve", lambda e: e.memset(QBD[1][:, :], 0.0), writes=["QBD1"])

    pt_rr = [0]
    s_rr = [0]

    PROC = [0, 1, 2, 3, 4, 5, 6, 7]
    POS = {c_: p_ for p_, c_ in enumerate(PROC)}

    SCH = {"t": 0.0, "cc_free": 0.0, "inj": None, "inj_chunk": None, "bseq": 0}
    pend_B = []
    pend_F = []
    pend_fin2 = []
    BSEQ = {}
    CC_A, CC_B, LOAD_LAT = 48.0, 14.0, 12.0

    def attention(i):
        mb = POS[i] % 2
        for hh in range(2):
            for qs in range(2):
                att_tile(i, mb, hh, qs)
        def fin_chunk():
            P.dma(lambda e: [e.dma_start(out=agin[i].rearrange("(b p) t -> p b t", p=128), in_=MIX[mb][:, :, :])],
                  "d_mix%d" % mb, 1, reads=["MIX%d" % mb], writes=["AGIN%d_a" % i])
            P.cc(lambda e: e.collective_compute("AllGather", ALU.bypass, replica_groups=[[0, 1, 2, 3], [4, 5, 6, 7]],
                                                ins=[agin[i]], outs=[agout[i]]),
                 "cc_a", reads=["AGIN%d_a" % i], writes=["AGOUT%d" % i])
            st = max(SCH["cc_free"], SCH["t"] + 2.0)
            SCH["cc_free"] = st + CC_A
            pend_B.append((i, SCH["cc_free"]))
        pend_post.append(fin_chunk)

    def sched_block(cost):
        SCH["t"] += cost
        if SCH["inj"] is None and pend_B and pend_B[0][1] + 8.0 <= SCH["t"]:
            k_, _r = pend_B.pop(0)
            start_B(k_)
        if SCH["inj"] is not None and SCH["inj_ready"] <= SCH["t"]:
            run_units(1)
        if SCH["inj"] is not None and pend_B and pend_B[0][1] + 8.0 <= SCH["t"]:
            prefetch_mixg(pend_B[0][0])

    PREF = set()

    def assign_seq(k_):
        if k_ not in BSEQ:
            BSEQ[k_] = SCH["bseq"]
            SCH["bseq"] += 1

    def prefetch_mixg(k_):
        if k_ in PREF:
            return
        assign_seq(k_)
        PREF.add(k_)
        outproj_prep_mixg(k_)

    def start_B(k_):
        assign_seq(k_)
        if BSEQ[k_] >= 2:
            force_finalize_upto(BSEQ[k_] - 2)
        cast_wo()
        pre = k_ in PREF
        prefetch_mixg(k_)
        outproj_prep(k_)
        SCH["inj"] = outproj_units(k_)
        SCH["inj_chunk"] = k_
        SCH["inj_ready"] = SCH["t"] + (6.0 if pre else LOAD_LAT)

    def run_units(n_):
        for _ in range(n_):
            if SCH["inj"] is None:
                return
            try:
                next(SCH["inj"])
                SCH["t"] += 0.25
            except StopIteration:
                k_ = SCH["inj_chunk"]
                SCH["inj"] = None
                st = max(SCH["cc_free"], SCH["t"] + 2.0)
                SCH["cc_free"] = st + CC_B
                pend_F.append((k_, SCH["cc_free"]))

    FIN_DONE = set()

    def force_finalize_upto(seq):
        for k_, sq_ in list(BSEQ.items()):
            if sq_ <= seq and k_ not in FIN_DONE:
                for it in list(pend_F):
                    if it[0] == k_:
                        pend_F.remove(it)
                do_finalize(k_, split=False)

    def do_finalize(k_, split):
        FIN_DONE.add(k_)
        finalize1(k_)
        if split:
            pend_fin2.append(lambda: finalize2(k_))
        else:
            finalize2(k_)

    def sched_tile_start():
        if pend_F and pend_F[0][1] + 45.0 <= SCH["t"]:
            k_, _r = pend_F.pop(0)
            if k_ not in FIN_DONE:
                do_finalize(k_, split=True)

    tile_rr = [0]
    LG = 12
    TILES = [(i_, hh_, qs_) for i_ in PROC for hh_ in range(2) for qs_ in range(2)]
    pend_post = []

    def emit_qbd(t):
        if t >= len(TILES):
            return
        i_, hh_, qs_ = TILES[t]
        q0_ = 256 * (2 * i_ + qs_)
        qb_ = t % 2
        P.op("pool", lambda e: e.tensor_copy(out=QBD[qb_][0:64, 0:256], in_=QT[0:64, hh_, q0_:q0_ + 256]),
             reads=["QT"], writes=["QBD%d" % qb_])
        P.op("pool", lambda e: e.tensor_copy(out=QBD[qb_][64:128, 256:512], in_=QT[64:128, hh_, q0_:q0_ + 256]),
             reads=["QT"], writes=["QBD%d" % qb_])

    def att_tile(i, mb, hh, qs):
        sched_tile_start()
        qi = 2 * i + qs
        q0 = 256 * qi
        tix = tile_rr[0]
        tile_rr[0] += 1
        ob = tix % 2
        psO = psb[0] if ob == 0 else pss[1][:, 0:512]
        psOk = "psb0" if ob == 0 else "pss1a"
        qb = tix % 2
        qbk = "QBD%d" % qb
        if tix == 0:
            emit_qbd(0)
        emit_qbd(tix + 1)
        blocks = [("real", m) for m in range(2 * qi + 2)] + [("meta", None)]
        nb = len(blocks)
        nreal = nb - 1
        grp_first = [None]
        linfo = {}
        l_started = [False]

        def s_stage(nidx):
            kind, m = blocks[nidx]
            sb = s_rr[0] % 3
            s_rr[0] += 1
            if kind == "meta":
                M, k0 = NM, 0
                spec = 0 if qi == 0 else None
            else:
                M, k0 = 128, NM + 128 * m
                spec = {2 * qi - 1: 1, 2 * qi: 2, 2 * qi + 1: 3}.get(m)
            psS = (pss[0][:, 0:512], pss[0][:, 512:1024], pss[1][:, 512:1024])[sb]
            psSk = ("pss0a", "pss0b", "pss1b")[sb]

            def mm(e):
                r = e.matmul(psS[0:M, :], lhsT=KT[:, hh, k0:k0 + M], rhs=QBD[qb][:, :], start=True, stop=(spec is None))
                if spec is not None:
                    r = e.matmul(psS[0:M, :], lhsT=IDB[0:M, 0:M], rhs=BIAS[0:M, hh, spec, :], start=False, stop=True)
                return r
            P.op("pe", mm, reads=["KT", qbk, "IDB", "BIAS"], writes=[psSk])
            pb = pt_rr[0] % NPT
            pt_rr[0] += 1
            ptk = "PT%d" % pb
            ptf = PT[pb][0:M, :, :].rearrange("p a b -> p (a b)")
            if spec is None:
                P.op("act", lambda e: e.activation(out=ptf, in_=psS[0:M, :], func=AF.Exp, bias=SM[0:M, c31col[hh]:c31col[hh] + 1]),
                     reads=[psSk, "SM_in"], writes=[ptk])
            else:
                P.op("act", lambda e: e.activation(out=ptf, in_=psS[0:M, :], func=AF.Exp), reads=[psSk], writes=[ptk])
            lrhs = None
            if kind == "meta":
                lrhs = (ptf, [ptk], M)
            else:
                g, pos = divmod(nidx, LG)
                gs = GS[g % 2]
                gk = "GS%d" % (g % 2)
                last_in_group = (pos == LG - 1) or (nidx == nreal - 1)
                if pos == 0:
                    grp_first[0] = (ptf, ptk)
                    if last_in_group:
                        lrhs = (ptf, [ptk], M)
                elif pos == 1:
                    f_ap, f_k = grp_first[0]
                    P.op("dve", lambda e: e.tensor_tensor(out=gs[:, :], in0=f_ap, in1=ptf, op=ALU.add),
                         reads=[f_k, ptk], writes=[gk])
                else:
                    P.op("dve", lambda e: e.tensor_tensor(out=gs[:, :], in0=gs[:, :], in1=ptf, op=ALU.add),
                         reads=[gk, ptk], writes=[gk])
                if pos >= 1 and last_in_group:
                    lrhs = (gs[:, :], [gk], 128)
            linfo[nidx] = lrhs
            return (pb, M, kind, m)

        def pv_stage(nidx, info):
            pb, M, kind, m = info
            vb = 0 if kind == "meta" else 1 + m
            rhs = PT[pb][0:M, :, :].rearrange("p a b -> p (a b)")
            P.op("pe", lambda e: e.matmul(psO[:, :], lhsT=VV[0:M, vb, hh * 128:(hh + 1) * 128], rhs=rhs,
                                          start=(nidx == 0), stop=(nidx == nb - 1)),
                 reads=["VV", "PT%d" % pb], writes=[psOk])
            lr = linfo.pop(nidx)
            if lr is not None:
                l_ap, l_keys, lM = lr
                first = not l_started[0]
                l_started[0] = True
                P.op("pe", lambda e: e.matmul(psb[1][:, :], lhsT=ONES[0:lM, :], rhs=l_ap, start=first, stop=(nidx == nb - 1)),
                     reads=["ONES"] + l_keys, writes=["psb1"])
            sched_block(0.72)
            if nidx == 2:
                while pend_post:
                    pend_post.pop(0)()
            if nidx == 4 and pend_fin2:
                pend_fin2.pop(0)()

        infos = {}
        AHEAD = 2
        for nidx in range(min(AHEAD, nb)):
            infos[nidx] = s_stage(nidx)
        for nidx in range(nb):
            if nidx + AHEAD < nb:
                infos[nidx + AHEAD] = s_stage(nidx + AHEAD)
            pv_stage(nidx, infos.pop(nidx))
        P.op("act", lambda e: e.activation(out=RL[:, :], in_=psb[1][:, :], func=AF.Ln), reads=["psb1"], writes=["RL"])
        P.op("act", lambda e: e.activation(out=RL[:, :], in_=RL[:, :], func=AF.Exp, scale=-1.0), reads=["RL"], writes=["RL"])
        P.op("dve", lambda e: e.tensor_tensor(out=ON[:, :], in0=psO[:, :], in1=RL[:, :], op=ALU.mult),
             reads=[psOk, "RL"], writes=["ON"])
        P.op("dve", lambda e: e.scalar_tensor_tensor(out=DIFF[:, :], in0=ON[:, 256:512], scalar=SM[:, 48:49], in1=ON[:, 0:256],
                                                     op0=ALU.mult, op1=ALU.add), reads=["ON", "SM_nl"], writes=["DIFF"])
        P.op("pool", lambda e: e.tensor_tensor(out=SQ[:, :], in0=DIFF[:, :], in1=DIFF[:, :], op=ALU.mult),
             reads=["DIFF"], writes=["SQ"])
        def post2():
            P.op("pe", lambda e: e.matmul(psb[2][:, 0:256], lhsT=ONESM[:, :], rhs=SQ[:, :], start=True, stop=True),
                 reads=["SQ", "ONESM"], writes=["psb2"])
            P.op("act", lambda e: e.activation(out=R2[:, :], in_=psb[2][:, 0:256], func=AF.Ln, bias=1e-5),
                 reads=["psb2"], writes=["R2"])
            P.op("act", lambda e: e.activation(out=R2[:, :], in_=R2[:, :], func=AF.Exp, scale=-0.5),
                 reads=["R2"], writes=["R2"])
            P.op("dve", lambda e: e.tensor_tensor(out=T1[:, :], in0=DIFF[:, :], in1=R2[:, :], op=ALU.mult),
                 reads=["DIFF", "R2"], writes=["T1x"])
            P.op("dve", lambda e: e.scalar_tensor_tensor(
                out=MIX[mb][:, hh, qs * 256:(qs + 1) * 256], in0=T1[:, :], scalar=SM[:, 49:50], in1=SGA[:, hh, q0:q0 + 256],
                op0=ALU.mult, op1=ALU.mult), reads=["T1x", "SM_gs", "SGA"], writes=["MIX%d" % mb])
        pend_post.append(post2)
        while pend_fin2:
            pend_fin2.pop(0)()

    def outproj_prep_mixg(i):
        gb = BSEQ[i] % 2
        P.dma(lambda e: [e.dma_start(out=MIXG[gb][:, 0:8, :], in_=agout[i].rearrange("(c p) t -> p c t", p=128)),
                         e.dma_start(out=MIXG[gb][:, 8:16, :], in_=agoutl[i].rearrange("(c p) t -> p c t", p=128))],
              "d_mg%d" % gb, 2, reads=["AGOUT%d" % i, "AGOUTL%d" % i], writes=["MIXG%d" % gb])

    def outproj_prep(i):
        P.dma(lambda e: [e.dma_start(out=XRES[:, :, :], in_=x_res[512 * i:512 * (i + 1), :].rearrange("(b p) n -> p b n", p=128))],
              "d_xr", 1, writes=["XRES"])
        P.op("dve", lambda e: e.memset(SSQ[:, :], 0.0), writes=["SSQ"])

    def outproj_units(i):
        gb = BSEQ[i] % 2
        yb = BSEQ[i] % 2
        for tb in range(4):
            for c in range(16):
                P.op("pe", lambda e, tb=tb, c=c: e.matmul(
                    psb[3][:, :], lhsT=MIXG[gb][:, c, tb * 128:(tb + 1) * 128], rhs=WO[:, c, :], start=(c == 0), stop=(c == 15)),
                    reads=["MIXG%d" % gb, "WO"], writes=["psb3"])
                if c == 15:
                    P.op("dve", lambda e, tb=tb: e.tensor_tensor(out=YB[yb][:, tb, :], in0=psb[3][:, :], in1=XRES[:, tb, :], op=ALU.add),
                         reads=["psb3", "XRES"], writes=["YB%d_%d" % (yb, tb)])
                    P.op("act", lambda e, tb=tb: e.activation(out=JUNK[:, :], in_=YB[yb][:, tb, :], func=AF.Square,
                                                              accum_out=SSQ[:, tb:tb + 1]),
                         reads=["YB%d_%d" % (yb, tb)], writes=["JUNK", "SSQ"])
                yield
        P.dma(lambda e: [e.dma_start(out=sqin[i], in_=SSQ[:, :])], "d_sq", 1, reads=["SSQ"], writes=["SQIN%d" % i])
        P.cc(lambda e: e.collective_compute("AllGather", ALU.bypass, replica_groups=[[0, 1, 2, 3], [4, 5, 6, 7]],
                                            ins=[sqin[i]], outs=[sqout[i]]),
             "cc_b", reads=["SQIN%d" % i], writes=["SQOUT%d" % i])

    def finalize1(i):
        P.dma(lambda e: [e.dma_start(out=SQG[:, :, :], in_=sqout[i].rearrange("(r p) b -> p r b", p=128))],
              "d_sqg", 1, reads=["SQOUT%d" % i], writes=["SQG"])
        P.op("dve", lambda e: e.tensor_tensor(out=TOT[:, :], in0=SQG[:, 0, :], in1=SQG[:, 1, :], op=ALU.add),
             reads=["SQG"], writes=["TOT"])
        P.op("dve", lambda e: e.tensor_tensor(out=TOT[:, :], in0=TOT[:, :], in1=SQG[:, 2, :], op=ALU.add),
             reads=["SQG", "TOT"], writes=["TOT"])
        P.op("dve", lambda e: e.tensor_tensor(out=TOT[:, :], in0=TOT[:, :], in1=SQG[:, 3, :], op=ALU.add),
             reads=["SQG", "TOT"], writes=["TOT"])

    def finalize2(i):
        yb = BSEQ[i] % 2
        P.op("act", lambda e: e.activation(out=RS[:, :], in_=TOT[:, :], func=AF.Ln, scale=1.0 / 2048.0, bias=1e-6),
             reads=["TOT"], writes=["RS"])
        P.op("act", lambda e: e.activation(out=RS[:, :], in_=RS[:, :], func=AF.Exp, scale=-0.5), reads=["RS"], writes=["RS"])
        for tb in range(4):
            P.op("dve", lambda e, tb=tb: e.scalar_tensor_tensor(
                out=YB[yb][:, tb, :], in0=YB[yb][:, tb, :], scalar=RS[:, tb:tb + 1], in1=FG[:, :], op0=ALU.mult, op1=ALU.mult),
                reads=["YB%d_%d" % (yb, tb), "RS", "FG"], writes=["YB%d_%d" % (yb, tb)])
        P.dma(lambda e: [e.dma_start(out=out[512 * i:512 * (i + 1), :].rearrange("(b p) n -> p b n", p=128), in_=YB[yb][:, :, :])],
              "d_out%d" % yb, 1, reads=["YB%d_%d" % (yb, tb) for tb in range(4)], writes=["OUT%d" % i])

    for ai in PROC:
        attention(ai)
    while pend_post:
        pend_post.pop(0)()
    while pend_B or SCH["inj"] is not None:
        if SCH["inj"] is None:
            k_, r_ = pend_B.pop(0)
            SCH["t"] = max(SCH["t"], r_)
            start_B(k_)
            SCH["t"] = max(SCH["t"], SCH["inj_ready"])
        if pend_B and pend_B[0][1] <= SCH["t"] + 16.0:
            prefetch_mixg(pend_B[0][0])
        run_units(64)
    while pend_F:
        k_, _r = pend_F.pop(0)
        if k_ not in FIN_DONE:
            do_finalize(k_, split=False)
    if DEBUG:
        P.dma(lambda e: [e.dma_start(out=dbg["ag0"], in_=agout[0])], "d_dbg2", 1, reads=["AGOUT0"], writes=["DBG2"])
    P.barrier()

    with ExitStack() as es:
        sems = {}
        for sname in P.cnt.keys():
            sems[sname] = es.enter_context(nc.semaphore("s_" + sname))
        block = es.enter_context(nc.Block())

        def run(e, eng):
            for waits, fn, sem, inc in P.ops[eng]:
                for (s_, v) in waits:
                    e.wait_ge(sems[s_], v)
                if fn is None:
                    continue
                r = fn(e)
                if isinstance(r, (list, tuple)):
                    for ins in r:
                        ins.then_inc(sems[sem], inc)
                else:
                    r.then_inc(sems[sem], inc)

        block.sync(lambda e: run(e, "sp"))
        block.tensor(lambda e: run(e, "pe"))
        block.scalar(lambda e: run(e, "act"))
        block.vector(lambda e: run(e, "dve"))
        block.gpsimd(lambda e: run(e, "pool"))
    return nc


def _bucket(dist):
    d = np.maximum(dist, 0).astype(np.int64)
    large = 16 + (np.log(np.maximum(d, 1).astype(np.float32) / np.float32(16.0)) / np.float32(math.log(128 / 16)) * np.float32(16)).astype(np.int32)
    large = np.minimum(large, 31)
    return np.where(d < 16, d, large).astype(np.int64)


def _bias_tiles(rel_bias, h):
    p = np.arange(128)[:, None]
    c = np.arange(256)[None, :]
    tiles = np.zeros((4, 128, 256), np.float32)
    d0 = 16 + c - p
    tiles[0] = rel_bias[_bucket(d0), h]
    d1 = c + 128 - p
    tiles[1] = rel_bias[_bucket(d1), h]
    d2 = c - p
    tiles[2] = np.where(d2 >= 0, rel_bias[_bucket(d2), h], np.float32(MASKV))
    d3 = c - 128 - p
    tiles[3] = np.where(d3 >= 0, rel_bias[_bucket(d3), h], np.float32(MASKV))
    return tiles


def _prep_inputs(x, meta_tokens, rel_bias, norm_g, w_in, conv_w, conv_b, w_a, b_a, w_x, b_x,
                 lru_lambda, lam_q1, lam_k1, lam_q2, lam_k2, subln_g, w_out, final_g):
    f = lambda a: np.asarray(a, dtype=np.float32)
    x, meta_tokens, rel_bias, norm_g, w_in = f(x), f(meta_tokens), f(rel_bias), f(norm_g), f(w_in)
    conv_w, conv_b, w_a, b_a, w_x, b_x = f(conv_w), f(conv_b), f(w_a), f(b_a), f(w_x), f(b_x)
    lru_lambda, subln_g, w_out, final_g = f(lru_lambda), f(subln_g), f(w_out), f(final_g)
    lamv = np.concatenate([f(lam_q1)[0], f(lam_k1)[0], f(lam_q2)[0], f(lam_k2)[0]])
    lamv = np.ascontiguousarray(np.broadcast_to(lamv[None, :], (128, 256)))
    ident = np.eye(128, dtype=np.float32)
    xTs = [np.ascontiguousarray(np.concatenate([meta_tokens, x[b]], axis=0).T) for b in range(2)]
    in_maps = []
    pidx = np.arange(128)
    for c in range(8):
        b, j = divmod(c, 4)
        hs = [2 * j, 2 * j + 1]
        cols = []
        for base in (0, 1024, 2048, 3072, 4096, 5120):
            for h in hs:
                cols.append(np.arange(base + h * 128, base + (h + 1) * 128))
        cols = np.concatenate(cols)
        w_in_c = np.ascontiguousarray(w_in[0][:, cols])
        w_out_c = np.ascontiguousarray(w_out[0][:, 512 * j:512 * (j + 1)])
        x_res = np.ascontiguousarray(x[b][:, 512 * j:512 * (j + 1)])
        fg = np.ascontiguousarray(np.broadcast_to(final_g[None, 512 * j:512 * (j + 1)], (128, 512)))
        sm = np.zeros((128, 40), np.float32)
        sm[:, 0:16] = norm_g[0].reshape(16, 128).T
        for bl in range(2):
            ch = hs[bl] * 128 + pidx
            for k in range(4):
                sm[:, 16 + 4 * bl + k] = conv_w[0][k, ch]
            sm[:, 24 + bl] = conv_b[0][ch]
            sm[:, 26 + bl] = b_a[0][ch]
            sm[:, 28 + bl] = b_x[0][ch]
            sm[:, 30 + bl] = lru_lambda[0][ch]
        sm[:, 32] = subln_g[0]
        sm[:, 33] = rel_bias[31, hs[0]]
        sm[:, 34] = rel_bias[31, hs[1]]
        wax = np.zeros((128, 4, 128), np.float32)
        for bl in range(2):
            wax[:, 2 * bl + 0, :] = w_a[0][hs[bl]]
            wax[:, 2 * bl + 1, :] = w_x[0][hs[bl]]
        bt = np.zeros((128, 2, 4, 256), np.float32)
        for hh in range(2):
            bt[:, hh] = np.transpose(_bias_tiles(rel_bias, hs[hh]), (1, 0, 2))
        in_maps.append({
            "xT": xTs[b], "w_in": w_in_c, "w_out": w_out_c, "x_res": x_res, "fg": fg, "smalls": sm,
            "lamv": lamv, "wax": np.ascontiguousarray(wax.reshape(128, 512)),
            "biast": np.ascontiguousarray(bt.reshape(128, 2048)), "ident": ident,
        })
    return in_maps


def kernel(**inputs):
    in_maps = _prep_inputs(**inputs)
    nc = build_program()
    res = run_bass_kernel_spmd(nc, in_maps, core_ids=list(range(8)))
    outp = np.zeros((2, S, D), np.float32)
    for c in range(8):
        b, j = divmod(c, 4)
        outp[b, :, 512 * j:512 * (j + 1)] = np.asarray(res.results[c]["out"], dtype=np.float32)
    return outp
```

```python
import math
from contextlib import ExitStack
import numpy as np
import concourse.bass as bass
import concourse.mybir as mybir
from concourse.bass_utils import run_bass_kernel_spmd

F32 = mybir.dt.float32
BF16 = mybir.dt.bfloat16
ALU = mybir.AluOpType
AF = mybir.ActivationFunctionType

D = 2048
NM = 16
S = 4096
T = S + NM
NT = 9
ENGS = ("pe", "act", "dve", "pool", "sp")
LAM_INIT = 0.8 - 0.6 * math.exp(0.0)
MASKV = -30000.0
DEBUG = False


class Prog:
    def __init__(self):
        self.ops = {e: [] for e in ENGS}
        self.cnt = {}
        self.waited = {e: {} for e in ENGS}
        self.lastw = {}
        self.readers = {}

    def _emit(self, eng, fn, reads, writes, sem, inc, ninst):
        deps = []
        for k in reads:
            if k in self.lastw:
                deps.append(self.lastw[k])
        for k in writes:
            if k in self.lastw:
                deps.append(self.lastw[k])
            deps.extend(self.readers.get(k, ()))
        waits = {}
        for (s, v) in deps:
            if s == "pe" and eng == "pe":
                continue
            if self.waited[eng].get(s, 0) >= v:
                continue
            if waits.get(s, 0) < v:
                waits[s] = v
        for s, v in waits.items():
            self.waited[eng][s] = v
        self.cnt[sem] = self.cnt.get(sem, 0) + inc * ninst
        tk = (sem, self.cnt[sem])
        self.ops[eng].append((sorted(waits.items()), fn, sem, inc))
        for k in writes:
            self.lastw[k] = tk
            self.readers[k] = []
        for k in reads:
            self.readers.setdefault(k, []).append(tk)
        return tk

    def op(self, eng, fn, reads=(), writes=()):
        return self._emit(eng, fn, reads, writes, eng, 1, 1)

    def dma(self, fn, sem, n, reads=(), writes=()):
        return self._emit("sp", fn, reads, writes, sem, 16, n)

    def dma_pool(self, fn, sem, n, reads=(), writes=()):
        return self._emit("pool", fn, reads, writes, sem, 16, n)

    def cc(self, fn, sem, reads=(), writes=()):
        return self._emit("pool", fn, reads, writes, sem, 1, 1)

    def barrier(self, skip_prefix=None):
        allk = [(s_, v_) for s_, v_ in self.cnt.items() if not (skip_prefix and s_.startswith(skip_prefix))]
        for e in ENGS:
            waits = []
            for s, v in allk:
                if self.waited[e].get(s, 0) < v and not (s == e == "pe"):
                    waits.append((s, v))
                    self.waited[e][s] = v
            if waits:
                self.ops[e].append((sorted(waits), None, None, 0))


def build_program():
    nc = bass.Bass("TRN2", target_bir_lowering=False)
    P = Prog()

    def din(name, shape, dt=F32):
        return nc.dram_tensor(name, list(shape), dt, kind="ExternalInput").ap()

    xT = din("xT", [D, T])
    w_in = din("w_in", [D, 1536])
    w_out = din("w_out", [D, 512])
    x_res = din("x_res", [S, 512])
    fg_in = din("fg", [128, 512])
    smalls_in = din("smalls", [128, 40])
    lamv_in = din("lamv", [128, 256])
    wax_in = din("wax", [128, 512])
    biast_in = din("biast", [128, 2048])
    ident_in = din("ident", [128, 128])
    out = nc.dram_tensor("out", [S, 512], F32, kind="ExternalOutput").ap()
    agin = [nc.dram_tensor("agin%d" % i, [256, 512], BF16).ap() for i in range(8)]
    agout = [nc.dram_tensor("agout%d" % i, [1024, 512], BF16).ap() for i in range(8)]
    aginl = [nc.dram_tensor("aginl%d" % i, [256, 512], BF16).ap() for i in range(8)]
    agoutl = [nc.dram_tensor("agoutl%d" % i, [1024, 512], BF16).ap() for i in range(8)]
    sqin = [nc.dram_tensor("sqin%d" % i, [128, 4], F32).ap() for i in range(8)]
    sqout = [nc.dram_tensor("sqout%d" % i, [512, 4], F32).ap() for i in range(8)]
    dbg = {}
    if DEBUG:
        dbg["qt"] = nc.dram_tensor("dbg_qt", [128, 2 * S], BF16, kind="ExternalOutput").ap()
        dbg["kt"] = nc.dram_tensor("dbg_kt", [128, 2 * T], BF16, kind="ExternalOutput").ap()
        dbg["vv"] = nc.dram_tensor("dbg_vv", [128, 33 * 256], BF16, kind="ExternalOutput").ap()
        dbg["sga"] = nc.dram_tensor("dbg_sga", [128, 2 * S], BF16, kind="ExternalOutput").ap()
        dbg["ag0"] = nc.dram_tensor("dbg_ag0", [1024, 512], BF16, kind="ExternalOutput").ap()

    ARENA_BYTES = 207 * 1024
    arena = nc.alloc_sbuf_tensor("arena", [128, ARENA_BYTES // 2], BF16)
    ptr = {"p": 0, 1: 0, 2: 0}

    def carve(phase, shape, dt):
        nel = 1
        for s_ in shape:
            nel *= s_
        nb = nel * (4 if dt == F32 else 2)
        nb_al = (nb + 63) // 64 * 64
        if phase == "p":
            off = ptr["p"]
            ptr["p"] += nb_al
            ptr[1] = ptr[2] = ptr["p"]
        else:
            off = ptr[phase]
            ptr[phase] += nb_al
        assert off + nb_al <= ARENA_BYTES, ("SBUF overflow", phase, off + nb_al)
        v = arena[:, off // 2: off // 2 + nb // 2]
        if dt == F32:
            v = v.bitcast(F32)
        if len(shape) == 2:
            return v.rearrange("p (a b) -> p a b", a=shape[0])
        if len(shape) == 3:
            return v.rearrange("p (a b c) -> p a b c", a=shape[0], b=shape[1])
        return v

    QT = carve("p", [2, S], BF16)
    KT = carve("p", [2, T], BF16)
    VV = carve("p", [33, 256], BF16)
    SGA = carve("p", [2, S], BF16)
    IDB = carve("p", [128], BF16)
    ONES = carve("p", [128], BF16)
    ONESM = carve("p", [128], BF16)
    ONESD = carve("p", [128], BF16)
    BIAS = carve("p", [2, 4, 512], BF16)
    SM = carve("p", [64], F32)
    WAX = carve("p", [4, 128], BF16)
    HST = carve("p", [2], F32)
    WP = carve(1, [16, 1536], BF16)
    XS = [carve(1, [4, 512], F32) for _ in range(2)]
    HB = [carve(1, [16, 512], BF16) for _ in range(2)]
    XSQ = carve(1, [4, 512], BF16)
    RSTD = [carve(1, [512], F32) for _ in range(2)]
    U = carve(1, [2, 520], F32)
    TT = [carve(1, [512], F32) for _ in range(6)]
    UCBF = carve(1, [512], BF16)
    SGL = carve(1, [2, 512], BF16)
    VT = carve(1, [512], BF16)
    LO = [carve(1, [2, 512], BF16) for _ in range(2)]
    TT6 = carve(1, [512], F32)
    UCBF2 = carve(1, [512], BF16)
    LAMV = XSQ[:, 0, :].bitcast(F32)
    WO = carve(2, [16, 512], BF16)
    MIXG = [carve(2, [16, 512], BF16) for _ in range(2)]
    XRES = carve(2, [4, 512], F32)
    YB = [carve(2, [4, 512], F32) for _ in range(2)]
    FG = carve(2, [512], F32)
    NPT = 6
    PT = [carve(2, [2, 256], BF16) for _ in range(NPT)]
    RL = carve(2, [512], F32)
    ON = carve(2, [512], F32)
    DIFF = carve(2, [256], F32)
    T1 = carve(2, [256], F32)
    R2 = carve(2, [256], F32)
    SQ = carve(2, [256], BF16)
    MIX = [carve(2, [2, 512], BF16) for _ in range(2)]
    SSQ = carve(2, [4], F32)
    SQG = carve(2, [4, 4], F32)
    TOT = carve(2, [4], F32)
    RS = carve(2, [4], F32)
    JUNK = carve(2, [512], F32)
    WOST32 = carve(2, [16, 512], F32)
    GS = [carve(2, [512], BF16) for _ in range(2)]
    QBD = [carve(2, [512], BF16) for _ in range(2)]

    psb = [nc.alloc_psum_tensor("psb%d" % i, [128, 512], F32) for i in range(4)]
    pss = [nc.alloc_psum_tensor("pss%d" % i, [128, 1024], F32) for i in range(2)]
    PSK = ["psb0", "psb1", "psb2", "psb3", "pss0", "pss1"]

    c31col = [33, 34]

    P.dma(lambda e: [e.dma_start(out=SM[:, 0:40], in_=smalls_in)], "d_sm", 1, writes=["SM_in"])
    P.dma(lambda e: [e.dma_start(out=LAMV[:, :], in_=lamv_in)], "d_lamv", 1, writes=["LAMV", "XSQ"])
    P.dma(lambda e: [e.dma_start(out=TT[0][:, 0:128], in_=ident_in)], "d_id", 1, writes=["T0"])
    P.op("dve", lambda e: e.tensor_copy(out=IDB[:, :], in_=TT[0][:, 0:128]), reads=["T0"], writes=["IDB"])
    P.op("dve", lambda e: e.memset(ONES[:, :], 1.0), writes=["ONES"])
    P.op("dve", lambda e: e.memset(ONESM[:, :], 1.0 / 128.0), writes=["ONESM"])
    P.op("dve", lambda e: e.memset(ONESD[:, :], 1.0 / 2048.0), writes=["ONESD"])
    P.op("dve", lambda e: e.memset(U[:, :, :], 0.0), writes=["U0", "U1"])
    P.op("dve", lambda e: e.memset(HST[:, :], 0.0), writes=["HST0", "HST1"])
    P.dma(lambda e: [e.dma_start(out=TT[1][:, :], in_=wax_in)], "d_wax", 1, writes=["T1"])
    P.op("dve", lambda e: e.tensor_copy(out=WAX[:, :, :].rearrange("p a b -> p (a b)"), in_=TT[1][:, :]),
         reads=["T1"], writes=["WAX"])
    P.dma(lambda e: [e.dma_start(out=TT[2 + q4][:, :], in_=biast_in[:, q4 * 512:(q4 + 1) * 512]) for q4 in range(4)],
          "d_bias", 4, writes=["T2", "T3", "T4", "T5"])
    for q4 in range(4):
        for half in range(2):
            P.op("dve", lambda e, q4=q4, half=half: e.tensor_copy(
                out=BIAS[:, q4 // 2, (q4 % 2) * 2:(q4 % 2) * 2 + 2, half * 256:(half + 1) * 256],
                in_=TT[2 + q4][:, :].rearrange("p (a b) -> p a b", a=2)),
                reads=["T%d" % (2 + q4)], writes=["BIAS"])
    P.op("act", lambda e: e.activation(out=SM[:, 40:42], in_=SM[:, 30:32], func=AF.Exp, scale=-1.0),
         reads=["SM_in"], writes=["SM_a"])
    P.op("act", lambda e: e.activation(out=SM[:, 40:42], in_=SM[:, 40:42], func=AF.Ln, bias=1.0),
         reads=["SM_a"], writes=["SM_a"])
    P.op("dve", lambda e: e.tensor_scalar(out=SM[:, 42:44], in0=SM[:, 40:42], scalar1=-16.0, scalar2=None, op0=ALU.mult),
         reads=["SM_a"], writes=["SM_c2"])
    P.op("dve", lambda e: e.tensor_scalar(out=SM[:, 40:42], in0=SM[:, 40:42], scalar1=-8.0, scalar2=None, op0=ALU.mult),
         reads=["SM_a", "SM_c2"], writes=["SM_a"])
    P.op("dve", lambda e: e.tensor_scalar(out=SM[:, 44:48], in0=SM[:, 26:30], scalar1=-1.0, scalar2=None, op0=ALU.mult),
         reads=["SM_in"], writes=["SM_nb"])
    P.op("dve", lambda e: e.tensor_tensor(out=LAMV[:, 0:64], in0=LAMV[:, 0:64], in1=LAMV[:, 64:128], op=ALU.mult),
         reads=["LAMV"], writes=["LAMV"])
    P.op("dve", lambda e: e.tensor_tensor(out=LAMV[:, 128:192], in0=LAMV[:, 128:192], in1=LAMV[:, 192:256], op=ALU.mult),
         reads=["LAMV"], writes=["LAMV"])
    P.op("dve", lambda e: e.tensor_reduce(out=SM[:, 50:51], in_=LAMV[:, 0:64], axis=mybir.AxisListType.X, op=ALU.add),
         reads=["LAMV"], writes=["SM_l"])
    P.op("dve", lambda e: e.tensor_reduce(out=SM[:, 51:52], in_=LAMV[:, 128:192], axis=mybir.AxisListType.X, op=ALU.add),
         reads=["LAMV", "SM_l"], writes=["SM_l", "XSQ"])
    P.op("act", lambda e: e.activation(out=SM[:, 52:54], in_=SM[:, 50:52], func=AF.Exp), reads=["SM_l"], writes=["SM_l2"])
    P.op("dve", lambda e: e.scalar_tensor_tensor(out=SM[:, 48:49], in0=SM[:, 53:54], scalar=-LAM_INIT, in1=SM[:, 52:53],
                                                 op0=ALU.add, op1=ALU.subtract), reads=["SM_l2"], writes=["SM_nl"])
    P.op("dve", lambda e: e.tensor_scalar(out=SM[:, 49:50], in0=SM[:, 32:33], scalar1=1.0 - LAM_INIT, scalar2=None, op0=ALU.mult),
         reads=["SM_in"], writes=["SM_gs"])
    SMK = ["SM_in", "SM_a", "SM_c2", "SM_nb", "SM_nl", "SM_gs"]

    xT_v = xT.rearrange("(c p) t -> p c t", p=128)

    def tile_tok(ti):
        return (0, NM) if ti == 0 else (NM + 512 * (ti - 1), 512)

    def load_piece(g):
        if g >= 4 * NT:
            return
        ti, k = divmod(g, 4)
        tok0, n = tile_tok(ti)
        sl = g % 2
        P.dma(lambda e: [e.dma_start(out=XS[sl][:, :, 0:n], in_=xT_v[:, 4 * k:4 * k + 4, tok0:tok0 + n])],
              "d_xs%d" % sl, 1, writes=["XS%d" % sl])

    def stats_piece(ti, k):
        g = 4 * ti + k
        tok0, n = tile_tok(ti)
        sl = g % 2
        hb = ti % 2
        for q in range(4):
            dch = 4 * k + q
            if q % 2 == 0:
                P.op("dve", lambda e, q=q, dch=dch: e.tensor_scalar(
                    out=HB[hb][:, dch, 0:n], in0=XS[sl][:, q, 0:n], scalar1=SM[:, dch:dch + 1], scalar2=None, op0=ALU.mult),
                    reads=["XS%d" % sl, "SM_in"], writes=["HB%d_%d" % (hb, dch)])
            else:
                P.op("act", lambda e, q=q, dch=dch: e.activation(
                    out=HB[hb][:, dch, 0:n], in_=XS[sl][:, q, 0:n], func=AF.Copy, scale=SM[:, dch:dch + 1]),
                    reads=["XS%d" % sl, "SM_in"], writes=["HB%d_%d" % (hb, dch)])
        P.op("pool", lambda e: e.tensor_tensor(out=XSQ[:, :, 0:n], in0=XS[sl][:, :, 0:n], in1=XS[sl][:, :, 0:n], op=ALU.mult),
             reads=["XS%d" % sl], writes=["XSQ"])
        load_piece(g + 2)

        def mm(e):
            r = None
            for q in range(4):
                r = e.matmul(psb[3][:, 0:n], lhsT=ONESD[:, :], rhs=XSQ[:, q, 0:n],
                             start=(k == 0 and q == 0), stop=(k == 3 and q == 3))
            return r
        P.op("pe", mm, reads=["XSQ", "ONESD"], writes=["psb3"])
        if k == 3:
            rb = ti % 2
            P.op("act", lambda e: e.activation(out=RSTD[rb][:, 0:n], in_=psb[3][:, 0:n], func=AF.Ln, bias=1e-6),
                 reads=["psb3"], writes=["RSTD%d" % rb])
            P.op("act", lambda e: e.activation(out=RSTD[rb][:, 0:n], in_=RSTD[rb][:, 0:n], func=AF.Exp, scale=-0.5),
                 reads=["RSTD%d" % rb], writes=["RSTD%d" % rb])

    w_in_v = w_in.rearrange("(c p) n -> p c n", p=128)
    W_ORDER = [8, 9, 0, 1, 2, 3, 4, 5, 6, 7, 10, 11]

    def load_w(cc):
        P.dma_pool(lambda e: [e.dma_start(out=WP[:, :, cc * 128:(cc + 1) * 128], in_=w_in_v[:, :, cc * 128:(cc + 1) * 128])],
                   "d_wp%d" % cc, 1, writes=["WP%d" % cc])

    load_piece(0)
    load_piece(1)
    for cc in W_ORDER[0:4]:
        load_w(cc)
    for k in range(4):
        stats_piece(0, k)
    for cc in W_ORDER[4:]:
        load_w(cc)

    bank_rr = [0]
    lo_cnt = [0]

    def phase1_tile(ti):
        tok0, n = tile_tok(ti)
        hb = ti % 2
        rb = ti % 2
        meta = (ti == 0)
        ci = ti - 1
        r0 = 512 * ci
        order = [8, 9, 2, 3, 4, 5] if meta else [8, 9, 0, 1, 2, 3, 4, 5, 6, 7, 10, 11]
        pend_tr = []
        HBK = ["HB%d_%d" % (hb, k) for k in range(16)]
        nstat = [0]

        def in_chunk(idx, cc):
            if pend_tr:
                pend_tr.pop(0)()
            bk = bank_rr[0] % 3
            bank_rr[0] += 1
            ps = psb[bk]
            psk = "psb%d" % bk

            def mm(e, cc=cc, ps=ps):
                r = None
                for c in range(16):
                    r = e.matmul(ps[:, 0:n], lhsT=WP[:, c, cc * 128:(cc + 1) * 128], rhs=HB[hb][:, c, 0:n],
                                 start=(c == 0), stop=(c == 15))
                return r
            P.op("pe", mm, reads=["WP%d" % cc] + HBK, writes=[psk])
            rk = "RSTD%d" % rb
            if cc in (0, 1):
                hh = cc
                P.op("dve", lambda e, ps=ps, hh=hh: e.scalar_tensor_tensor(
                    out=QT[:, hh, r0:r0 + n], in0=ps[:, 0:n], scalar=0.125, in1=RSTD[rb][:, 0:n], op0=ALU.mult, op1=ALU.mult),
                    reads=[psk, rk], writes=["QT"])
            elif cc in (2, 3):
                hh = cc - 2
                P.op("dve", lambda e, ps=ps, hh=hh: e.tensor_tensor(
                    out=KT[:, hh, tok0:tok0 + n], in0=ps[:, 0:n], in1=RSTD[rb][:, 0:n], op=ALU.mult),
                    reads=[psk, rk], writes=["KT"])
            elif cc in (4, 5):
                hh = cc - 4
                P.op("dve", lambda e, ps=ps: e.tensor_tensor(out=VT[:, 0:n], in0=ps[:, 0:n], in1=RSTD[rb][:, 0:n], op=ALU.mult),
                     reads=[psk, rk], writes=["VT"])
                nblk = 1 if meta else 4
                blk0 = 0 if meta else 1 + 4 * ci
                bw = NM if meta else 128
                ptv = pss[1][:, 0:256].bitcast(BF16)

                def tr(e, nblk=nblk, bw=bw, ptv=ptv):
                    r = None
                    for jb in range(nblk):
                        r = e.transpose(out=ptv[0:bw, jb * 128:(jb + 1) * 128], in_=VT[:, jb * bw:(jb + 1) * bw], identity=IDB[:, :])
                    return r
                def deferred(hh=hh, nblk=nblk, bw=bw, blk0=blk0, ptv=ptv, tr=tr):
                    P.op("pe", tr, reads=["VT", "IDB"], writes=["pss1"])
                    P.op("act", lambda e: e.activation(
                        out=VV[0:bw, blk0:blk0 + nblk, hh * 128:(hh + 1) * 128],
                        in_=ptv[0:bw, 0:nblk * 128].rearrange("p (a b) -> p a b", a=nblk), func=AF.Copy),
                        reads=["pss1"], writes=["VV"])
                pend_tr.append(deferred)
            elif cc in (6, 7, 10, 11):
                tg, te = TT[4], TT[5]
                P.op("dve", lambda e, ps=ps: e.tensor_tensor(out=tg[:, 0:n], in0=ps[:, 0:n], in1=RSTD[rb][:, 0:n], op=ALU.mult),
                     reads=[psk, rk], writes=["T4"])
                P.op("act", lambda e: e.activation(out=te[:, 0:n], in_=tg[:, 0:n], func=AF.Exp, scale=-1.0),
                     reads=["T4"], writes=["T5"])
                P.op("act", lambda e: e.activation(out=te[:, 0:n], in_=te[:, 0:n], func=AF.Ln, bias=1.0),
                     reads=["T5"], writes=["T5"])
                P.op("act", lambda e: e.activation(out=te[:, 0:n], in_=te[:, 0:n], func=AF.Exp, scale=-1.0),
                     reads=["T5"], writes=["T5"])
                if cc in (6, 7):
                    hh = cc - 6
                    P.op("dve", lambda e, hh=hh: e.tensor_tensor(out=SGA[:, hh, r0:r0 + n], in0=tg[:, 0:n], in1=te[:, 0:n], op=ALU.mult),
                         reads=["T4", "T5"], writes=["SGA"])
                else:
                    blk = cc - 10
                    P.op("dve", lambda e, blk=blk: e.tensor_tensor(out=SGL[:, blk, 0:n], in0=tg[:, 0:n], in1=te[:, 0:n], op=ALU.mult),
                         reads=["T4", "T5"], writes=["SGL%d" % blk])
            elif cc in (8, 9):
                blk = cc - 8
                P.op("dve", lambda e, ps=ps, blk=blk: e.tensor_tensor(
                    out=U[:, blk, 3:3 + n], in0=ps[:, 0:n], in1=RSTD[rb][:, 0:n], op=ALU.mult),
                    reads=[psk, rk], writes=["U%d" % blk])
            if ti + 1 < NT and idx >= 1 and idx % 2 == 1 and nstat[0] < 4:
                stats_piece(ti + 1, nstat[0])
                nstat[0] += 1

        hooks = make_hooks_for(ti)
        for idx, cc in enumerate(order):
            in_chunk(idx, cc)
            for h in hooks.pop(idx, []):
                h()
        while pend_tr:
            pend_tr.pop(0)()
        while ti + 1 < NT and nstat[0] < 4:
            stats_piece(ti + 1, nstat[0])
            nstat[0] += 1

        for idx in sorted(hooks.keys()):
            for h in hooks[idx]:
                h()

    def make_hooks_for(ti):
        hooks = {}
        if ti >= 1:
            hooks[1] = [lambda: lru_B1(ti - 1, 0)]
            hooks[3] = [lambda: lru_B2(ti - 1, 0)]
            hooks[4] = [lambda: lru_B1(ti - 1, 1)]
            hooks[6] = [lambda: lru_B2(ti - 1, 1)]
            hooks[9] = [lambda: lru_gather(ti - 1)]
        hooks[7] = [lambda: lru_A(ti, 0), lambda: lru_A(ti, 1)]
        return hooks

    UCB = [TT[0], TT6]
    UCK = ["T0", "T6"]
    UCBFS = [UCBF, UCBF2]
    UCBFK = ["UCBF0", "UCBF1"]

    def lru_A(ti, blk):
        tok0, n = tile_tok(ti)
        uk = "U%d" % blk
        uc, uck = UCB[blk], UCK[blk]
        cw = 16 + 4 * blk
        P.op("dve", lambda e: e.tensor_scalar(
            out=uc[:, 0:n], in0=U[:, blk, 3:3 + n], scalar1=SM[:, cw + 3:cw + 4], scalar2=SM[:, 24 + blk:25 + blk],
            op0=ALU.mult, op1=ALU.add), reads=[uk, "SM_in"], writes=[uck])
        for kk in (2, 1, 0):
            P.op("dve", lambda e, kk=kk: e.scalar_tensor_tensor(
                out=uc[:, 0:n], in0=U[:, blk, kk:kk + n], scalar=SM[:, cw + kk:cw + kk + 1], in1=uc[:, 0:n],
                op0=ALU.mult, op1=ALU.add), reads=[uk, "SM_in", uck], writes=[uck])
        P.op("pool", lambda e: e.tensor_copy(out=U[:, blk, 0:3], in_=U[:, blk, n:n + 3]), reads=[uk], writes=[uk])
        P.op("act", lambda e: e.activation(out=UCBFS[blk][:, 0:n], in_=uc[:, 0:n], func=AF.Copy),
             reads=[uck], writes=[UCBFK[blk]])

    def lru_B1(ti, blk):
        tok0, n = tile_tok(ti)
        tr_, ti_, ta, tm = TT[1], TT[2], TT[3], TT[4]

        def gmm(e):
            e.matmul(pss[0][:, 0:n], lhsT=WAX[:, 2 * blk, :], rhs=UCBFS[blk][:, 0:n], start=True, stop=True)
            return e.matmul(pss[0][:, 512:512 + n], lhsT=WAX[:, 2 * blk + 1, :], rhs=UCBFS[blk][:, 0:n], start=True, stop=True)
        P.op("pe", gmm, reads=["WAX", UCBFK[blk]], writes=["pss0"])
        for gi, (tdst, tkey, off) in enumerate(((tr_, "T1", 0), (ti_, "T2", 512))):
            P.op("act", lambda e, tdst=tdst, off=off, gi=gi: e.activation(
                out=tdst[:, 0:n], in_=pss[0][:, off:off + n], func=AF.Exp, scale=-1.0,
                bias=SM[:, 44 + 2 * gi + blk:45 + 2 * gi + blk]), reads=["pss0", "SM_nb"], writes=[tkey])
            P.op("act", lambda e, tdst=tdst: e.activation(out=tdst[:, 0:n], in_=tdst[:, 0:n], func=AF.Ln, bias=1.0),
                 reads=[tkey], writes=[tkey])
            P.op("act", lambda e, tdst=tdst: e.activation(out=tdst[:, 0:n], in_=tdst[:, 0:n], func=AF.Exp, scale=-1.0),
                 reads=[tkey], writes=[tkey])
        P.op("act", lambda e: e.activation(out=ta[:, 0:n], in_=tr_[:, 0:n], func=AF.Exp, scale=SM[:, 40 + blk:41 + blk]),
             reads=["T1", "SM_a"], writes=["T3"])
        P.op("act", lambda e: e.activation(out=tm[:, 0:n], in_=tr_[:, 0:n], func=AF.Exp, scale=SM[:, 42 + blk:43 + blk]),
             reads=["T1", "SM_c2"], writes=["T4"])
        P.op("act", lambda e: e.activation(out=tm[:, 0:n], in_=tm[:, 0:n], func=AF.Ln, scale=-1.0, bias=1.0),
             reads=["T4"], writes=["T4"])
        P.op("act", lambda e: e.activation(out=tm[:, 0:n], in_=tm[:, 0:n], func=AF.Exp, scale=0.5),
             reads=["T4"], writes=["T4"])

    def lru_B2(ti, blk):
        tok0, n = tile_tok(ti)
        meta = (ti == 0)
        ci = ti - 1
        uc, uck = UCB[blk], UCK[blk]
        tr_, ti_, ta, tm = TT[1], TT[2], TT[3], TT[4]
        if meta:
            P.op("dve", lambda e: e.memset(tm[:, 0:1], 1.0), reads=["T4"], writes=["T4"])
        P.op("dve", lambda e: e.tensor_tensor(out=ti_[:, 0:n], in0=ti_[:, 0:n], in1=tm[:, 0:n], op=ALU.mult),
             reads=["T2", "T4"], writes=["T2"])
        P.op("dve", lambda e: e.tensor_tensor(out=ti_[:, 0:n], in0=ti_[:, 0:n], in1=uc[:, 0:n], op=ALU.mult),
             reads=["T2", uck], writes=["T2"])
        P.op("dve", lambda e: e.tensor_tensor_scan(
            out=tr_[:, 0:n], data0=ta[:, 0:n], data1=ti_[:, 0:n], initial=HST[:, blk:blk + 1], op0=ALU.mult, op1=ALU.add),
            reads=["T3", "T2", "HST%d" % blk, "T1"], writes=["T1"])
        P.op("dve", lambda e: e.tensor_copy(out=HST[:, blk:blk + 1], in_=tr_[:, n - 1:n]),
             reads=["T1"], writes=["HST%d" % blk])
        if not meta:
            lb = ci % 2
            P.op("dve", lambda e: e.tensor_tensor(out=LO[lb][:, blk, :], in0=tr_[:, 0:n], in1=SGL[:, blk, 0:n], op=ALU.mult),
                 reads=["T1", "SGL%d" % blk], writes=["LO%d" % lb])
            if blk == 1:
                P.dma(lambda e: [e.dma_start(out=aginl[ci].rearrange("(b p) t -> p b t", p=128), in_=LO[lb][:, :, :])],
                      "d_lo%d" % lb, 1, reads=["LO%d" % lb], writes=["AGINL%d" % ci])

    def lru_gather(ti):
        ci = ti - 1
        if ci < 0:
            return
        P.cc(lambda e: e.collective_compute("AllGather", ALU.bypass, replica_groups=[[0, 1, 2, 3], [4, 5, 6, 7]],
                                            ins=[aginl[ci]], outs=[agoutl[ci]]),
             "cc_l", reads=["AGINL%d" % ci], writes=["AGOUTL%d" % ci])

    for ti in range(NT):
        phase1_tile(ti)
    lru_B1(NT - 1, 0)
    lru_B2(NT - 1, 0)
    lru_B1(NT - 1, 1)
    lru_B2(NT - 1, 1)
    lru_gather(NT - 1)

    if DEBUG:
        P.dma(lambda e: [e.dma_start(out=dbg["qt"], in_=QT[:, :, :].rearrange("p a b -> p (a b)")),
                         e.dma_start(out=dbg["kt"], in_=KT[:, :, :].rearrange("p a b -> p (a b)")),
                         e.dma_start(out=dbg["vv"], in_=VV[:, :, :].rearrange("p a b -> p (a b)")),
                         e.dma_start(out=dbg["sga"], in_=SGA[:, :, :].rearrange("p a b -> p (a b)"))],
              "d_dbg", 4, reads=["QT", "KT", "VV", "SGA"], writes=["DBG"])

    P.barrier(skip_prefix="cc_")
    w_out_v = w_out.rearrange("(c p) n -> p c n", p=128)
    P.dma(lambda e: [e.dma_start(out=WOST32[:, 4 * q_:4 * q_ + 4, :], in_=w_out_v[:, 4 * q_:4 * q_ + 4, :]) for q_ in range(4)],
          "d_wo", 4, writes=["WOST32"])
    WO_CAST = [False]

    def cast_wo():
        if WO_CAST[0]:
            return
        WO_CAST[0] = True
        for q_ in range(4):
            P.op("dve", lambda e, q_=q_: e.tensor_copy(out=WO[:, 4 * q_:4 * q_ + 4, :], in_=WOST32[:, 4 * q_:4 * q_ + 4, :]),
                 reads=["WOST32"], writes=["WO"])
    P.dma(lambda e: [e.dma_start(out=FG[:, :], in_=fg_in)], "d_fg", 1, writes=["FG"])
    P.op("dve", lambda e: e.memset(QBD[0][:, :], 0.0), writes=["QBD0"])
    P.op("dve", lambda e: e.memset(QBD[1][:, :], 0.0), writes=["QBD1"])

    pt_rr = [0]
    s_rr = [0]

    PROC = [0, 1, 2, 3, 4, 5, 6, 7]
    POS = {c_: p_ for p_, c_ in enumerate(PROC)}

    SCH = {"t": 0.0, "cc_free": 0.0, "inj": None, "inj_chunk": None, "bseq": 0}
    pend_B = []
    pend_F = []
    pend_fin2 = []
    BSEQ = {}
    CC_A, CC_B, LOAD_LAT = 48.0, 14.0, 12.0

    def attention(i):
        mb = POS[i] % 2
        for hh in range(2):
            for qs in range(2):
                att_tile(i, mb, hh, qs)
        def fin_chunk():
            P.dma(lambda e: [e.dma_start(out=agin[i].rearrange("(b p) t -> p b t", p=128), in_=MIX[mb][:, :, :])],
                  "d_mix%d" % mb, 1, reads=["MIX%d" % mb], writes=["AGIN%d_a" % i])
            P.cc(lambda e: e.collective_compute("AllGather", ALU.bypass, replica_groups=[[0, 1, 2, 3], [4, 5, 6, 7]],
                                                ins=[agin[i]], outs=[agout[i]]),
                 "cc_a", reads=["AGIN%d_a" % i], writes=["AGOUT%d" % i])
            st = max(SCH["cc_free"], SCH["t"] + 2.0)
            SCH["cc_free"] = st + CC_A
            pend_B.append((i, SCH["cc_free"]))
        pend_post.append(fin_chunk)

    def sched_block(cost):
        SCH["t"] += cost
        if SCH["inj"] is None and pend_B and pend_B[0][1] + 8.0 <= SCH["t"]:
            k_, _r = pend_B.pop(0)
            start_B(k_)
        if SCH["inj"] is not None and SCH["inj_ready"] <= SCH["t"]:
            run_units(1)
        if SCH["inj"] is not None and pend_B and pend_B[0][1] + 8.0 <= SCH["t"]:
            prefetch_mixg(pend_B[0][0])

    PREF = set()

    def assign_seq(k_):
        if k_ not in BSEQ:
            BSEQ[k_] = SCH["bseq"]
            SCH["bseq"] += 1

    def prefetch_mixg(k_):
        if k_ in PREF:
            return
        assign_seq(k_)
        PREF.add(k_)
        outproj_prep_mixg(k_)

    def start_B(k_):
        assign_seq(k_)
        if BSEQ[k_] >= 2:
            force_finalize_upto(BSEQ[k_] - 2)
        cast_wo()
        pre = k_ in PREF
        prefetch_mixg(k_)
        outproj_prep(k_)
        SCH["inj"] = outproj_units(k_)
        SCH["inj_chunk"] = k_
        SCH["inj_ready"] = SCH["t"] + (6.0 if pre else LOAD_LAT)

    def run_units(n_):
        for _ in range(n_):
            if SCH["inj"] is None:
                return
            try:
                next(SCH["inj"])
                SCH["t"] += 0.12
            except StopIteration:
                k_ = SCH["inj_chunk"]
                SCH["inj"] = None
                st = max(SCH["cc_free"], SCH["t"] + 2.0)
                SCH["cc_free"] = st + CC_B
                pend_F.append((k_, SCH["cc_free"]))

    FIN_DONE = set()

    def force_finalize_upto(seq):
        for k_, sq_ in list(BSEQ.items()):
            if sq_ <= seq and k_ not in FIN_DONE:
                for it in list(pend_F):
                    if it[0] == k_:
                        pend_F.remove(it)
                do_finalize(k_, split=False)

    def do_finalize(k_, split):
        FIN_DONE.add(k_)
        finalize1(k_)
        if split:
            pend_fin2.append(lambda: finalize2(k_))
        else:
            finalize2(k_)

    def sched_tile_start():
        if pend_F and pend_F[0][1] + 45.0 <= SCH["t"]:
            k_, _r = pend_F.pop(0)
            if k_ not in FIN_DONE:
                do_finalize(k_, split=True)

    tile_rr = [0]
    LG = 12
    TILES = [(i_, hh_, qs_) for i_ in PROC for hh_ in range(2) for qs_ in range(2)]
    pend_post = []

    def emit_qbd(t):
        if t >= len(TILES):
            return
        i_, hh_, qs_ = TILES[t]
        q0_ = 256 * (2 * i_ + qs_)
        qb_ = t % 2
        P.op("pool", lambda e: e.tensor_copy(out=QBD[qb_][0:64, 0:256], in_=QT[0:64, hh_, q0_:q0_ + 256]),
             reads=["QT"], writes=["QBD%d" % qb_])
        P.op("pool", lambda e: e.tensor_copy(out=QBD[qb_][64:128, 256:512], in_=QT[64:128, hh_, q0_:q0_ + 256]),
             reads=["QT"], writes=["QBD%d" % qb_])

    def att_tile(i, mb, hh, qs):
        sched_tile_start()
        qi = 2 * i + qs
        q0 = 256 * qi
        tix = tile_rr[0]
        tile_rr[0] += 1
        ob = tix % 2
        psO = psb[0] if ob == 0 else pss[1][:, 0:512]
        psOk = "psb0" if ob == 0 else "pss1a"
        qb = tix % 2
        qbk = "QBD%d" % qb
        if tix == 0:
            emit_qbd(0)
        emit_qbd(tix + 1)
        blocks = [("real", m) for m in range(2 * qi + 2)] + [("meta", None)]
        nb = len(blocks)
        nreal = nb - 1
        grp_first = [None]
        linfo = {}
        l_started = [False]

        def s_stage(nidx):
            kind, m = blocks[nidx]
            sb = s_rr[0] % 3
            s_rr[0] += 1
            if kind == "meta":
                M, k0 = NM, 0
                spec = 0 if qi == 0 else None
            else:
                M, k0 = 128, NM + 128 * m
                spec = {2 * qi - 1: 1, 2 * qi: 2, 2 * qi + 1: 3}.get(m)
            psS = (pss[0][:, 0:512], pss[0][:, 512:1024], pss[1][:, 512:1024])[sb]
            psSk = ("pss0a", "pss0b", "pss1b")[sb]

            def mm(e):
                r = e.matmul(psS[0:M, :], lhsT=KT[:, hh, k0:k0 + M], rhs=QBD[qb][:, :], start=True, stop=(spec is None))
                if spec is not None:
                    r = e.matmul(psS[0:M, :], lhsT=IDB[0:M, 0:M], rhs=BIAS[0:M, hh, spec, :], start=False, stop=True)
                return r
            P.op("pe", mm, reads=["KT", qbk, "IDB", "BIAS"], writes=[psSk])
            pb = pt_rr[0] % NPT
            pt_rr[0] += 1
            ptk = "PT%d" % pb
            ptf = PT[pb][0:M, :, :].rearrange("p a b -> p (a b)")
            if spec is None:
                P.op("act", lambda e: e.activation(out=ptf, in_=psS[0:M, :], func=AF.Exp, bias=SM[0:M, c31col[hh]:c31col[hh] + 1]),
                     reads=[psSk, "SM_in"], writes=[ptk])
            else:
                P.op("act", lambda e: e.activation(out=ptf, in_=psS[0:M, :], func=AF.Exp), reads=[psSk], writes=[ptk])
            lrhs = None
            if kind == "meta":
                lrhs = (ptf, [ptk], M)
            else:
                g, pos = divmod(nidx, LG)
                gs = GS[g % 2]
                gk = "GS%d" % (g % 2)
                last_in_group = (pos == LG - 1) or (nidx == nreal - 1)
                if pos == 0:
                    grp_first[0] = (ptf, ptk)
                    if last_in_group:
                        lrhs = (ptf, [ptk], M)
                elif pos == 1:
                    f_ap, f_k = grp_first[0]
                    P.op("dve", lambda e: e.tensor_tensor(out=gs[:, :], in0=f_ap, in1=ptf, op=ALU.add),
                         reads=[f_k, ptk], writes=[gk])
                else:
                    P.op("dve", lambda e: e.tensor_tensor(out=gs[:, :], in0=gs[:, :], in1=ptf, op=ALU.add),
                         reads=[gk, ptk], writes=[gk])
                if pos >= 1 and last_in_group:
                    lrhs = (gs[:, :], [gk], 128)
            linfo[nidx] = lrhs
            return (pb, M, kind, m)

        def pv_stage(nidx, info):
            pb, M, kind, m = info
            vb = 0 if kind == "meta" else 1 + m
            rhs = PT[pb][0:M, :, :].rearrange("p a b -> p (a b)")
            P.op("pe", lambda e: e.matmul(psO[:, :], lhsT=VV[0:M, vb, hh * 128:(hh + 1) * 128], rhs=rhs,
                                          start=(nidx == 0), stop=(nidx == nb - 1)),
                 reads=["VV", "PT%d" % pb], writes=[psOk])
            lr = linfo.pop(nidx)
            if lr is not None:
                l_ap, l_keys, lM = lr
                first = not l_started[0]
                l_started[0] = True
                P.op("pe", lambda e: e.matmul(psb[1][:, :], lhsT=ONES[0:lM, :], rhs=l_ap, start=first, stop=(nidx == nb - 1)),
                     reads=["ONES"] + l_keys, writes=["psb1"])
            sched_block(0.85)
            if nidx == 2:
                while pend_post:
                    pend_post.pop(0)()
            if nidx == 4 and pend_fin2:
                pend_fin2.pop(0)()

        infos = {}
        AHEAD = 2
        for nidx in range(min(AHEAD, nb)):
            infos[nidx] = s_stage(nidx)
        for nidx in range(nb):
            if nidx + AHEAD < nb:
                infos[nidx + AHEAD] = s_stage(nidx + AHEAD)
            pv_stage(nidx, infos.pop(nidx))
        P.op("act", lambda e: e.activation(out=RL[:, :], in_=psb[1][:, :], func=AF.Ln), reads=["psb1"], writes=["RL"])
        P.op("act", lambda e: e.activation(out=RL[:, :], in_=RL[:, :], func=AF.Exp, scale=-1.0), reads=["RL"], writes=["RL"])
        P.op("dve", lambda e: e.tensor_tensor(out=ON[:, :], in0=psO[:, :], in1=RL[:, :], op=ALU.mult),
             reads=[psOk, "RL"], writes=["ON"])
        P.op("dve", lambda e: e.scalar_tensor_tensor(out=DIFF[:, :], in0=ON[:, 256:512], scalar=SM[:, 48:49], in1=ON[:, 0:256],
                                                     op0=ALU.mult, op1=ALU.add), reads=["ON", "SM_nl"], writes=["DIFF"])
        P.op("pool", lambda e: e.tensor_tensor(out=SQ[:, :], in0=DIFF[:, :], in1=DIFF[:, :], op=ALU.mult),
             reads=["DIFF"], writes=["SQ"])
        def post2():
            P.op("pe", lambda e: e.matmul(psb[2][:, 0:256], lhsT=ONESM[:, :], rhs=SQ[:, :], start=True, stop=True),
                 reads=["SQ", "ONESM"], writes=["psb2"])
            P.op("act", lambda e: e.activation(out=R2[:, :], in_=psb[2][:, 0:256], func=AF.Ln, bias=1e-5),
                 reads=["psb2"], writes=["R2"])
            P.op("act", lambda e: e.activation(out=R2[:, :], in_=R2[:, :], func=AF.Exp, scale=-0.5),
                 reads=["R2"], writes=["R2"])
            P.op("dve", lambda e: e.tensor_tensor(out=T1[:, :], in0=DIFF[:, :], in1=R2[:, :], op=ALU.mult),
                 reads=["DIFF", "R2"], writes=["T1x"])
            P.op("dve", lambda e: e.scalar_tensor_tensor(
                out=MIX[mb][:, hh, qs * 256:(qs + 1) * 256], in0=T1[:, :], scalar=SM[:, 49:50], in1=SGA[:, hh, q0:q0 + 256],
                op0=ALU.mult, op1=ALU.mult), reads=["T1x", "SM_gs", "SGA"], writes=["MIX%d" % mb])
        pend_post.append(post2)
        while pend_fin2:
            pend_fin2.pop(0)()

    def outproj_prep_mixg(i):
        gb = BSEQ[i] % 2
        P.dma(lambda e: [e.dma_start(out=MIXG[gb][:, 0:8, :], in_=agout[i].rearrange("(c p) t -> p c t", p=128)),
                         e.dma_start(out=MIXG[gb][:, 8:16, :], in_=agoutl[i].rearrange("(c p) t -> p c t", p=128))],
              "d_mg%d" % gb, 2, reads=["AGOUT%d" % i, "AGOUTL%d" % i], writes=["MIXG%d" % gb])

    def outproj_prep(i):
        P.dma(lambda e: [e.dma_start(out=XRES[:, :, :], in_=x_res[512 * i:512 * (i + 1), :].rearrange("(b p) n -> p b n", p=128))],
              "d_xr", 1, writes=["XRES"])
        P.op("dve", lambda e: e.memset(SSQ[:, :], 0.0), writes=["SSQ"])

    def outproj_units(i):
        gb = BSEQ[i] % 2
        yb = BSEQ[i] % 2
        for tb in range(4):
            for c in range(16):
                P.op("pe", lambda e, tb=tb, c=c: e.matmul(
                    psb[3][:, :], lhsT=MIXG[gb][:, c, tb * 128:(tb + 1) * 128], rhs=WO[:, c, :], start=(c == 0), stop=(c == 15)),
                    reads=["MIXG%d" % gb, "WO"], writes=["psb3"])
                if c == 15:
                    P.op("dve", lambda e, tb=tb: e.tensor_tensor(out=YB[yb][:, tb, :], in0=psb[3][:, :], in1=XRES[:, tb, :], op=ALU.add),
                         reads=["psb3", "XRES"], writes=["YB%d_%d" % (yb, tb)])
                    P.op("act", lambda e, tb=tb: e.activation(out=JUNK[:, :], in_=YB[yb][:, tb, :], func=AF.Square,
                                                              accum_out=SSQ[:, tb:tb + 1]),
                         reads=["YB%d_%d" % (yb, tb)], writes=["JUNK", "SSQ"])
                yield
        P.dma(lambda e: [e.dma_start(out=sqin[i], in_=SSQ[:, :])], "d_sq", 1, reads=["SSQ"], writes=["SQIN%d" % i])
        P.cc(lambda e: e.collective_compute("AllGather", ALU.bypass, replica_groups=[[0, 1, 2, 3], [4, 5, 6, 7]],
                                            ins=[sqin[i]], outs=[sqout[i]]),
             "cc_b", reads=["SQIN%d" % i], writes=["SQOUT%d" % i])

    def finalize1(i):
        P.dma(lambda e: [e.dma_start(out=SQG[:, :, :], in_=sqout[i].rearrange("(r p) b -> p r b", p=128))],
              "d_sqg", 1, reads=["SQOUT%d" % i], writes=["SQG"])
        P.op("dve", lambda e: e.tensor_tensor(out=TOT[:, :], in0=SQG[:, 0, :], in1=SQG[:, 1, :], op=ALU.add),
             reads=["SQG"], writes=["TOT"])
        P.op("dve", lambda e: e.tensor_tensor(out=TOT[:, :], in0=TOT[:, :], in1=SQG[:, 2, :], op=ALU.add),
             reads=["SQG", "TOT"], writes=["TOT"])
        P.op("dve", lambda e: e.tensor_tensor(out=TOT[:, :], in0=TOT[:, :], in1=SQG[:, 3, :], op=ALU.add),
             reads=["SQG", "TOT"], writes=["TOT"])

    def finalize2(i):
        yb = BSEQ[i] % 2
        P.op("act", lambda e: e.activation(out=RS[:, :], in_=TOT[:, :], func=AF.Ln, scale=1.0 / 2048.0, bias=1e-6),
             reads=["TOT"], writes=["RS"])
        P.op("act", lambda e: e.activation(out=RS[:, :], in_=RS[:, :], func=AF.Exp, scale=-0.5), reads=["RS"], writes=["RS"])
        for tb in range(4):
            P.op("dve", lambda e, tb=tb: e.scalar_tensor_tensor(
                out=YB[yb][:, tb, :], in0=YB[yb][:, tb, :], scalar=RS[:, tb:tb + 1], in1=FG[:, :], op0=ALU.mult, op1=ALU.mult),
                reads=["YB%d_%d" % (yb, tb), "RS", "FG"], writes=["YB%d_%d" % (yb, tb)])
        P.dma(lambda e: [e.dma_start(out=out[512 * i:512 * (i + 1), :].rearrange("(b p) n -> p b n", p=128), in_=YB[yb][:, :, :])],
              "d_out%d" % yb, 1, reads=["YB%d_%d" % (yb, tb) for tb in range(4)], writes=["OUT%d" % i])

    for ai in PROC:
        attention(ai)
    while pend_post:
        pend_post.pop(0)()
    while pend_B or SCH["inj"] is not None:
        if SCH["inj"] is None:
            k_, r_ = pend_B.pop(0)
            SCH["t"] = max(SCH["t"], r_)
            start_B(k_)
            SCH["t"] = max(SCH["t"], SCH["inj_ready"])
        if pend_B and pend_B[0][1] <= SCH["t"] + 16.0:
            prefetch_mixg(pend_B[0][0])
        run_units(64)
    while pend_F:
        k_, _r = pend_F.pop(0)
        if k_ not in FIN_DONE:
            do_finalize(k_, split=False)
    if DEBUG:
        P.dma(lambda e: [e.dma_start(out=dbg["ag0"], in_=agout[0])], "d_dbg2", 1, reads=["AGOUT0"], writes=["DBG2"])
    P.barrier()

    with ExitStack() as es:
        sems = {}
        for sname in P.cnt.keys():
            sems[sname] = es.enter_context(nc.semaphore("s_" + sname))
        block = es.enter_context(nc.Block())

        def run(e, eng):
            for waits, fn, sem, inc in P.ops[eng]:
                for (s_, v) in waits:
                    e.wait_ge(sems[s_], v)
                if fn is None:
                    continue
                r = fn(e)
                if isinstance(r, (list, tuple)):
                    for ins in r:
                        ins.then_inc(sems[sem], inc)
                else:
                    r.then_inc(sems[sem], inc)

        block.sync(lambda e: run(e, "sp"))
        block.tensor(lambda e: run(e, "pe"))
        block.scalar(lambda e: run(e, "act"))
        block.vector(lambda e: run(e, "dve"))
        block.gpsimd(lambda e: run(e, "pool"))
    return nc


def _bucket(dist):
    d = np.maximum(dist, 0).astype(np.int64)
    large = 16 + (np.log(np.maximum(d, 1).astype(np.float32) / np.float32(16.0)) / np.float32(math.log(128 / 16)) * np.float32(16)).astype(np.int32)
    large = np.minimum(large, 31)
    return np.where(d < 16, d, large).astype(np.int64)


def _bias_tiles(rel_bias, h):
    p = np.arange(128)[:, None]
    c = np.arange(256)[None, :]
    tiles = np.zeros((4, 128, 256), np.float32)
    d0 = 16 + c - p
    tiles[0] = rel_bias[_bucket(d0), h]
    d1 = c + 128 - p
    tiles[1] = rel_bias[_bucket(d1), h]
    d2 = c - p
    tiles[2] = np.where(d2 >= 0, rel_bias[_bucket(d2), h], np.float32(MASKV))
    d3 = c - 128 - p
    tiles[3] = np.where(d3 >= 0, rel_bias[_bucket(d3), h], np.float32(MASKV))
    return tiles


def _prep_inputs(x, meta_tokens, rel_bias, norm_g, w_in, conv_w, conv_b, w_a, b_a, w_x, b_x,
                 lru_lambda, lam_q1, lam_k1, lam_q2, lam_k2, subln_g, w_out, final_g):
    f = lambda a: np.asarray(a, dtype=np.float32)
    x, meta_tokens, rel_bias, norm_g, w_in = f(x), f(meta_tokens), f(rel_bias), f(norm_g), f(w_in)
    conv_w, conv_b, w_a, b_a, w_x, b_x = f(conv_w), f(conv_b), f(w_a), f(b_a), f(w_x), f(b_x)
    lru_lambda, subln_g, w_out, final_g = f(lru_lambda), f(subln_g), f(w_out), f(final_g)
    lamv = np.concatenate([f(lam_q1)[0], f(lam_k1)[0], f(lam_q2)[0], f(lam_k2)[0]])
    lamv = np.ascontiguousarray(np.broadcast_to(lamv[None, :], (128, 256)))
    ident = np.eye(128, dtype=np.float32)
    xTs = [np.ascontiguousarray(np.concatenate([meta_tokens, x[b]], axis=0).T) for b in range(2)]
    in_maps = []
    pidx = np.arange(128)
    for c in range(8):
        b, j = divmod(c, 4)
        hs = [2 * j, 2 * j + 1]
        cols = []
        for base in (0, 1024, 2048, 3072, 4096, 5120):
            for h in hs:
                cols.append(np.arange(base + h * 128, base + (h + 1) * 128))
        cols = np.concatenate(cols)
        w_in_c = np.ascontiguousarray(w_in[0][:, cols])
        w_out_c = np.ascontiguousarray(w_out[0][:, 512 * j:512 * (j + 1)])
        x_res = np.ascontiguousarray(x[b][:, 512 * j:512 * (j + 1)])
        fg = np.ascontiguousarray(np.broadcast_to(final_g[None, 512 * j:512 * (j + 1)], (128, 512)))
        sm = np.zeros((128, 40), np.float32)
        sm[:, 0:16] = norm_g[0].reshape(16, 128).T
        for bl in range(2):
            ch = hs[bl] * 128 + pidx
            for k in range(4):
                sm[:, 16 + 4 * bl + k] = conv_w[0][k, ch]
            sm[:, 24 + bl] = conv_b[0][ch]
            sm[:, 26 + bl] = b_a[0][ch]
            sm[:, 28 + bl] = b_x[0][ch]
            sm[:, 30 + bl] = lru_lambda[0][ch]
        sm[:, 32] = subln_g[0]
        sm[:, 33] = rel_bias[31, hs[0]]
        sm[:, 34] = rel_bias[31, hs[1]]
        wax = np.zeros((128, 4, 128), np.float32)
        for bl in range(2):
            wax[:, 2 * bl + 0, :] = w_a[0][hs[bl]]
            wax[:, 2 * bl + 1, :] = w_x[0][hs[bl]]
        bt = np.zeros((128, 2, 4, 256), np.float32)
        for hh in range(2):
            bt[:, hh] = np.transpose(_bias_tiles(rel_bias, hs[hh]), (1, 0, 2))
        in_maps.append({
            "xT": xTs[b], "w_in": w_in_c, "w_out": w_out_c, "x_res": x_res, "fg": fg, "smalls": sm,
            "lamv": lamv, "wax": np.ascontiguousarray(wax.reshape(128, 512)),
            "biast": np.ascontiguousarray(bt.reshape(128, 2048)), "ident": ident,
        })
    return in_maps


def kernel(**inputs):
    in_maps = _prep_inputs(**inputs)
    nc = build_program()
    res = run_bass_kernel_spmd(nc, in_maps, core_ids=list(range(8)))
    outp = np.zeros((2, S, D), np.float32)
    for c in range(8):
        b, j = divmod(c, 4)
        outp[b, :, 512 * j:512 * (j + 1)] = np.asarray(res.results[c]["out"], dtype=np.float32)
    return outp
```

```python
import math
from contextlib import ExitStack
import numpy as np
import concourse.bass as bass
import concourse.mybir as mybir
from concourse.bass_utils import run_bass_kernel_spmd

F32 = mybir.dt.float32
BF16 = mybir.dt.bfloat16
ALU = mybir.AluOpType
AF = mybir.ActivationFunctionType

D = 2048
NM = 16
S = 4096
T = S + NM
NT = 9
ENGS = ("pe", "act", "dve", "pool", "sp")
LAM_INIT = 0.8 - 0.6 * math.exp(0.0)
MASKV = -30000.0
DEBUG = False


class Prog:
    def __init__(self):
        self.ops = {e: [] for e in ENGS}
        self.cnt = {}
        self.waited = {e: {} for e in ENGS}
        self.lastw = {}
        self.readers = {}

    def _emit(self, eng, fn, reads, writes, sem, inc, ninst):
        deps = []
        for k in reads:
            if k in self.lastw:
                deps.append(self.lastw[k])
        for k in writes:
            if k in self.lastw:
                deps.append(self.lastw[k])
            deps.extend(self.readers.get(k, ()))
        waits = {}
        for (s, v) in deps:
            if s == "pe" and eng == "pe":
                continue
            if self.waited[eng].get(s, 0) >= v:
                continue
            if waits.get(s, 0) < v:
                waits[s] = v
        for s, v in waits.items():
            self.waited[eng][s] = v
        self.cnt[sem] = self.cnt.get(sem, 0) + inc * ninst
        tk = (sem, self.cnt[sem])
        self.ops[eng].append((sorted(waits.items()), fn, sem, inc))
        for k in writes:
            self.lastw[k] = tk
            self.readers[k] = []
        for k in reads:
            self.readers.setdefault(k, []).append(tk)
        return tk

    def op(self, eng, fn, reads=(), writes=()):
        return self._emit(eng, fn, reads, writes, eng, 1, 1)

    def dma(self, fn, sem, n, reads=(), writes=()):
        return self._emit("sp", fn, reads, writes, sem, 16, n)

    def dma_pool(self, fn, sem, n, reads=(), writes=()):
        return self._emit("pool", fn, reads, writes, sem, 16, n)

    def cc(self, fn, sem, reads=(), writes=()):
        return self._emit("pool", fn, reads, writes, sem, 1, 1)

    def barrier(self, skip_prefix=None):
        allk = [(s_, v_) for s_, v_ in self.cnt.items() if not (skip_prefix and s_.startswith(skip_prefix))]
        for e in ENGS:
            waits = []
            for s, v in allk:
                if self.waited[e].get(s, 0) < v and not (s == e == "pe"):
                    waits.append((s, v))
                    self.waited[e][s] = v
            if waits:
                self.ops[e].append((sorted(waits), None, None, 0))


def build_program():
    nc = bass.Bass("TRN2", target_bir_lowering=False)
    P = Prog()

    def din(name, shape, dt=F32):
        return nc.dram_tensor(name, list(shape), dt, kind="ExternalInput").ap()

    xT = din("xT", [D, T])
    w_in = din("w_in", [D, 1536])
    w_out = din("w_out", [D, 512])
    x_res = din("x_res", [S, 512])
    fg_in = din("fg", [128, 512])
    smalls_in = din("smalls", [128, 40])
    lamv_in = din("lamv", [128, 256])
    wax_in = din("wax", [128, 512])
    biast_in = din("biast", [128, 2048])
    ident_in = din("ident", [128, 128])
    out = nc.dram_tensor("out", [S, 512], F32, kind="ExternalOutput").ap()
    agin = [nc.dram_tensor("agin%d" % i, [256, 512], BF16).ap() for i in range(8)]
    agout = [nc.dram_tensor("agout%d" % i, [1024, 512], BF16).ap() for i in range(8)]
    aginl = [nc.dram_tensor("aginl%d" % i, [256, 512], BF16).ap() for i in range(8)]
    agoutl = [nc.dram_tensor("agoutl%d" % i, [1024, 512], BF16).ap() for i in range(8)]
    sqin = [nc.dram_tensor("sqin%d" % i, [128, 4], F32).ap() for i in range(8)]
    sqout = [nc.dram_tensor("sqout%d" % i, [512, 4], F32).ap() for i in range(8)]
    dbg = {}
    if DEBUG:
        dbg["qt"] = nc.dram_tensor("dbg_qt", [128, 2 * S], BF16, kind="ExternalOutput").ap()
        dbg["kt"] = nc.dram_tensor("dbg_kt", [128, 2 * T], BF16, kind="ExternalOutput").ap()
        dbg["vv"] = nc.dram_tensor("dbg_vv", [128, 33 * 256], BF16, kind="ExternalOutput").ap()
        dbg["sga"] = nc.dram_tensor("dbg_sga", [128, 2 * S], BF16, kind="ExternalOutput").ap()
        dbg["ag0"] = nc.dram_tensor("dbg_ag0", [1024, 512], BF16, kind="ExternalOutput").ap()

    ARENA_BYTES = 207 * 1024
    arena = nc.alloc_sbuf_tensor("arena", [128, ARENA_BYTES // 2], BF16)
    ptr = {"p": 0, 1: 0, 2: 0}

    def carve(phase, shape, dt):
        nel = 1
        for s_ in shape:
            nel *= s_
        nb = nel * (4 if dt == F32 else 2)
        nb_al = (nb + 63) // 64 * 64
        if phase == "p":
            off = ptr["p"]
            ptr["p"] += nb_al
            ptr[1] = ptr[2] = ptr["p"]
        else:
            off = ptr[phase]
            ptr[phase] += nb_al
        assert off + nb_al <= ARENA_BYTES, ("SBUF overflow", phase, off + nb_al)
        v = arena[:, off // 2: off // 2 + nb // 2]
        if dt == F32:
            v = v.bitcast(F32)
        if len(shape) == 2:
            return v.rearrange("p (a b) -> p a b", a=shape[0])
        if len(shape) == 3:
            return v.rearrange("p (a b c) -> p a b c", a=shape[0], b=shape[1])
        return v

    QT = carve("p", [2, S], BF16)
    KT = carve("p", [2, T], BF16)
    VV = carve("p", [33, 256], BF16)
    SGA = carve("p", [2, S], BF16)
    IDB = carve("p", [128], BF16)
    ONES = carve("p", [128], BF16)
    ONESM = carve("p", [128], BF16)
    ONESD = carve("p", [128], BF16)
    BIAS = carve("p", [2, 4, 512], BF16)
    SM = carve("p", [64], F32)
    WAX = carve("p", [4, 128], BF16)
    HST = carve("p", [2], F32)
    WP = carve(1, [16, 1536], BF16)
    XS = [carve(1, [4, 512], F32) for _ in range(2)]
    HB = [carve(1, [16, 512], BF16) for _ in range(2)]
    XSQ = carve(1, [4, 512], BF16)
    RSTD = [carve(1, [512], F32) for _ in range(2)]
    U = carve(1, [2, 520], F32)
    TT = [carve(1, [512], F32) for _ in range(6)]
    UCBF = carve(1, [512], BF16)
    SGL = carve(1, [2, 512], BF16)
    VT = carve(1, [512], BF16)
    LO = [carve(1, [2, 512], BF16) for _ in range(2)]
    TT6 = carve(1, [512], F32)
    UCBF2 = carve(1, [512], BF16)
    LAMV = XSQ[:, 0, :].bitcast(F32)
    WO = carve(2, [16, 512], BF16)
    MIXG = [carve(2, [16, 512], BF16) for _ in range(2)]
    XRES = carve(2, [4, 512], F32)
    YB = [carve(2, [4, 512], F32) for _ in range(2)]
    FG = carve(2, [512], F32)
    NPT = 6
    PT = [carve(2, [2, 256], BF16) for _ in range(NPT)]
    RL = carve(2, [512], F32)
    ON = carve(2, [512], F32)
    DIFF = carve(2, [256], F32)
    T1 = carve(2, [256], F32)
    R2 = carve(2, [256], F32)
    SQ = carve(2, [256], BF16)
    MIX = [carve(2, [2, 512], BF16) for _ in range(2)]
    SSQ = carve(2, [4], F32)
    SQG = carve(2, [4, 4], F32)
    TOT = carve(2, [4], F32)
    RS = carve(2, [4], F32)
    JUNK = carve(2, [512], F32)
    WOST32 = carve(2, [16, 512], F32)
    GS = [carve(2, [512], BF16) for _ in range(2)]
    QBD = [carve(2, [512], BF16) for _ in range(2)]

    psb = [nc.alloc_psum_tensor("psb%d" % i, [128, 512], F32) for i in range(4)]
    pss = [nc.alloc_psum_tensor("pss%d" % i, [128, 1024], F32) for i in range(2)]
    PSK = ["psb0", "psb1", "psb2", "psb3", "pss0", "pss1"]

    c31col = [33, 34]

    P.dma(lambda e: [e.dma_start(out=SM[:, 0:40], in_=smalls_in)], "d_sm", 1, writes=["SM_in"])
    P.dma(lambda e: [e.dma_start(out=LAMV[:, :], in_=lamv_in)], "d_lamv", 1, writes=["LAMV", "XSQ"])
    P.dma(lambda e: [e.dma_start(out=TT[0][:, 0:128], in_=ident_in)], "d_id", 1, writes=["T0"])
    P.op("dve", lambda e: e.tensor_copy(out=IDB[:, :], in_=TT[0][:, 0:128]), reads=["T0"], writes=["IDB"])
    P.op("dve", lambda e: e.memset(ONES[:, :], 1.0), writes=["ONES"])
    P.op("dve", lambda e: e.memset(ONESM[:, :], 1.0 / 128.0), writes=["ONESM"])
    P.op("dve", lambda e: e.memset(ONESD[:, :], 1.0 / 2048.0), writes=["ONESD"])
    P.op("dve", lambda e: e.memset(U[:, :, :], 0.0), writes=["U0", "U1"])
    P.op("dve", lambda e: e.memset(HST[:, :], 0.0), writes=["HST0", "HST1"])
    P.dma(lambda e: [e.dma_start(out=TT[1][:, :], in_=wax_in)], "d_wax", 1, writes=["T1"])
    P.op("dve", lambda e: e.tensor_copy(out=WAX[:, :, :].rearrange("p a b -> p (a b)"), in_=TT[1][:, :]),
         reads=["T1"], writes=["WAX"])
    P.dma(lambda e: [e.dma_start(out=TT[2 + q4][:, :], in_=biast_in[:, q4 * 512:(q4 + 1) * 512]) for q4 in range(4)],
          "d_bias", 4, writes=["T2", "T3", "T4", "T5"])
    for q4 in range(4):
        for half in range(2):
            P.op("dve", lambda e, q4=q4, half=half: e.tensor_copy(
                out=BIAS[:, q4 // 2, (q4 % 2) * 2:(q4 % 2) * 2 + 2, half * 256:(half + 1) * 256],
                in_=TT[2 + q4][:, :].rearrange("p (a b) -> p a b", a=2)),
                reads=["T%d" % (2 + q4)], writes=["BIAS"])
    P.op("act", lambda e: e.activation(out=SM[:, 40:42], in_=SM[:, 30:32], func=AF.Exp, scale=-1.0),
         reads=["SM_in"], writes=["SM_a"])
    P.op("act", lambda e: e.activation(out=SM[:, 40:42], in_=SM[:, 40:42], func=AF.Ln, bias=1.0),
         reads=["SM_a"], writes=["SM_a"])
    P.op("dve", lambda e: e.tensor_scalar(out=SM[:, 42:44], in0=SM[:, 40:42], scalar1=-16.0, scalar2=None, op0=ALU.mult),
         reads=["SM_a"], writes=["SM_c2"])
    P.op("dve", lambda e: e.tensor_scalar(out=SM[:, 40:42], in0=SM[:, 40:42], scalar1=-8.0, scalar2=None, op0=ALU.mult),
         reads=["SM_a", "SM_c2"], writes=["SM_a"])
    P.op("dve", lambda e: e.tensor_scalar(out=SM[:, 44:48], in0=SM[:, 26:30], scalar1=-1.0, scalar2=None, op0=ALU.mult),
         reads=["SM_in"], writes=["SM_nb"])
    P.op("dve", lambda e: e.tensor_tensor(out=LAMV[:, 0:64], in0=LAMV[:, 0:64], in1=LAMV[:, 64:128], op=ALU.mult),
         reads=["LAMV"], writes=["LAMV"])
    P.op("dve", lambda e: e.tensor_tensor(out=LAMV[:, 128:192], in0=LAMV[:, 128:192], in1=LAMV[:, 192:256], op=ALU.mult),
         reads=["LAMV"], writes=["LAMV"])
    P.op("dve", lambda e: e.tensor_reduce(out=SM[:, 50:51], in_=LAMV[:, 0:64], axis=mybir.AxisListType.X, op=ALU.add),
         reads=["LAMV"], writes=["SM_l"])
    P.op("dve", lambda e: e.tensor_reduce(out=SM[:, 51:52], in_=LAMV[:, 128:192], axis=mybir.AxisListType.X, op=ALU.add),
         reads=["LAMV", "SM_l"], writes=["SM_l", "XSQ"])
    P.op("act", lambda e: e.activation(out=SM[:, 52:54], in_=SM[:, 50:52], func=AF.Exp), reads=["SM_l"], writes=["SM_l2"])
    P.op("dve", lambda e: e.scalar_tensor_tensor(out=SM[:, 48:49], in0=SM[:, 53:54], scalar=-LAM_INIT, in1=SM[:, 52:53],
                                                 op0=ALU.add, op1=ALU.subtract), reads=["SM_l2"], writes=["SM_nl"])
    P.op("dve", lambda e: e.tensor_scalar(out=SM[:, 49:50], in0=SM[:, 32:33], scalar1=1.0 - LAM_INIT, scalar2=None, op0=ALU.mult),
         reads=["SM_in"], writes=["SM_gs"])
    SMK = ["SM_in", "SM_a", "SM_c2", "SM_nb", "SM_nl", "SM_gs"]

    xT_v = xT.rearrange("(c p) t -> p c t", p=128)

    def tile_tok(ti):
        return (0, NM) if ti == 0 else (NM + 512 * (ti - 1), 512)

    def load_piece(g):
        if g >= 4 * NT:
            return
        ti, k = divmod(g, 4)
        tok0, n = tile_tok(ti)
        sl = g % 2
        P.dma(lambda e: [e.dma_start(out=XS[sl][:, :, 0:n], in_=xT_v[:, 4 * k:4 * k + 4, tok0:tok0 + n])],
              "d_xs%d" % sl, 1, writes=["XS%d" % sl])

    def stats_piece(ti, k):
        g = 4 * ti + k
        tok0, n = tile_tok(ti)
        sl = g % 2
        hb = ti % 2
        for q in range(4):
            dch = 4 * k + q
            if q % 2 == 0:
                P.op("dve", lambda e, q=q, dch=dch: e.tensor_scalar(
                    out=HB[hb][:, dch, 0:n], in0=XS[sl][:, q, 0:n], scalar1=SM[:, dch:dch + 1], scalar2=None, op0=ALU.mult),
                    reads=["XS%d" % sl, "SM_in"], writes=["HB%d_%d" % (hb, dch)])
            else:
                P.op("act", lambda e, q=q, dch=dch: e.activation(
                    out=HB[hb][:, dch, 0:n], in_=XS[sl][:, q, 0:n], func=AF.Copy, scale=SM[:, dch:dch + 1]),
                    reads=["XS%d" % sl, "SM_in"], writes=["HB%d_%d" % (hb, dch)])
        P.op("pool", lambda e: e.tensor_tensor(out=XSQ[:, :, 0:n], in0=XS[sl][:, :, 0:n], in1=XS[sl][:, :, 0:n], op=ALU.mult),
             reads=["XS%d" % sl], writes=["XSQ"])
        load_piece(g + 2)

        def mm(e):
            r = None
            for q in range(4):
                r = e.matmul(psb[3][:, 0:n], lhsT=ONESD[:, :], rhs=XSQ[:, q, 0:n],
                             start=(k == 0 and q == 0), stop=(k == 3 and q == 3))
            return r
        P.op("pe", mm, reads=["XSQ", "ONESD"], writes=["psb3"])
        if k == 3:
            rb = ti % 2
            P.op("act", lambda e: e.activation(out=RSTD[rb][:, 0:n], in_=psb[3][:, 0:n], func=AF.Ln, bias=1e-6),
                 reads=["psb3"], writes=["RSTD%d" % rb])
            P.op("act", lambda e: e.activation(out=RSTD[rb][:, 0:n], in_=RSTD[rb][:, 0:n], func=AF.Exp, scale=-0.5),
                 reads=["RSTD%d" % rb], writes=["RSTD%d" % rb])

    w_in_v = w_in.rearrange("(c p) n -> p c n", p=128)
    W_ORDER = [8, 9, 0, 1, 2, 3, 4, 5, 6, 7, 10, 11]

    def load_w(cc):
        P.dma_pool(lambda e: [e.dma_start(out=WP[:, :, cc * 128:(cc + 1) * 128], in_=w_in_v[:, :, cc * 128:(cc + 1) * 128])],
                   "d_wp%d" % cc, 1, writes=["WP%d" % cc])

    load_piece(0)
    load_piece(1)
    for cc in W_ORDER[0:4]:
        load_w(cc)
    for k in range(4):
        stats_piece(0, k)
    for cc in W_ORDER[4:]:
        load_w(cc)

    bank_rr = [0]
    lo_cnt = [0]

    def phase1_tile(ti):
        tok0, n = tile_tok(ti)
        hb = ti % 2
        rb = ti % 2
        meta = (ti == 0)
        ci = ti - 1
        r0 = 512 * ci
        order = [8, 9, 2, 3, 4, 5] if meta else [8, 9, 0, 1, 2, 3, 4, 5, 6, 7, 10, 11]
        pend_tr = []
        HBK = ["HB%d_%d" % (hb, k) for k in range(16)]
        nstat = [0]

        def in_chunk(idx, cc):
            if pend_tr:
                pend_tr.pop(0)()
            bk = bank_rr[0] % 3
            bank_rr[0] += 1
            ps = psb[bk]
            psk = "psb%d" % bk

            def mm(e, cc=cc, ps=ps):
                r = None
                for c in range(16):
                    r = e.matmul(ps[:, 0:n], lhsT=WP[:, c, cc * 128:(cc + 1) * 128], rhs=HB[hb][:, c, 0:n],
                                 start=(c == 0), stop=(c == 15))
                return r
            P.op("pe", mm, reads=["WP%d" % cc] + HBK, writes=[psk])
            rk = "RSTD%d" % rb
            if cc in (0, 1):
                hh = cc
                P.op("dve", lambda e, ps=ps, hh=hh: e.scalar_tensor_tensor(
                    out=QT[:, hh, r0:r0 + n], in0=ps[:, 0:n], scalar=0.125, in1=RSTD[rb][:, 0:n], op0=ALU.mult, op1=ALU.mult),
                    reads=[psk, rk], writes=["QT"])
            elif cc in (2, 3):
                hh = cc - 2
                P.op("dve", lambda e, ps=ps, hh=hh: e.tensor_tensor(
                    out=KT[:, hh, tok0:tok0 + n], in0=ps[:, 0:n], in1=RSTD[rb][:, 0:n], op=ALU.mult),
                    reads=[psk, rk], writes=["KT"])
            elif cc in (4, 5):
                hh = cc - 4
                P.op("dve", lambda e, ps=ps: e.tensor_tensor(out=VT[:, 0:n], in0=ps[:, 0:n], in1=RSTD[rb][:, 0:n], op=ALU.mult),
                     reads=[psk, rk], writes=["VT"])
                nblk = 1 if meta else 4
                blk0 = 0 if meta else 1 + 4 * ci
                bw = NM if meta else 128
                ptv = pss[1][:, 0:256].bitcast(BF16)

                def tr(e, nblk=nblk, bw=bw, ptv=ptv):
                    r = None
                    for jb in range(nblk):
                        r = e.transpose(out=ptv[0:bw, jb * 128:(jb + 1) * 128], in_=VT[:, jb * bw:(jb + 1) * bw], identity=IDB[:, :])
                    return r
                def deferred(hh=hh, nblk=nblk, bw=bw, blk0=blk0, ptv=ptv, tr=tr):
                    P.op("pe", tr, reads=["VT", "IDB"], writes=["pss1"])
                    P.op("act", lambda e: e.activation(
                        out=VV[0:bw, blk0:blk0 + nblk, hh * 128:(hh + 1) * 128],
                        in_=ptv[0:bw, 0:nblk * 128].rearrange("p (a b) -> p a b", a=nblk), func=AF.Copy),
                        reads=["pss1"], writes=["VV"])
                pend_tr.append(deferred)
            elif cc in (6, 7, 10, 11):
                tg, te = TT[4], TT[5]
                P.op("dve", lambda e, ps=ps: e.tensor_tensor(out=tg[:, 0:n], in0=ps[:, 0:n], in1=RSTD[rb][:, 0:n], op=ALU.mult),
                     reads=[psk, rk], writes=["T4"])
                P.op("act", lambda e: e.activation(out=te[:, 0:n], in_=tg[:, 0:n], func=AF.Exp, scale=-1.0),
                     reads=["T4"], writes=["T5"])
                P.op("act", lambda e: e.activation(out=te[:, 0:n], in_=te[:, 0:n], func=AF.Ln, bias=1.0),
                     reads=["T5"], writes=["T5"])
                P.op("act", lambda e: e.activation(out=te[:, 0:n], in_=te[:, 0:n], func=AF.Exp, scale=-1.0),
                     reads=["T5"], writes=["T5"])
                if cc in (6, 7):
                    hh = cc - 6
                    P.op("dve", lambda e, hh=hh: e.tensor_tensor(out=SGA[:, hh, r0:r0 + n], in0=tg[:, 0:n], in1=te[:, 0:n], op=ALU.mult),
                         reads=["T4", "T5"], writes=["SGA"])
                else:
                    blk = cc - 10
                    P.op("dve", lambda e, blk=blk: e.tensor_tensor(out=SGL[:, blk, 0:n], in0=tg[:, 0:n], in1=te[:, 0:n], op=ALU.mult),
                         reads=["T4", "T5"], writes=["SGL%d" % blk])
            elif cc in (8, 9):
                blk = cc - 8
                P.op("dve", lambda e, ps=ps, blk=blk: e.tensor_tensor(
                    out=U[:, blk, 3:3 + n], in0=ps[:, 0:n], in1=RSTD[rb][:, 0:n], op=ALU.mult),
                    reads=[psk, rk], writes=["U%d" % blk])
            if ti + 1 < NT and idx >= 1 and idx % 2 == 1 and nstat[0] < 4:
                stats_piece(ti + 1, nstat[0])
                nstat[0] += 1

        hooks = make_hooks_for(ti)
        for idx, cc in enumerate(order):
            in_chunk(idx, cc)
            for h in hooks.pop(idx, []):
                h()
        while pend_tr:
            pend_tr.pop(0)()
        while ti + 1 < NT and nstat[0] < 4:
            stats_piece(ti + 1, nstat[0])
            nstat[0] += 1

        for idx in sorted(hooks.keys()):
            for h in hooks[idx]:
                h()

    def make_hooks_for(ti):
        hooks = {}
        if ti >= 1:
            hooks[1] = [lambda: lru_B1(ti - 1, 0)]
            hooks[3] = [lambda: lru_B2(ti - 1, 0)]
            hooks[4] = [lambda: lru_B1(ti - 1, 1)]
            hooks[6] = [lambda: lru_B2(ti - 1, 1)]
            hooks[9] = [lambda: lru_gather(ti - 1)]
        hooks[7] = [lambda: lru_A(ti, 0), lambda: lru_A(ti, 1)]
        return hooks

    UCB = [TT[0], TT6]
    UCK = ["T0", "T6"]
    UCBFS = [UCBF, UCBF2]
    UCBFK = ["UCBF0", "UCBF1"]

    def lru_A(ti, blk):
        tok0, n = tile_tok(ti)
        uk = "U%d" % blk
        uc, uck = UCB[blk], UCK[blk]
        cw = 16 + 4 * blk
        P.op("dve", lambda e: e.tensor_scalar(
            out=uc[:, 0:n], in0=U[:, blk, 3:3 + n], scalar1=SM[:, cw + 3:cw + 4], scalar2=SM[:, 24 + blk:25 + blk],
            op0=ALU.mult, op1=ALU.add), reads=[uk, "SM_in"], writes=[uck])
        for kk in (2, 1, 0):
            P.op("dve", lambda e, kk=kk: e.scalar_tensor_tensor(
                out=uc[:, 0:n], in0=U[:, blk, kk:kk + n], scalar=SM[:, cw + kk:cw + kk + 1], in1=uc[:, 0:n],
                op0=ALU.mult, op1=ALU.add), reads=[uk, "SM_in", uck], writes=[uck])
        P.op("pool", lambda e: e.tensor_copy(out=U[:, blk, 0:3], in_=U[:, blk, n:n + 3]), reads=[uk], writes=[uk])
        P.op("act", lambda e: e.activation(out=UCBFS[blk][:, 0:n], in_=uc[:, 0:n], func=AF.Copy),
             reads=[uck], writes=[UCBFK[blk]])

    def lru_B1(ti, blk):
        tok0, n = tile_tok(ti)
        tr_, ti_, ta, tm = TT[1], TT[2], TT[3], TT[4]

        def gmm(e):
            e.matmul(pss[0][:, 0:n], lhsT=WAX[:, 2 * blk, :], rhs=UCBFS[blk][:, 0:n], start=True, stop=True)
            return e.matmul(pss[0][:, 512:512 + n], lhsT=WAX[:, 2 * blk + 1, :], rhs=UCBFS[blk][:, 0:n], start=True, stop=True)
        P.op("pe", gmm, reads=["WAX", UCBFK[blk]], writes=["pss0"])
        for gi, (tdst, tkey, off) in enumerate(((tr_, "T1", 0), (ti_, "T2", 512))):
            P.op("act", lambda e, tdst=tdst, off=off, gi=gi: e.activation(
                out=tdst[:, 0:n], in_=pss[0][:, off:off + n], func=AF.Exp, scale=-1.0,
                bias=SM[:, 44 + 2 * gi + blk:45 + 2 * gi + blk]), reads=["pss0", "SM_nb"], writes=[tkey])
            P.op("act", lambda e, tdst=tdst: e.activation(out=tdst[:, 0:n], in_=tdst[:, 0:n], func=AF.Ln, bias=1.0),
                 reads=[tkey], writes=[tkey])
            P.op("act", lambda e, tdst=tdst: e.activation(out=tdst[:, 0:n], in_=tdst[:, 0:n], func=AF.Exp, scale=-1.0),
                 reads=[tkey], writes=[tkey])
        P.op("act", lambda e: e.activation(out=ta[:, 0:n], in_=tr_[:, 0:n], func=AF.Exp, scale=SM[:, 40 + blk:41 + blk]),
             reads=["T1", "SM_a"], writes=["T3"])
        P.op("act", lambda e: e.activation(out=tm[:, 0:n], in_=tr_[:, 0:n], func=AF.Exp, scale=SM[:, 42 + blk:43 + blk]),
             reads=["T1", "SM_c2"], writes=["T4"])
        P.op("act", lambda e: e.activation(out=tm[:, 0:n], in_=tm[:, 0:n], func=AF.Ln, scale=-1.0, bias=1.0),
             reads=["T4"], writes=["T4"])
        P.op("act", lambda e: e.activation(out=tm[:, 0:n], in_=tm[:, 0:n], func=AF.Exp, scale=0.5),
             reads=["T4"], writes=["T4"])

    def lru_B2(ti, blk):
        tok0, n = tile_tok(ti)
        meta = (ti == 0)
        ci = ti - 1
        uc, uck = UCB[blk], UCK[blk]
        tr_, ti_, ta, tm = TT[1], TT[2], TT[3], TT[4]
        if meta:
            P.op("dve", lambda e: e.memset(tm[:, 0:1], 1.0), reads=["T4"], writes=["T4"])
        P.op("dve", lambda e: e.tensor_tensor(out=ti_[:, 0:n], in0=ti_[:, 0:n], in1=tm[:, 0:n], op=ALU.mult),
             reads=["T2", "T4"], writes=["T2"])
        P.op("dve", lambda e: e.tensor_tensor(out=ti_[:, 0:n], in0=ti_[:, 0:n], in1=uc[:, 0:n], op=ALU.mult),
             reads=["T2", uck], writes=["T2"])
        P.op("dve", lambda e: e.tensor_tensor_scan(
            out=tr_[:, 0:n], data0=ta[:, 0:n], data1=ti_[:, 0:n], initial=HST[:, blk:blk + 1], op0=ALU.mult, op1=ALU.add),
            reads=["T3", "T2", "HST%d" % blk, "T1"], writes=["T1"])
        P.op("dve", lambda e: e.tensor_copy(out=HST[:, blk:blk + 1], in_=tr_[:, n - 1:n]),
             reads=["T1"], writes=["HST%d" % blk])
        if not meta:
            lb = ci % 2
            P.op("dve", lambda e: e.tensor_tensor(out=LO[lb][:, blk, :], in0=tr_[:, 0:n], in1=SGL[:, blk, 0:n], op=ALU.mult),
                 reads=["T1", "SGL%d" % blk], writes=["LO%d" % lb])
            if blk == 1:
                P.dma(lambda e: [e.dma_start(out=aginl[ci].rearrange("(b p) t -> p b t", p=128), in_=LO[lb][:, :, :])],
                      "d_lo%d" % lb, 1, reads=["LO%d" % lb], writes=["AGINL%d" % ci])

    def lru_gather(ti):
        ci = ti - 1
        if ci < 0:
            return
        P.cc(lambda e: e.collective_compute("AllGather", ALU.bypass, replica_groups=[[0, 1, 2, 3], [4, 5, 6, 7]],
                                            ins=[aginl[ci]], outs=[agoutl[ci]]),
             "cc_l", reads=["AGINL%d" % ci], writes=["AGOUTL%d" % ci])

    for ti in range(NT):
        phase1_tile(ti)
    lru_B1(NT - 1, 0)
    lru_B2(NT - 1, 0)
    lru_B1(NT - 1, 1)
    lru_B2(NT - 1, 1)
    lru_gather(NT - 1)

    if DEBUG:
        P.dma(lambda e: [e.dma_start(out=dbg["qt"], in_=QT[:, :, :].rearrange("p a b -> p (a b)")),
                         e.dma_start(out=dbg["kt"], in_=KT[:, :, :].rearrange("p a b -> p (a b)")),
                         e.dma_start(out=dbg["vv"], in_=VV[:, :, :].rearrange("p a b -> p (a b)")),
                         e.dma_start(out=dbg["sga"], in_=SGA[:, :, :].rearrange("p a b -> p (a b)"))],
              "d_dbg", 4, reads=["QT", "KT", "VV", "SGA"], writes=["DBG"])

    P.barrier(skip_prefix="cc_")
    w_out_v = w_out.rearrange("(c p) n -> p c n", p=128)
    P.dma(lambda e: [e.dma_start(out=WOST32[:, 4 * q_:4 * q_ + 4, :], in_=w_out_v[:, 4 * q_:4 * q_ + 4, :]) for q_ in range(4)],
          "d_wo", 4, writes=["WOST32"])
    WO_CAST = [False]

    def cast_wo():
        if WO_CAST[0]:
            return
        WO_CAST[0] = True
        for q_ in range(4):
            P.op("dve", lambda e, q_=q_: e.tensor_copy(out=WO[:, 4 * q_:4 * q_ + 4, :], in_=WOST32[:, 4 * q_:4 * q_ + 4, :]),
                 reads=["WOST32"], writes=["WO"])
    P.dma(lambda e: [e.dma_start(out=FG[:, :], in_=fg_in)], "d_fg", 1, writes=["FG"])
    P.op("dve", lambda e: e.memset(QBD[0][:, :], 0.0), writes=["QBD0"])
    P.op("dve", lambda e: e.memset(QBD[1][:, :], 0.0), writes=["QBD1"])

    pt_rr = [0]
    s_rr = [0]

    PROC = [0, 1, 2, 3, 4, 5, 6, 7]
    POS = {c_: p_ for p_, c_ in enumerate(PROC)}

    SCH = {"t": 0.0, "cc_free": 0.0, "inj": None, "inj_chunk": None, "bseq": 0}
    pend_B = []
    pend_F = []
    pend_fin2 = []
    BSEQ = {}
    CC_A, CC_B, LOAD_LAT = 48.0, 14.0, 12.0

    def attention(i):
        mb = POS[i] % 2
        for hh in range(2):
            for qs in range(2):
                att_tile(i, mb, hh, qs)
        def fin_chunk():
            P.dma(lambda e: [e.dma_start(out=agin[i].rearrange("(b p) t -> p b t", p=128), in_=MIX[mb][:, :, :])],
                  "d_mix%d" % mb, 1, reads=["MIX%d" % mb], writes=["AGIN%d_a" % i])
            P.cc(lambda e: e.collective_compute("AllGather", ALU.bypass, replica_groups=[[0, 1, 2, 3], [4, 5, 6, 7]],
                                                ins=[agin[i]], outs=[agout[i]]),
                 "cc_a", reads=["AGIN%d_a" % i], writes=["AGOUT%d" % i])
            st = max(SCH["cc_free"], SCH["t"] + 2.0)
            SCH["cc_free"] = st + CC_A
            pend_B.append((i, SCH["cc_free"]))
        pend_post.append(fin_chunk)

    def sched_block(cost):
        SCH["t"] += cost
        if SCH["inj"] is None and pend_B and pend_B[0][1] + 8.0 <= SCH["t"]:
            k_, _r = pend_B.pop(0)
            start_B(k_)
        if SCH["inj"] is not None and SCH["inj_ready"] <= SCH["t"]:
            run_units(1)
        if SCH["inj"] is not None and pend_B and pend_B[0][1] + 8.0 <= SCH["t"]:
            prefetch_mixg(pend_B[0][0])

    PREF = set()

    def assign_seq(k_):
        if k_ not in BSEQ:
            BSEQ[k_] = SCH["bseq"]
            SCH["bseq"] += 1

    def prefetch_mixg(k_):
        if k_ in PREF:
            return
        assign_seq(k_)
        PREF.add(k_)
        outproj_prep_mixg(k_)

    def start_B(k_):
        assign_seq(k_)
        if BSEQ[k_] >= 2:
            force_finalize_upto(BSEQ[k_] - 2)
        cast_wo()
        pre = k_ in PREF
        prefetch_mixg(k_)
        outproj_prep(k_)
        SCH["inj"] = outproj_units(k_)
        SCH["inj_chunk"] = k_
        SCH["inj_ready"] = SCH["t"] + (6.0 if pre else LOAD_LAT)

    def run_units(n_):
        for _ in range(n_):
            if SCH["inj"] is None:
                return
            try:
                next(SCH["inj"])
                SCH["t"] += 0.06
            except StopIteration:
                k_ = SCH["inj_chunk"]
                SCH["inj"] = None
                st = max(SCH["cc_free"], SCH["t"] + 2.0)
                SCH["cc_free"] = st + CC_B
                pend_F.append((k_, SCH["cc_free"]))

    FIN_DONE = set()

    def force_finalize_upto(seq):
        for k_, sq_ in list(BSEQ.items()):
            if sq_ <= seq and k_ not in FIN_DONE:
                for it in list(pend_F):
                    if it[0] == k_:
                        pend_F.remove(it)
                do_finalize(k_, split=False)

    def do_finalize(k_, split):
        FIN_DONE.add(k_)
        finalize1(k_)
        if split:
            pend_fin2.append(lambda: finalize2(k_))
        else:
            finalize2(k_)

    def sched_tile_start():
        if pend_F and pend_F[0][1] + 45.0 <= SCH["t"]:
            k_, _r = pend_F.pop(0)
            if k_ not in FIN_DONE:
                do_finalize(k_, split=True)

    tile_rr = [0]
    LG = 12
    TILES = [(i_, hh_, qs_) for i_ in PROC for hh_ in range(2) for qs_ in range(2)]
    pend_post = []

    def emit_qbd(t):
        if t >= len(TILES):
            return
        i_, hh_, qs_ = TILES[t]
        q0_ = 256 * (2 * i_ + qs_)
        qb_ = t % 2
        P.op("pool", lambda e: e.tensor_copy(out=QBD[qb_][0:64, 0:256], in_=QT[0:64, hh_, q0_:q0_ + 256]),
             reads=["QT"], writes=["QBD%d" % qb_])
        P.op("pool", lambda e: e.tensor_copy(out=QBD[qb_][64:128, 256:512], in_=QT[64:128, hh_, q0_:q0_ + 256]),
             reads=["QT"], writes=["QBD%d" % qb_])

    def att_tile(i, mb, hh, qs):
        sched_tile_start()
        qi = 2 * i + qs
        q0 = 256 * qi
        tix = tile_rr[0]
        tile_rr[0] += 1
        ob = tix % 2
        psO = psb[0] if ob == 0 else pss[1][:, 0:512]
        psOk = "psb0" if ob == 0 else "pss1a"
        qb = tix % 2
        qbk = "QBD%d" % qb
        if tix == 0:
            emit_qbd(0)
        emit_qbd(tix + 1)
        blocks = [("real", m) for m in range(2 * qi + 2)] + [("meta", None)]
        nb = len(blocks)
        nreal = nb - 1
        grp_first = [None]
        linfo = {}
        l_started = [False]

        def s_stage(nidx):
            kind, m = blocks[nidx]
            sb = s_rr[0] % 3
            s_rr[0] += 1
            if kind == "meta":
                M, k0 = NM, 0
                spec = 0 if qi == 0 else None
            else:
                M, k0 = 128, NM + 128 * m
                spec = {2 * qi - 1: 1, 2 * qi: 2, 2 * qi + 1: 3}.get(m)
            psS = (pss[0][:, 0:512], pss[0][:, 512:1024], pss[1][:, 512:1024])[sb]
            psSk = ("pss0a", "pss0b", "pss1b")[sb]

            def mm(e):
                r = e.matmul(psS[0:M, :], lhsT=KT[:, hh, k0:k0 + M], rhs=QBD[qb][:, :], start=True, stop=(spec is None))
                if spec is not None:
                    r = e.matmul(psS[0:M, :], lhsT=IDB[0:M, 0:M], rhs=BIAS[0:M, hh, spec, :], start=False, stop=True)
                return r
            P.op("pe", mm, reads=["KT", qbk, "IDB", "BIAS"], writes=[psSk])
            pb = pt_rr[0] % NPT
            pt_rr[0] += 1
            ptk = "PT%d" % pb
            ptf = PT[pb][0:M, :, :].rearrange("p a b -> p (a b)")
            if spec is None:
                P.op("act", lambda e: e.activation(out=ptf, in_=psS[0:M, :], func=AF.Exp, bias=SM[0:M, c31col[hh]:c31col[hh] + 1]),
                     reads=[psSk, "SM_in"], writes=[ptk])
            else:
                P.op("act", lambda e: e.activation(out=ptf, in_=psS[0:M, :], func=AF.Exp), reads=[psSk], writes=[ptk])
            lrhs = None
            if kind == "meta":
                lrhs = (ptf, [ptk], M)
            else:
                g, pos = divmod(nidx, LG)
                gs = GS[g % 2]
                gk = "GS%d" % (g % 2)
                last_in_group = (pos == LG - 1) or (nidx == nreal - 1)
                if pos == 0:
                    grp_first[0] = (ptf, ptk)
                    if last_in_group:
                        lrhs = (ptf, [ptk], M)
                elif pos == 1:
                    f_ap, f_k = grp_first[0]
                    P.op("dve", lambda e: e.tensor_tensor(out=gs[:, :], in0=f_ap, in1=ptf, op=ALU.add),
                         reads=[f_k, ptk], writes=[gk])
                else:
                    P.op("dve", lambda e: e.tensor_tensor(out=gs[:, :], in0=gs[:, :], in1=ptf, op=ALU.add),
                         reads=[gk, ptk], writes=[gk])
                if pos >= 1 and last_in_group:
                    lrhs = (gs[:, :], [gk], 128)
            linfo[nidx] = lrhs
            return (pb, M, kind, m)

        def pv_stage(nidx, info):
            pb, M, kind, m = info
            vb = 0 if kind == "meta" else 1 + m
            rhs = PT[pb][0:M, :, :].rearrange("p a b -> p (a b)")
            P.op("pe", lambda e: e.matmul(psO[:, :], lhsT=VV[0:M, vb, hh * 128:(hh + 1) * 128], rhs=rhs,
                                          start=(nidx == 0), stop=(nidx == nb - 1)),
                 reads=["VV", "PT%d" % pb], writes=[psOk])
            lr = linfo.pop(nidx)
            if lr is not None:
                l_ap, l_keys, lM = lr
                first = not l_started[0]
                l_started[0] = True
                P.op("pe", lambda e: e.matmul(psb[1][:, :], lhsT=ONES[0:lM, :], rhs=l_ap, start=first, stop=(nidx == nb - 1)),
                     reads=["ONES"] + l_keys, writes=["psb1"])
            sched_block(0.93)
            if nidx == 2:
                while pend_post:
                    pend_post.pop(0)()
            if nidx == 4 and pend_fin2:
                pend_fin2.pop(0)()

        infos = {}
        AHEAD = 2
        for nidx in range(min(AHEAD, nb)):
            infos[nidx] = s_stage(nidx)
        for nidx in range(nb):
            if nidx + AHEAD < nb:
                infos[nidx + AHEAD] = s_stage(nidx + AHEAD)
            pv_stage(nidx, infos.pop(nidx))
        P.op("act", lambda e: e.activation(out=RL[:, :], in_=psb[1][:, :], func=AF.Ln), reads=["psb1"], writes=["RL"])
        P.op("act", lambda e: e.activation(out=RL[:, :], in_=RL[:, :], func=AF.Exp, scale=-1.0), reads=["RL"], writes=["RL"])
        P.op("dve", lambda e: e.tensor_tensor(out=ON[:, :], in0=psO[:, :], in1=RL[:, :], op=ALU.mult),
             reads=[psOk, "RL"], writes=["ON"])
        P.op("dve", lambda e: e.scalar_tensor_tensor(out=DIFF[:, :], in0=ON[:, 256:512], scalar=SM[:, 48:49], in1=ON[:, 0:256],
                                                     op0=ALU.mult, op1=ALU.add), reads=["ON", "SM_nl"], writes=["DIFF"])
        P.op("pool", lambda e: e.tensor_tensor(out=SQ[:, :], in0=DIFF[:, :], in1=DIFF[:, :], op=ALU.mult),
             reads=["DIFF"], writes=["SQ"])
        def post2():
            P.op("pe", lambda e: e.matmul(psb[2][:, 0:256], lhsT=ONESM[:, :], rhs=SQ[:, :], start=True, stop=True),
                 reads=["SQ", "ONESM"], writes=["psb2"])
            P.op("act", lambda e: e.activation(out=R2[:, :], in_=psb[2][:, 0:256], func=AF.Ln, bias=1e-5),
                 reads=["psb2"], writes=["R2"])
            P.op("act", lambda e: e.activation(out=R2[:, :], in_=R2[:, :], func=AF.Exp, scale=-0.5),
                 reads=["R2"], writes=["R2"])
            P.op("dve", lambda e: e.tensor_tensor(out=T1[:, :], in0=DIFF[:, :], in1=R2[:, :], op=ALU.mult),
                 reads=["DIFF", "R2"], writes=["T1x"])
            P.op("dve", lambda e: e.scalar_tensor_tensor(
                out=MIX[mb][:, hh, qs * 256:(qs + 1) * 256], in0=T1[:, :], scalar=SM[:, 49:50], in1=SGA[:, hh, q0:q0 + 256],
                op0=ALU.mult, op1=ALU.mult), reads=["T1x", "SM_gs", "SGA"], writes=["MIX%d" % mb])
        pend_post.append(post2)
        while pend_fin2:
            pend_fin2.pop(0)()

    def outproj_prep_mixg(i):
        gb = BSEQ[i] % 2
        P.dma(lambda e: [e.dma_start(out=MIXG[gb][:, 0:8, :], in_=agout[i].rearrange("(c p) t -> p c t", p=128)),
                         e.dma_start(out=MIXG[gb][:, 8:16, :], in_=agoutl[i].rearrange("(c p) t -> p c t", p=128))],
              "d_mg%d" % gb, 2, reads=["AGOUT%d" % i, "AGOUTL%d" % i], writes=["MIXG%d" % gb])

    def outproj_prep(i):
        P.dma(lambda e: [e.dma_start(out=XRES[:, :, :], in_=x_res[512 * i:512 * (i + 1), :].rearrange("(b p) n -> p b n", p=128))],
              "d_xr", 1, writes=["XRES"])
        P.op("dve", lambda e: e.memset(SSQ[:, :], 0.0), writes=["SSQ"])

    def outproj_units(i):
        gb = BSEQ[i] % 2
        yb = BSEQ[i] % 2
        for tb in range(4):
            for c in range(16):
                P.op("pe", lambda e, tb=tb, c=c: e.matmul(
                    psb[3][:, :], lhsT=MIXG[gb][:, c, tb * 128:(tb + 1) * 128], rhs=WO[:, c, :], start=(c == 0), stop=(c == 15)),
                    reads=["MIXG%d" % gb, "WO"], writes=["psb3"])
                if c == 15:
                    P.op("dve", lambda e, tb=tb: e.tensor_tensor(out=YB[yb][:, tb, :], in0=psb[3][:, :], in1=XRES[:, tb, :], op=ALU.add),
                         reads=["psb3", "XRES"], writes=["YB%d_%d" % (yb, tb)])
                    P.op("act", lambda e, tb=tb: e.activation(out=JUNK[:, :], in_=YB[yb][:, tb, :], func=AF.Square,
                                                              accum_out=SSQ[:, tb:tb + 1]),
                         reads=["YB%d_%d" % (yb, tb)], writes=["JUNK", "SSQ"])
                yield
        P.dma(lambda e: [e.dma_start(out=sqin[i], in_=SSQ[:, :])], "d_sq", 1, reads=["SSQ"], writes=["SQIN%d" % i])
        P.cc(lambda e: e.collective_compute("AllGather", ALU.bypass, replica_groups=[[0, 1, 2, 3], [4, 5, 6, 7]],
                                            ins=[sqin[i]], outs=[sqout[i]]),
             "cc_b", reads=["SQIN%d" % i], writes=["SQOUT%d" % i])

    def finalize1(i):
        P.dma(lambda e: [e.dma_start(out=SQG[:, :, :], in_=sqout[i].rearrange("(r p) b -> p r b", p=128))],
              "d_sqg", 1, reads=["SQOUT%d" % i], writes=["SQG"])
        P.op("dve", lambda e: e.tensor_tensor(out=TOT[:, :], in0=SQG[:, 0, :], in1=SQG[:, 1, :], op=ALU.add),
             reads=["SQG"], writes=["TOT"])
        P.op("dve", lambda e: e.tensor_tensor(out=TOT[:, :], in0=TOT[:, :], in1=SQG[:, 2, :], op=ALU.add),
             reads=["SQG", "TOT"], writes=["TOT"])
        P.op("dve", lambda e: e.tensor_tensor(out=TOT[:, :], in0=TOT[:, :], in1=SQG[:, 3, :], op=ALU.add),
             reads=["SQG", "TOT"], writes=["TOT"])

    def finalize2(i):
        yb = BSEQ[i] % 2
        P.op("act", lambda e: e.activation(out=RS[:, :], in_=TOT[:, :], func=AF.Ln, scale=1.0 / 2048.0, bias=1e-6),
             reads=["TOT"], writes=["RS"])
        P.op("act", lambda e: e.activation(out=RS[:, :], in_=RS[:, :], func=AF.Exp, scale=-0.5), reads=["RS"], writes=["RS"])
        for tb in range(4):
            P.op("dve", lambda e, tb=tb: e.scalar_tensor_tensor(
                out=YB[yb][:, tb, :], in0=YB[yb][:, tb, :], scalar=RS[:, tb:tb + 1], in1=FG[:, :], op0=ALU.mult, op1=ALU.mult),
                reads=["YB%d_%d" % (yb, tb), "RS", "FG"], writes=["YB%d_%d" % (yb, tb)])
        P.dma(lambda e: [e.dma_start(out=out[512 * i:512 * (i + 1), :].rearrange("(b p) n -> p b n", p=128), in_=YB[yb][:, :, :])],
              "d_out%d" % yb, 1, reads=["YB%d_%d" % (yb, tb) for tb in range(4)], writes=["OUT%d" % i])

    for ai in PROC:
        attention(ai)
    while pend_post:
        pend_post.pop(0)()
    while pend_B or SCH["inj"] is not None:
        if SCH["inj"] is None:
            k_, r_ = pend_B.pop(0)
            SCH["t"] = max(SCH["t"], r_)
            start_B(k_)
            SCH["t"] = max(SCH["t"], SCH["inj_ready"])
        if pend_B and pend_B[0][1] <= SCH["t"] + 16.0:
            prefetch_mixg(pend_B[0][0])
        run_units(64)
    while pend_F:
        k_, _r = pend_F.pop(0)
        if k_ not in FIN_DONE:
            do_finalize(k_, split=False)
    if DEBUG:
        P.dma(lambda e: [e.dma_start(out=dbg["ag0"], in_=agout[0])], "d_dbg2", 1, reads=["AGOUT0"], writes=["DBG2"])
    P.barrier()

    with ExitStack() as es:
        sems = {}
        for sname in P.cnt.keys():
            sems[sname] = es.enter_context(nc.semaphore("s_" + sname))
        block = es.enter_context(nc.Block())

        def run(e, eng):
            for waits, fn, sem, inc in P.ops[eng]:
                for (s_, v) in waits:
                    e.wait_ge(sems[s_], v)
                if fn is None:
                    continue
                r = fn(e)
                if isinstance(r, (list, tuple)):
                    for ins in r:
                        ins.then_inc(sems[sem], inc)
                else:
                    r.then_inc(sems[sem], inc)

        block.sync(lambda e: run(e, "sp"))
        block.tensor(lambda e: run(e, "pe"))
        block.scalar(lambda e: run(e, "act"))
        block.vector(lambda e: run(e, "dve"))
        block.gpsimd(lambda e: run(e, "pool"))
    return nc


def _bucket(dist):
    d = np.maximum(dist, 0).astype(np.int64)
    large = 16 + (np.log(np.maximum(d, 1).astype(np.float32) / np.float32(16.0)) / np.float32(math.log(128 / 16)) * np.float32(16)).astype(np.int32)
    large = np.minimum(large, 31)
    return np.where(d < 16, d, large).astype(np.int64)


def _bias_tiles(rel_bias, h):
    p = np.arange(128)[:, None]
    c = np.arange(256)[None, :]
    tiles = np.zeros((4, 128, 256), np.float32)
    d0 = 16 + c - p
    tiles[0] = rel_bias[_bucket(d0), h]
    d1 = c + 128 - p
    tiles[1] = rel_bias[_bucket(d1), h]
    d2 = c - p
    tiles[2] = np.where(d2 >= 0, rel_bias[_bucket(d2), h], np.float32(MASKV))
    d3 = c - 128 - p
    tiles[3] = np.where(d3 >= 0, rel_bias[_bucket(d3), h], np.float32(MASKV))
    return tiles


def _prep_inputs(x, meta_tokens, rel_bias, norm_g, w_in, conv_w, conv_b, w_a, b_a, w_x, b_x,
                 lru_lambda, lam_q1, lam_k1, lam_q2, lam_k2, subln_g, w_out, final_g):
    f = lambda a: np.asarray(a, dtype=np.float32)
    x, meta_tokens, rel_bias, norm_g, w_in = f(x), f(meta_tokens), f(rel_bias), f(norm_g), f(w_in)
    conv_w, conv_b, w_a, b_a, w_x, b_x = f(conv_w), f(conv_b), f(w_a), f(b_a), f(w_x), f(b_x)
    lru_lambda, subln_g, w_out, final_g = f(lru_lambda), f(subln_g), f(w_out), f(final_g)
    lamv = np.concatenate([f(lam_q1)[0], f(lam_k1)[0], f(lam_q2)[0], f(lam_k2)[0]])
    lamv = np.ascontiguousarray(np.broadcast_to(lamv[None, :], (128, 256)))
    ident = np.eye(128, dtype=np.float32)
    xTs = [np.ascontiguousarray(np.concatenate([meta_tokens, x[b]], axis=0).T) for b in range(2)]
    in_maps = []
    pidx = np.arange(128)
    for c in range(8):
        b, j = divmod(c, 4)
        hs = [2 * j, 2 * j + 1]
        cols = []
        for base in (0, 1024, 2048, 3072, 4096, 5120):
            for h in hs:
                cols.append(np.arange(base + h * 128, base + (h + 1) * 128))
        cols = np.concatenate(cols)
        w_in_c = np.ascontiguousarray(w_in[0][:, cols])
        w_out_c = np.ascontiguousarray(w_out[0][:, 512 * j:512 * (j + 1)])
        x_res = np.ascontiguousarray(x[b][:, 512 * j:512 * (j + 1)])
        fg = np.ascontiguousarray(np.broadcast_to(final_g[None, 512 * j:512 * (j + 1)], (128, 512)))
        sm = np.zeros((128, 40), np.float32)
        sm[:, 0:16] = norm_g[0].reshape(16, 128).T
        for bl in range(2):
            ch = hs[bl] * 128 + pidx
            for k in range(4):
                sm[:, 16 + 4 * bl + k] = conv_w[0][k, ch]
            sm[:, 24 + bl] = conv_b[0][ch]
            sm[:, 26 + bl] = b_a[0][ch]
            sm[:, 28 + bl] = b_x[0][ch]
            sm[:, 30 + bl] = lru_lambda[0][ch]
        sm[:, 32] = subln_g[0]
        sm[:, 33] = rel_bias[31, hs[0]]
        sm[:, 34] = rel_bias[31, hs[1]]
        wax = np.zeros((128, 4, 128), np.float32)
        for bl in range(2):
            wax[:, 2 * bl + 0, :] = w_a[0][hs[bl]]
            wax[:, 2 * bl + 1, :] = w_x[0][hs[bl]]
        bt = np.zeros((128, 2, 4, 256), np.float32)
        for hh in range(2):
            bt[:, hh] = np.transpose(_bias_tiles(rel_bias, hs[hh]), (1, 0, 2))
        in_maps.append({
            "xT": xTs[b], "w_in": w_in_c, "w_out": w_out_c, "x_res": x_res, "fg": fg, "smalls": sm,
            "lamv": lamv, "wax": np.ascontiguousarray(wax.reshape(128, 512)),
            "biast": np.ascontiguousarray(bt.reshape(128, 2048)), "ident": ident,
        })
    return in_maps


def kernel(**inputs):
    in_maps = _prep_inputs(**inputs)
    nc = build_program()
    res = run_bass_kernel_spmd(nc, in_maps, core_ids=list(range(8)))
    outp = np.zeros((2, S, D), np.float32)
    for c in range(8):
        b, j = divmod(c, 4)
        outp[b, :, 512 * j:512 * (j + 1)] = np.asarray(res.results[c]["out"], dtype=np.float32)
    return outp
```

```python
import math
from contextlib import ExitStack
import numpy as np
import concourse.bass as bass
import concourse.mybir as mybir
from concourse.bass_utils import run_bass_kernel_spmd

F32 = mybir.dt.float32
BF16 = mybir.dt.bfloat16
ALU = mybir.AluOpType
AF = mybir.ActivationFunctionType

D = 2048
NM = 16
S = 4096
T = S + NM
NT = 9
ENGS = ("pe", "act", "dve", "pool", "sp")
LAM_INIT = 0.8 - 0.6 * math.exp(0.0)
MASKV = -30000.0
DEBUG = False


class Prog:
    def __init__(self):
        self.ops = {e: [] for e in ENGS}
        self.cnt = {}
        self.waited = {e: {} for e in ENGS}
        self.lastw = {}
        self.readers = {}

    def _emit(self, eng, fn, reads, writes, sem, inc, ninst):
        deps = []
        for k in reads:
            if k in self.lastw:
                deps.append(self.lastw[k])
        for k in writes:
            if k in self.lastw:
                deps.append(self.lastw[k])
            deps.extend(self.readers.get(k, ()))
        waits = {}
        for (s, v) in deps:
            if s == "pe" and eng == "pe":
                continue
            if self.waited[eng].get(s, 0) >= v:
                continue
            if waits.get(s, 0) < v:
                waits[s] = v
        for s, v in waits.items():
            self.waited[eng][s] = v
        if inc == 0:
            tk = (sem, self.cnt.get(sem, 0) + 1)
        else:
            self.cnt[sem] = self.cnt.get(sem, 0) + inc * ninst
            tk = (sem, self.cnt[sem])
        self.ops[eng].append((sorted(waits.items()), fn, sem, inc))
        for k in writes:
            self.lastw[k] = tk
            self.readers[k] = []
        for k in reads:
            self.readers.setdefault(k, []).append(tk)
        return tk

    def op(self, eng, fn, reads=(), writes=()):
        return self._emit(eng, fn, reads, writes, eng, 1, 1)

    def op_noinc(self, eng, fn, reads=(), writes=()):
        assert eng == "pe"
        return self._emit(eng, fn, reads, writes, eng, 0, 1)

    def dma(self, fn, sem, n, reads=(), writes=()):
        return self._emit("sp", fn, reads, writes, sem, 16, n)

    def dma_pool(self, fn, sem, n, reads=(), writes=()):
        return self._emit("pool", fn, reads, writes, sem, 16, n)

    def cc(self, fn, sem, reads=(), writes=()):
        return self._emit("pool", fn, reads, writes, sem, 1, 1)

    def barrier(self, skip_prefix=None):
        allk = [(s_, v_) for s_, v_ in self.cnt.items() if not (skip_prefix and s_.startswith(skip_prefix))]
        for e in ENGS:
            waits = []
            for s, v in allk:
                if self.waited[e].get(s, 0) < v and not (s == e == "pe"):
                    waits.append((s, v))
                    self.waited[e][s] = v
            if waits:
                self.ops[e].append((sorted(waits), None, None, 0))


def build_program():
    nc = bass.Bass("TRN2", target_bir_lowering=False)
    P = Prog()

    def din(name, shape, dt=F32):
        return nc.dram_tensor(name, list(shape), dt, kind="ExternalInput").ap()

    xT = din("xT", [D, T])
    w_in = din("w_in", [D, 1536])
    w_out = din("w_out", [D, 512])
    x_res = din("x_res", [S, 512])
    fg_in = din("fg", [128, 512])
    smalls_in = din("smalls", [128, 40])
    lamv_in = din("lamv", [128, 256])
    wax_in = din("wax", [128, 512])
    biast_in = din("biast", [128, 2048])
    ident_in = din("ident", [128, 128])
    out = nc.dram_tensor("out", [S, 512], F32, kind="ExternalOutput").ap()
    agin = [nc.dram_tensor("agin%d" % i, [256, 512], BF16).ap() for i in range(8)]
    agout = [nc.dram_tensor("agout%d" % i, [1024, 512], BF16).ap() for i in range(8)]
    aginl = [nc.dram_tensor("aginl%d" % i, [256, 512], BF16).ap() for i in range(8)]
    agoutl = [nc.dram_tensor("agoutl%d" % i, [1024, 512], BF16).ap() for i in range(8)]
    sqin = [nc.dram_tensor("sqin%d" % i, [128, 4], F32).ap() for i in range(8)]
    sqout = [nc.dram_tensor("sqout%d" % i, [512, 4], F32).ap() for i in range(8)]
    dbg = {}
    if DEBUG:
        dbg["qt"] = nc.dram_tensor("dbg_qt", [128, 2 * S], BF16, kind="ExternalOutput").ap()
        dbg["kt"] = nc.dram_tensor("dbg_kt", [128, 2 * T], BF16, kind="ExternalOutput").ap()
        dbg["vv"] = nc.dram_tensor("dbg_vv", [128, 33 * 256], BF16, kind="ExternalOutput").ap()
        dbg["sga"] = nc.dram_tensor("dbg_sga", [128, 2 * S], BF16, kind="ExternalOutput").ap()
        dbg["ag0"] = nc.dram_tensor("dbg_ag0", [1024, 512], BF16, kind="ExternalOutput").ap()

    ARENA_BYTES = 207 * 1024
    arena = nc.alloc_sbuf_tensor("arena", [128, ARENA_BYTES // 2], BF16)
    ptr = {"p": 0, 1: 0, 2: 0}

    def carve(phase, shape, dt):
        nel = 1
        for s_ in shape:
            nel *= s_
        nb = nel * (4 if dt == F32 else 2)
        nb_al = (nb + 63) // 64 * 64
        if phase == "p":
            off = ptr["p"]
            ptr["p"] += nb_al
            ptr[1] = ptr[2] = ptr["p"]
        else:
            off = ptr[phase]
            ptr[phase] += nb_al
        assert off + nb_al <= ARENA_BYTES, ("SBUF overflow", phase, off + nb_al)
        v = arena[:, off // 2: off // 2 + nb // 2]
        if dt == F32:
            v = v.bitcast(F32)
        if len(shape) == 2:
            return v.rearrange("p (a b) -> p a b", a=shape[0])
        if len(shape) == 3:
            return v.rearrange("p (a b c) -> p a b c", a=shape[0], b=shape[1])
        return v

    QT = carve("p", [2, S], BF16)
    KT = carve("p", [2, T], BF16)
    VV = carve("p", [33, 256], BF16)
    SGA = carve("p", [2, S], BF16)
    IDB = carve("p", [128], BF16)
    ONES = carve("p", [128], BF16)
    ONESM = carve("p", [128], BF16)
    ONESD = carve("p", [128], BF16)
    BIAS = carve("p", [2, 4, 512], BF16)
    SM = carve("p", [64], F32)
    WAX = carve("p", [4, 128], BF16)
    HST = carve("p", [2], F32)
    WP = carve(1, [16, 1536], BF16)
    XS = [carve(1, [4, 512], F32) for _ in range(2)]
    HB = [carve(1, [16, 512], BF16) for _ in range(2)]
    XSQ = carve(1, [4, 512], BF16)
    RSTD = [carve(1, [512], F32) for _ in range(2)]
    U = carve(1, [2, 520], F32)
    TT = [carve(1, [512], F32) for _ in range(6)]
    UCBF = carve(1, [512], BF16)
    SGL = carve(1, [2, 512], BF16)
    VT = carve(1, [512], BF16)
    LO = [carve(1, [2, 512], BF16) for _ in range(2)]
    TT6 = carve(1, [512], F32)
    UCBF2 = carve(1, [512], BF16)
    LAMV = XSQ[:, 0, :].bitcast(F32)
    WO = carve(2, [16, 512], BF16)
    MIXG = [carve(2, [16, 512], BF16) for _ in range(2)]
    XRES = carve(2, [4, 512], F32)
    YB = [carve(2, [4, 512], F32) for _ in range(2)]
    FG = carve(2, [512], F32)
    NPT = 6
    PT = [carve(2, [2, 256], BF16) for _ in range(NPT)]
    RL = carve(2, [512], F32)
    ON = carve(2, [512], F32)
    DIFF = carve(2, [256], F32)
    T1 = carve(2, [256], F32)
    R2 = carve(2, [256], F32)
    SQ = carve(2, [256], BF16)
    MIX = [carve(2, [2, 512], BF16) for _ in range(2)]
    SSQ = carve(2, [4], F32)
    SQG = carve(2, [4, 4], F32)
    TOT = carve(2, [4], F32)
    RS = carve(2, [4], F32)
    JUNK = carve(2, [512], F32)
    WOST32 = carve(2, [16, 512], F32)
    GS = [carve(2, [512], BF16) for _ in range(2)]
    QBD = [carve(2, [512], BF16) for _ in range(2)]

    psb = [nc.alloc_psum_tensor("psb%d" % i, [128, 512], F32) for i in range(4)]
    pss = [nc.alloc_psum_tensor("pss%d" % i, [128, 1024], F32) for i in range(2)]
    PSK = ["psb0", "psb1", "psb2", "psb3", "pss0", "pss1"]

    c31col = [33, 34]

    P.dma(lambda e: [e.dma_start(out=SM[:, 0:40], in_=smalls_in)], "d_sm", 1, writes=["SM_in"])
    P.dma(lambda e: [e.dma_start(out=LAMV[:, :], in_=lamv_in)], "d_lamv", 1, writes=["LAMV", "XSQ"])
    P.dma(lambda e: [e.dma_start(out=TT[0][:, 0:128], in_=ident_in)], "d_id", 1, writes=["T0"])
    P.op("dve", lambda e: e.tensor_copy(out=IDB[:, :], in_=TT[0][:, 0:128]), reads=["T0"], writes=["IDB"])
    P.op("dve", lambda e: e.memset(ONES[:, :], 1.0), writes=["ONES"])
    P.op("dve", lambda e: e.memset(ONESM[:, :], 1.0 / 128.0), writes=["ONESM"])
    P.op("dve", lambda e: e.memset(ONESD[:, :], 1.0 / 2048.0), writes=["ONESD"])
    P.op("dve", lambda e: e.memset(U[:, :, :], 0.0), writes=["U0", "U1"])
    P.op("dve", lambda e: e.memset(HST[:, :], 0.0), writes=["HST0", "HST1"])
    P.dma(lambda e: [e.dma_start(out=TT[1][:, :], in_=wax_in)], "d_wax", 1, writes=["T1"])
    P.op("dve", lambda e: e.tensor_copy(out=WAX[:, :, :].rearrange("p a b -> p (a b)"), in_=TT[1][:, :]),
         reads=["T1"], writes=["WAX"])
    P.dma(lambda e: [e.dma_start(out=TT[2 + q4][:, :], in_=biast_in[:, q4 * 512:(q4 + 1) * 512]) for q4 in range(4)],
          "d_bias", 4, writes=["T2", "T3", "T4", "T5"])
    for q4 in range(4):
        for half in range(2):
            P.op("dve", lambda e, q4=q4, half=half: e.tensor_copy(
                out=BIAS[:, q4 // 2, (q4 % 2) * 2:(q4 % 2) * 2 + 2, half * 256:(half + 1) * 256],
                in_=TT[2 + q4][:, :].rearrange("p (a b) -> p a b", a=2)),
                reads=["T%d" % (2 + q4)], writes=["BIAS"])
    P.op("act", lambda e: e.activation(out=SM[:, 40:42], in_=SM[:, 30:32], func=AF.Exp, scale=-1.0),
         reads=["SM_in"], writes=["SM_a"])
    P.op("act", lambda e: e.activation(out=SM[:, 40:42], in_=SM[:, 40:42], func=AF.Ln, bias=1.0),
         reads=["SM_a"], writes=["SM_a"])
    P.op("dve", lambda e: e.tensor_scalar(out=SM[:, 42:44], in0=SM[:, 40:42], scalar1=-16.0, scalar2=None, op0=ALU.mult),
         reads=["SM_a"], writes=["SM_c2"])
    P.op("dve", lambda e: e.tensor_scalar(out=SM[:, 40:42], in0=SM[:, 40:42], scalar1=-8.0, scalar2=None, op0=ALU.mult),
         reads=["SM_a", "SM_c2"], writes=["SM_a"])
    P.op("dve", lambda e: e.tensor_scalar(out=SM[:, 44:48], in0=SM[:, 26:30], scalar1=-1.0, scalar2=None, op0=ALU.mult),
         reads=["SM_in"], writes=["SM_nb"])
    P.op("dve", lambda e: e.tensor_tensor(out=LAMV[:, 0:64], in0=LAMV[:, 0:64], in1=LAMV[:, 64:128], op=ALU.mult),
         reads=["LAMV"], writes=["LAMV"])
    P.op("dve", lambda e: e.tensor_tensor(out=LAMV[:, 128:192], in0=LAMV[:, 128:192], in1=LAMV[:, 192:256], op=ALU.mult),
         reads=["LAMV"], writes=["LAMV"])
    P.op("dve", lambda e: e.tensor_reduce(out=SM[:, 50:51], in_=LAMV[:, 0:64], axis=mybir.AxisListType.X, op=ALU.add),
         reads=["LAMV"], writes=["SM_l"])
    P.op("dve", lambda e: e.tensor_reduce(out=SM[:, 51:52], in_=LAMV[:, 128:192], axis=mybir.AxisListType.X, op=ALU.add),
         reads=["LAMV", "SM_l"], writes=["SM_l", "XSQ"])
    P.op("act", lambda e: e.activation(out=SM[:, 52:54], in_=SM[:, 50:52], func=AF.Exp), reads=["SM_l"], writes=["SM_l2"])
    P.op("dve", lambda e: e.scalar_tensor_tensor(out=SM[:, 48:49], in0=SM[:, 53:54], scalar=-LAM_INIT, in1=SM[:, 52:53],
                                                 op0=ALU.add, op1=ALU.subtract), reads=["SM_l2"], writes=["SM_nl"])
    P.op("dve", lambda e: e.tensor_scalar(out=SM[:, 49:50], in0=SM[:, 32:33], scalar1=1.0 - LAM_INIT, scalar2=None, op0=ALU.mult),
         reads=["SM_in"], writes=["SM_gs"])
    SMK = ["SM_in", "SM_a", "SM_c2", "SM_nb", "SM_nl", "SM_gs"]

    xT_v = xT.rearrange("(c p) t -> p c t", p=128)

    def tile_tok(ti):
        return (0, NM) if ti == 0 else (NM + 512 * (ti - 1), 512)

    def load_piece(g):
        if g >= 4 * NT:
            return
        ti, k = divmod(g, 4)
        tok0, n = tile_tok(ti)
        sl = g % 2
        P.dma(lambda e: [e.dma_start(out=XS[sl][:, :, 0:n], in_=xT_v[:, 4 * k:4 * k + 4, tok0:tok0 + n])],
              "d_xs%d" % sl, 1, writes=["XS%d" % sl])

    def stats_piece(ti, k):
        g = 4 * ti + k
        tok0, n = tile_tok(ti)
        sl = g % 2
        hb = ti % 2
        for q in range(4):
            dch = 4 * k + q
            if q % 2 == 0:
                P.op("dve", lambda e, q=q, dch=dch: e.tensor_scalar(
                    out=HB[hb][:, dch, 0:n], in0=XS[sl][:, q, 0:n], scalar1=SM[:, dch:dch + 1], scalar2=None, op0=ALU.mult),
                    reads=["XS%d" % sl, "SM_in"], writes=["HB%d_%d" % (hb, dch)])
            else:
                P.op("act", lambda e, q=q, dch=dch: e.activation(
                    out=HB[hb][:, dch, 0:n], in_=XS[sl][:, q, 0:n], func=AF.Copy, scale=SM[:, dch:dch + 1]),
                    reads=["XS%d" % sl, "SM_in"], writes=["HB%d_%d" % (hb, dch)])
        P.op("pool", lambda e: e.tensor_tensor(out=XSQ[:, :, 0:n], in0=XS[sl][:, :, 0:n], in1=XS[sl][:, :, 0:n], op=ALU.mult),
             reads=["XS%d" % sl], writes=["XSQ"])
        load_piece(g + 2)

        def mm(e):
            r = None
            for q in range(4):
                r = e.matmul(psb[3][:, 0:n], lhsT=ONESD[:, :], rhs=XSQ[:, q, 0:n],
                             start=(k == 0 and q == 0), stop=(k == 3 and q == 3))
            return r
        P.op("pe", mm, reads=["XSQ", "ONESD"], writes=["psb3"])
        if k == 3:
            rb = ti % 2
            P.op("act", lambda e: e.activation(out=RSTD[rb][:, 0:n], in_=psb[3][:, 0:n], func=AF.Ln, bias=1e-6),
                 reads=["psb3"], writes=["RSTD%d" % rb])
            P.op("act", lambda e: e.activation(out=RSTD[rb][:, 0:n], in_=RSTD[rb][:, 0:n], func=AF.Exp, scale=-0.5),
                 reads=["RSTD%d" % rb], writes=["RSTD%d" % rb])

    w_in_v = w_in.rearrange("(c p) n -> p c n", p=128)
    W_ORDER = [8, 9, 0, 1, 2, 3, 4, 5, 6, 7, 10, 11]

    def load_w(cc):
        P.dma_pool(lambda e: [e.dma_start(out=WP[:, :, cc * 128:(cc + 1) * 128], in_=w_in_v[:, :, cc * 128:(cc + 1) * 128])],
                   "d_wp%d" % cc, 1, writes=["WP%d" % cc])

    load_piece(0)
    load_piece(1)
    for cc in W_ORDER[0:4]:
        load_w(cc)
    for k in range(4):
        stats_piece(0, k)
    for cc in W_ORDER[4:]:
        load_w(cc)

    bank_rr = [0]
    lo_cnt = [0]

    def phase1_tile(ti):
        tok0, n = tile_tok(ti)
        hb = ti % 2
        rb = ti % 2
        meta = (ti == 0)
        ci = ti - 1
        r0 = 512 * ci
        order = [8, 9, 2, 3, 4, 5] if meta else [8, 9, 0, 1, 2, 3, 4, 5, 6, 7, 10, 11]
        pend_tr = []
        HBK = ["HB%d_%d" % (hb, k) for k in range(16)]
        nstat = [0]

        def in_chunk(idx, cc):
            if pend_tr:
                pend_tr.pop(0)()
            bk = bank_rr[0] % 3
            bank_rr[0] += 1
            ps = psb[bk]
            psk = "psb%d" % bk

            def mm(e, cc=cc, ps=ps):
                r = None
                for c in range(16):
                    r = e.matmul(ps[:, 0:n], lhsT=WP[:, c, cc * 128:(cc + 1) * 128], rhs=HB[hb][:, c, 0:n],
                                 start=(c == 0), stop=(c == 15))
                return r
            P.op("pe", mm, reads=["WP%d" % cc] + HBK, writes=[psk])
            rk = "RSTD%d" % rb
            if cc in (0, 1):
                hh = cc
                P.op("dve", lambda e, ps=ps, hh=hh: e.scalar_tensor_tensor(
                    out=QT[:, hh, r0:r0 + n], in0=ps[:, 0:n], scalar=0.125, in1=RSTD[rb][:, 0:n], op0=ALU.mult, op1=ALU.mult),
                    reads=[psk, rk], writes=["QT"])
            elif cc in (2, 3):
                hh = cc - 2
                P.op("dve", lambda e, ps=ps, hh=hh: e.tensor_tensor(
                    out=KT[:, hh, tok0:tok0 + n], in0=ps[:, 0:n], in1=RSTD[rb][:, 0:n], op=ALU.mult),
                    reads=[psk, rk], writes=["KT"])
            elif cc in (4, 5):
                hh = cc - 4
                P.op("dve", lambda e, ps=ps: e.tensor_tensor(out=VT[:, 0:n], in0=ps[:, 0:n], in1=RSTD[rb][:, 0:n], op=ALU.mult),
                     reads=[psk, rk], writes=["VT"])
                nblk = 1 if meta else 4
                blk0 = 0 if meta else 1 + 4 * ci
                bw = NM if meta else 128
                ptv = pss[1][:, 0:256].bitcast(BF16)

                def tr(e, nblk=nblk, bw=bw, ptv=ptv):
                    r = None
                    for jb in range(nblk):
                        r = e.transpose(out=ptv[0:bw, jb * 128:(jb + 1) * 128], in_=VT[:, jb * bw:(jb + 1) * bw], identity=IDB[:, :])
                    return r
                def deferred(hh=hh, nblk=nblk, bw=bw, blk0=blk0, ptv=ptv, tr=tr):
                    P.op("pe", tr, reads=["VT", "IDB"], writes=["pss1"])
                    P.op("act", lambda e: e.activation(
                        out=VV[0:bw, blk0:blk0 + nblk, hh * 128:(hh + 1) * 128],
                        in_=ptv[0:bw, 0:nblk * 128].rearrange("p (a b) -> p a b", a=nblk), func=AF.Copy),
                        reads=["pss1"], writes=["VV"])
                pend_tr.append(deferred)
            elif cc in (6, 7, 10, 11):
                tg, te = TT[4], TT[5]
                P.op("dve", lambda e, ps=ps: e.tensor_tensor(out=tg[:, 0:n], in0=ps[:, 0:n], in1=RSTD[rb][:, 0:n], op=ALU.mult),
                     reads=[psk, rk], writes=["T4"])
                P.op("act", lambda e: e.activation(out=te[:, 0:n], in_=tg[:, 0:n], func=AF.Exp, scale=-1.0),
                     reads=["T4"], writes=["T5"])
                P.op("act", lambda e: e.activation(out=te[:, 0:n], in_=te[:, 0:n], func=AF.Ln, bias=1.0),
                     reads=["T5"], writes=["T5"])
                P.op("act", lambda e: e.activation(out=te[:, 0:n], in_=te[:, 0:n], func=AF.Exp, scale=-1.0),
                     reads=["T5"], writes=["T5"])
                if cc in (6, 7):
                    hh = cc - 6
                    P.op("dve", lambda e, hh=hh: e.tensor_tensor(out=SGA[:, hh, r0:r0 + n], in0=tg[:, 0:n], in1=te[:, 0:n], op=ALU.mult),
                         reads=["T4", "T5"], writes=["SGA"])
                else:
                    blk = cc - 10
                    P.op("dve", lambda e, blk=blk: e.tensor_tensor(out=SGL[:, blk, 0:n], in0=tg[:, 0:n], in1=te[:, 0:n], op=ALU.mult),
                         reads=["T4", "T5"], writes=["SGL%d" % blk])
            elif cc in (8, 9):
                blk = cc - 8
                P.op("dve", lambda e, ps=ps, blk=blk: e.tensor_tensor(
                    out=U[:, blk, 3:3 + n], in0=ps[:, 0:n], in1=RSTD[rb][:, 0:n], op=ALU.mult),
                    reads=[psk, rk], writes=["U%d" % blk])
            if ti + 1 < NT and idx >= 1 and idx % 2 == 1 and nstat[0] < 4:
                stats_piece(ti + 1, nstat[0])
                nstat[0] += 1

        hooks = make_hooks_for(ti)
        for idx, cc in enumerate(order):
            in_chunk(idx, cc)
            for h in hooks.pop(idx, []):
                h()
        while pend_tr:
            pend_tr.pop(0)()
        while ti + 1 < NT and nstat[0] < 4:
            stats_piece(ti + 1, nstat[0])
            nstat[0] += 1

        for idx in sorted(hooks.keys()):
            for h in hooks[idx]:
                h()

    def make_hooks_for(ti):
        hooks = {}
        if ti >= 1:
            hooks[1] = [lambda: lru_B1(ti - 1, 0)]
            hooks[3] = [lambda: lru_B2(ti - 1, 0)]
            hooks[4] = [lambda: lru_B1(ti - 1, 1)]
            hooks[6] = [lambda: lru_B2(ti - 1, 1)]
            hooks[9] = [lambda: lru_gather(ti - 1)]
        hooks[7] = [lambda: lru_A(ti, 0), lambda: lru_A(ti, 1)]
        return hooks

    UCB = [TT[0], TT6]
    UCK = ["T0", "T6"]
    UCBFS = [UCBF, UCBF2]
    UCBFK = ["UCBF0", "UCBF1"]

    def lru_A(ti, blk):
        tok0, n = tile_tok(ti)
        uk = "U%d" % blk
        uc, uck = UCB[blk], UCK[blk]
        cw = 16 + 4 * blk
        P.op("dve", lambda e: e.tensor_scalar(
            out=uc[:, 0:n], in0=U[:, blk, 3:3 + n], scalar1=SM[:, cw + 3:cw + 4], scalar2=SM[:, 24 + blk:25 + blk],
            op0=ALU.mult, op1=ALU.add), reads=[uk, "SM_in"], writes=[uck])
        for kk in (2, 1, 0):
            P.op("dve", lambda e, kk=kk: e.scalar_tensor_tensor(
                out=uc[:, 0:n], in0=U[:, blk, kk:kk + n], scalar=SM[:, cw + kk:cw + kk + 1], in1=uc[:, 0:n],
                op0=ALU.mult, op1=ALU.add), reads=[uk, "SM_in", uck], writes=[uck])
        P.op("pool", lambda e: e.tensor_copy(out=U[:, blk, 0:3], in_=U[:, blk, n:n + 3]), reads=[uk], writes=[uk])
        P.op("act", lambda e: e.activation(out=UCBFS[blk][:, 0:n], in_=uc[:, 0:n], func=AF.Copy),
             reads=[uck], writes=[UCBFK[blk]])

    def lru_B1(ti, blk):
        tok0, n = tile_tok(ti)
        tr_, ti_, ta, tm = TT[1], TT[2], TT[3], TT[4]

        def gmm(e):
            e.matmul(pss[0][:, 0:n], lhsT=WAX[:, 2 * blk, :], rhs=UCBFS[blk][:, 0:n], start=True, stop=True)
            return e.matmul(pss[0][:, 512:512 + n], lhsT=WAX[:, 2 * blk + 1, :], rhs=UCBFS[blk][:, 0:n], start=True, stop=True)
        P.op("pe", gmm, reads=["WAX", UCBFK[blk]], writes=["pss0"])
        for gi, (tdst, tkey, off) in enumerate(((tr_, "T1", 0), (ti_, "T2", 512))):
            P.op("act", lambda e, tdst=tdst, off=off, gi=gi: e.activation(
                out=tdst[:, 0:n], in_=pss[0][:, off:off + n], func=AF.Exp, scale=-1.0,
                bias=SM[:, 44 + 2 * gi + blk:45 + 2 * gi + blk]), reads=["pss0", "SM_nb"], writes=[tkey])
            P.op("act", lambda e, tdst=tdst: e.activation(out=tdst[:, 0:n], in_=tdst[:, 0:n], func=AF.Ln, bias=1.0),
                 reads=[tkey], writes=[tkey])
            P.op("act", lambda e, tdst=tdst: e.activation(out=tdst[:, 0:n], in_=tdst[:, 0:n], func=AF.Exp, scale=-1.0),
                 reads=[tkey], writes=[tkey])
        P.op("act", lambda e: e.activation(out=ta[:, 0:n], in_=tr_[:, 0:n], func=AF.Exp, scale=SM[:, 40 + blk:41 + blk]),
             reads=["T1", "SM_a"], writes=["T3"])
        P.op("act", lambda e: e.activation(out=tm[:, 0:n], in_=tr_[:, 0:n], func=AF.Exp, scale=SM[:, 42 + blk:43 + blk]),
             reads=["T1", "SM_c2"], writes=["T4"])
        P.op("act", lambda e: e.activation(out=tm[:, 0:n], in_=tm[:, 0:n], func=AF.Ln, scale=-1.0, bias=1.0),
             reads=["T4"], writes=["T4"])
        P.op("act", lambda e: e.activation(out=tm[:, 0:n], in_=tm[:, 0:n], func=AF.Exp, scale=0.5),
             reads=["T4"], writes=["T4"])

    def lru_B2(ti, blk):
        tok0, n = tile_tok(ti)
        meta = (ti == 0)
        ci = ti - 1
        uc, uck = UCB[blk], UCK[blk]
        tr_, ti_, ta, tm = TT[1], TT[2], TT[3], TT[4]
        if meta:
            P.op("dve", lambda e: e.memset(tm[:, 0:1], 1.0), reads=["T4"], writes=["T4"])
        P.op("dve", lambda e: e.tensor_tensor(out=ti_[:, 0:n], in0=ti_[:, 0:n], in1=tm[:, 0:n], op=ALU.mult),
             reads=["T2", "T4"], writes=["T2"])
        P.op("dve", lambda e: e.tensor_tensor(out=ti_[:, 0:n], in0=ti_[:, 0:n], in1=uc[:, 0:n], op=ALU.mult),
             reads=["T2", uck], writes=["T2"])
        P.op("dve", lambda e: e.tensor_tensor_scan(
            out=tr_[:, 0:n], data0=ta[:, 0:n], data1=ti_[:, 0:n], initial=HST[:, blk:blk + 1], op0=ALU.mult, op1=ALU.add),
            reads=["T3", "T2", "HST%d" % blk, "T1"], writes=["T1"])
        P.op("dve", lambda e: e.tensor_copy(out=HST[:, blk:blk + 1], in_=tr_[:, n - 1:n]),
             reads=["T1"], writes=["HST%d" % blk])
        if not meta:
            lb = ci % 2
            P.op("dve", lambda e: e.tensor_tensor(out=LO[lb][:, blk, :], in0=tr_[:, 0:n], in1=SGL[:, blk, 0:n], op=ALU.mult),
                 reads=["T1", "SGL%d" % blk], writes=["LO%d" % lb])
            if blk == 1:
                P.dma(lambda e: [e.dma_start(out=aginl[ci].rearrange("(b p) t -> p b t", p=128), in_=LO[lb][:, :, :])],
                      "d_lo%d" % lb, 1, reads=["LO%d" % lb], writes=["AGINL%d" % ci])

    def lru_gather(ti):
        ci = ti - 1
        if ci < 0:
            return
        P.cc(lambda e: e.collective_compute("AllGather", ALU.bypass, replica_groups=[[0, 1, 2, 3], [4, 5, 6, 7]],
                                            ins=[aginl[ci]], outs=[agoutl[ci]]),
             "cc_l", reads=["AGINL%d" % ci], writes=["AGOUTL%d" % ci])

    for ti in range(NT):
        phase1_tile(ti)
    lru_B1(NT - 1, 0)
    lru_B2(NT - 1, 0)
    lru_B1(NT - 1, 1)
    lru_B2(NT - 1, 1)
    lru_gather(NT - 1)

    if DEBUG:
        P.dma(lambda e: [e.dma_start(out=dbg["qt"], in_=QT[:, :, :].rearrange("p a b -> p (a b)")),
                         e.dma_start(out=dbg["kt"], in_=KT[:, :, :].rearrange("p a b -> p (a b)")),
                         e.dma_start(out=dbg["vv"], in_=VV[:, :, :].rearrange("p a b -> p (a b)")),
                         e.dma_start(out=dbg["sga"], in_=SGA[:, :, :].rearrange("p a b -> p (a b)"))],
              "d_dbg", 4, reads=["QT", "KT", "VV", "SGA"], writes=["DBG"])

    P.barrier(skip_prefix="cc_")
    w_out_v = w_out.rearrange("(c p) n -> p c n", p=128)
    P.dma(lambda e: [e.dma_start(out=WOST32[:, 4 * q_:4 * q_ + 4, :], in_=w_out_v[:, 4 * q_:4 * q_ + 4, :]) for q_ in range(4)],
          "d_wo", 4, writes=["WOST32"])
    WO_CAST = [False]

    def cast_wo():
        if WO_CAST[0]:
            return
        WO_CAST[0] = True
        for q_ in range(4):
            P.op("dve", lambda e, q_=q_: e.tensor_copy(out=WO[:, 4 * q_:4 * q_ + 4, :], in_=WOST32[:, 4 * q_:4 * q_ + 4, :]),
                 reads=["WOST32"], writes=["WO"])
    P.dma(lambda e: [e.dma_start(out=FG[:, :], in_=fg_in)], "d_fg", 1, writes=["FG"])
    P.op("dve", lambda e: e.memset(QBD[0][:, :], 0.0), writes=["QBD0"])
    P.op("dve", lambda e: e.memset(QBD[1][:, :], 0.0), writes=["QBD1"])

    pt_rr = [0]
    s_rr = [0]

    PROC = [0, 1, 2, 3, 4, 5, 6, 7]
    POS = {c_: p_ for p_, c_ in enumerate(PROC)}

    SCH = {"t": 0.0, "cc_free": 0.0, "inj": None, "inj_chunk": None, "bseq": 0}
    pend_B = []
    pend_F = []
    pend_fin2 = []
    BSEQ = {}
    CC_A, CC_B, LOAD_LAT = 48.0, 14.0, 12.0

    def attention(i):
        mb = POS[i] % 2
        for hh in range(2):
            for qs in range(2):
                att_tile(i, mb, hh, qs)
        def fin_chunk():
            P.dma(lambda e: [e.dma_start(out=agin[i].rearrange("(b p) t -> p b t", p=128), in_=MIX[mb][:, :, :])],
                  "d_mix%d" % mb, 1, reads=["MIX%d" % mb], writes=["AGIN%d_a" % i])
            P.cc(lambda e: e.collective_compute("AllGather", ALU.bypass, replica_groups=[[0, 1, 2, 3], [4, 5, 6, 7]],
                                                ins=[agin[i]], outs=[agout[i]]),
                 "cc_a", reads=["AGIN%d_a" % i], writes=["AGOUT%d" % i])
            st = max(SCH["cc_free"], SCH["t"] + 2.0)
            SCH["cc_free"] = st + CC_A
            pend_B.append((i, SCH["cc_free"]))
        pend_post.append(fin_chunk)

    def sched_block(cost):
        SCH["t"] += cost
        if SCH["inj"] is None and pend_B and pend_B[0][1] + 8.0 <= SCH["t"]:
            k_, _r = pend_B.pop(0)
            start_B(k_)
        if SCH["inj"] is not None and SCH["inj_ready"] <= SCH["t"]:
            run_units(1)
        if SCH["inj"] is not None and pend_B and pend_B[0][1] + 8.0 <= SCH["t"]:
            prefetch_mixg(pend_B[0][0])

    PREF = set()

    def assign_seq(k_):
        if k_ not in BSEQ:
            BSEQ[k_] = SCH["bseq"]
            SCH["bseq"] += 1

    def prefetch_mixg(k_):
        if k_ in PREF:
            return
        assign_seq(k_)
        PREF.add(k_)
        outproj_prep_mixg(k_)

    def start_B(k_):
        assign_seq(k_)
        if BSEQ[k_] >= 2:
            force_finalize_upto(BSEQ[k_] - 2)
        cast_wo()
        pre = k_ in PREF
        prefetch_mixg(k_)
        outproj_prep(k_)
        SCH["inj"] = outproj_units(k_)
        SCH["inj_chunk"] = k_
        SCH["inj_ready"] = SCH["t"] + (6.0 if pre else LOAD_LAT)

    def run_units(n_):
        for _ in range(n_):
            if SCH["inj"] is None:
                return
            try:
                next(SCH["inj"])
                SCH["t"] += 0.06
            except StopIteration:
                k_ = SCH["inj_chunk"]
                SCH["inj"] = None
                st = max(SCH["cc_free"], SCH["t"] + 2.0)
                SCH["cc_free"] = st + CC_B
                pend_F.append((k_, SCH["cc_free"]))

    FIN_DONE = set()

    def force_finalize_upto(seq):
        for k_, sq_ in list(BSEQ.items()):
            if sq_ <= seq and k_ not in FIN_DONE:
                for it in list(pend_F):
                    if it[0] == k_:
                        pend_F.remove(it)
                do_finalize(k_, split=False)

    def do_finalize(k_, split):
        FIN_DONE.add(k_)
        finalize1(k_)
        if split:
            pend_fin2.append(lambda: finalize2(k_))
        else:
            finalize2(k_)

    def sched_tile_start():
        if pend_F and pend_F[0][1] + 45.0 <= SCH["t"]:
            k_, _r = pend_F.pop(0)
            if k_ not in FIN_DONE:
                do_finalize(k_, split=True)

    tile_rr = [0]
    LG = 12
    TILES = [(i_, hh_, qs_) for i_ in PROC for hh_ in range(2) for qs_ in range(2)]
    pend_post = []

    def emit_qbd(t):
        if t >= len(TILES):
            return
        i_, hh_, qs_ = TILES[t]
        q0_ = 256 * (2 * i_ + qs_)
        qb_ = t % 2
        P.op("pool", lambda e: e.tensor_copy(out=QBD[qb_][0:64, 0:256], in_=QT[0:64, hh_, q0_:q0_ + 256]),
             reads=["QT"], writes=["QBD%d" % qb_])
        P.op("pool", lambda e: e.tensor_copy(out=QBD[qb_][64:128, 256:512], in_=QT[64:128, hh_, q0_:q0_ + 256]),
             reads=["QT"], writes=["QBD%d" % qb_])

    def att_tile(i, mb, hh, qs):
        sched_tile_start()
        qi = 2 * i + qs
        q0 = 256 * qi
        tix = tile_rr[0]
        tile_rr[0] += 1
        ob = tix % 2
        psO = psb[0] if ob == 0 else pss[1][:, 0:512]
        psOk = "psb0" if ob == 0 else "pss1a"
        qb = tix % 2
        qbk = "QBD%d" % qb
        if tix == 0:
            emit_qbd(0)
        emit_qbd(tix + 1)
        blocks = [("real", m) for m in range(2 * qi + 2)] + [("meta", None)]
        nb = len(blocks)
        nreal = nb - 1
        grp_first = [None]
        linfo = {}
        l_started = [False]

        def s_stage(nidx):
            kind, m = blocks[nidx]
            sb = s_rr[0] % 3
            s_rr[0] += 1
            if kind == "meta":
                M, k0 = NM, 0
                spec = 0 if qi == 0 else None
            else:
                M, k0 = 128, NM + 128 * m
                spec = {2 * qi - 1: 1, 2 * qi: 2, 2 * qi + 1: 3}.get(m)
            psS = (pss[0][:, 0:512], pss[0][:, 512:1024], pss[1][:, 512:1024])[sb]
            psSk = ("pss0a", "pss0b", "pss1b")[sb]

            def mm(e):
                r = e.matmul(psS[0:M, :], lhsT=KT[:, hh, k0:k0 + M], rhs=QBD[qb][:, :], start=True, stop=(spec is None))
                if spec is not None:
                    r = e.matmul(psS[0:M, :], lhsT=IDB[0:M, 0:M], rhs=BIAS[0:M, hh, spec, :], start=False, stop=True)
                return r
            P.op("pe", mm, reads=["KT", qbk, "IDB", "BIAS"], writes=[psSk])
            pb = pt_rr[0] % NPT
            pt_rr[0] += 1
            ptk = "PT%d" % pb
            ptf = PT[pb][0:M, :, :].rearrange("p a b -> p (a b)")
            if spec is None:
                P.op("act", lambda e: e.activation(out=ptf, in_=psS[0:M, :], func=AF.Exp, bias=SM[0:M, c31col[hh]:c31col[hh] + 1]),
                     reads=[psSk, "SM_in"], writes=[ptk])
            else:
                P.op("act", lambda e: e.activation(out=ptf, in_=psS[0:M, :], func=AF.Exp), reads=[psSk], writes=[ptk])
            lrhs = None
            if kind == "meta":
                lrhs = (ptf, [ptk], M)
            else:
                g, pos = divmod(nidx, LG)
                gs = GS[g % 2]
                gk = "GS%d" % (g % 2)
                last_in_group = (pos == LG - 1) or (nidx == nreal - 1)
                if pos == 0:
                    grp_first[0] = (ptf, ptk)
                    if last_in_group:
                        lrhs = (ptf, [ptk], M)
                elif pos == 1:
                    f_ap, f_k = grp_first[0]
                    P.op("dve", lambda e: e.tensor_tensor(out=gs[:, :], in0=f_ap, in1=ptf, op=ALU.add),
                         reads=[f_k, ptk], writes=[gk])
                else:
                    P.op("dve", lambda e: e.tensor_tensor(out=gs[:, :], in0=gs[:, :], in1=ptf, op=ALU.add),
                         reads=[gk, ptk], writes=[gk])
                if pos >= 1 and last_in_group:
                    lrhs = (gs[:, :], [gk], 128)
            linfo[nidx] = lrhs
            return (pb, M, kind, m)

        def pv_stage(nidx, info):
            pb, M, kind, m = info
            vb = 0 if kind == "meta" else 1 + m
            rhs = PT[pb][0:M, :, :].rearrange("p a b -> p (a b)")
            P.op("pe", lambda e: e.matmul(psO[:, :], lhsT=VV[0:M, vb, hh * 128:(hh + 1) * 128], rhs=rhs,
                                          start=(nidx == 0), stop=(nidx == nb - 1)),
                 reads=["VV", "PT%d" % pb], writes=[psOk])
            lr = linfo.pop(nidx)
            if lr is not None:
                l_ap, l_keys, lM = lr
                first = not l_started[0]
                l_started[0] = True
                P.op("pe", lambda e: e.matmul(psb[1][:, :], lhsT=ONES[0:lM, :], rhs=l_ap, start=first, stop=(nidx == nb - 1)),
                     reads=["ONES"] + l_keys, writes=["psb1"])
            sched_block(0.93)
            if nidx == 2:
                while pend_post:
                    pend_post.pop(0)()
            if nidx == 4 and pend_fin2:
                pend_fin2.pop(0)()

        infos = {}
        AHEAD = 2
        for nidx in range(min(AHEAD, nb)):
            infos[nidx] = s_stage(nidx)
        for nidx in range(nb):
            if nidx + AHEAD < nb:
                infos[nidx + AHEAD] = s_stage(nidx + AHEAD)
            pv_stage(nidx, infos.pop(nidx))
        P.op("act", lambda e: e.activation(out=RL[:, :], in_=psb[1][:, :], func=AF.Ln), reads=["psb1"], writes=["RL"])
        P.op("act", lambda e: e.activation(out=RL[:, :], in_=RL[:, :], func=AF.Exp, scale=-1.0), reads=["RL"], writes=["RL"])
        P.op("dve", lambda e: e.tensor_tensor(out=ON[:, :], in0=psO[:, :], in1=RL[:, :], op=ALU.mult),
             reads=[psOk, "RL"], writes=["ON"])
        P.op("dve", lambda e: e.scalar_tensor_tensor(out=DIFF[:, :], in0=ON[:, 256:512], scalar=SM[:, 48:49], in1=ON[:, 0:256],
                                                     op0=ALU.mult, op1=ALU.add), reads=["ON", "SM_nl"], writes=["DIFF"])
        P.op("pool", lambda e: e.tensor_tensor(out=SQ[:, :], in0=DIFF[:, :], in1=DIFF[:, :], op=ALU.mult),
             reads=["DIFF"], writes=["SQ"])
        def post2():
            P.op("pe", lambda e: e.matmul(psb[2][:, 0:256], lhsT=ONESM[:, :], rhs=SQ[:, :], start=True, stop=True),
                 reads=["SQ", "ONESM"], writes=["psb2"])
            P.op("act", lambda e: e.activation(out=R2[:, :], in_=psb[2][:, 0:256], func=AF.Ln, bias=1e-5),
                 reads=["psb2"], writes=["R2"])
            P.op("act", lambda e: e.activation(out=R2[:, :], in_=R2[:, :], func=AF.Exp, scale=-0.5),
                 reads=["R2"], writes=["R2"])
            P.op("dve", lambda e: e.tensor_tensor(out=T1[:, :], in0=DIFF[:, :], in1=R2[:, :], op=ALU.mult),
                 reads=["DIFF", "R2"], writes=["T1x"])
            P.op("dve", lambda e: e.scalar_tensor_tensor(
                out=MIX[mb][:, hh, qs * 256:(qs + 1) * 256], in0=T1[:, :], scalar=SM[:, 49:50], in1=SGA[:, hh, q0:q0 + 256],
                op0=ALU.mult, op1=ALU.mult), reads=["T1x", "SM_gs", "SGA"], writes=["MIX%d" % mb])
        pend_post.append(post2)
        while pend_fin2:
            pend_fin2.pop(0)()

    def outproj_prep_mixg(i):
        gb = BSEQ[i] % 2
        P.dma(lambda e: [e.dma_start(out=MIXG[gb][:, 0:8, :], in_=agout[i].rearrange("(c p) t -> p c t", p=128)),
                         e.dma_start(out=MIXG[gb][:, 8:16, :], in_=agoutl[i].rearrange("(c p) t -> p c t", p=128))],
              "d_mg%d" % gb, 2, reads=["AGOUT%d" % i, "AGOUTL%d" % i], writes=["MIXG%d" % gb])

    def outproj_prep(i):
        P.dma(lambda e: [e.dma_start(out=XRES[:, :, :], in_=x_res[512 * i:512 * (i + 1), :].rearrange("(b p) n -> p b n", p=128))],
              "d_xr", 1, writes=["XRES"])
        P.op("dve", lambda e: e.memset(SSQ[:, :], 0.0), writes=["SSQ"])

    def outproj_units(i):
        gb = BSEQ[i] % 2
        yb = BSEQ[i] % 2
        for tb in range(4):
            for c in range(16):
                (P.op if c == 15 else P.op_noinc)("pe", lambda e, tb=tb, c=c: e.matmul(
                    psb[3][:, :], lhsT=MIXG[gb][:, c, tb * 128:(tb + 1) * 128], rhs=WO[:, c, :], start=(c == 0), stop=(c == 15)),
                    reads=["MIXG%d" % gb, "WO"], writes=["psb3"])
                if c == 15:
                    P.op("dve", lambda e, tb=tb: e.tensor_tensor(out=YB[yb][:, tb, :], in0=psb[3][:, :], in1=XRES[:, tb, :], op=ALU.add),
                         reads=["psb3", "XRES"], writes=["YB%d_%d" % (yb, tb)])
                    P.op("act", lambda e, tb=tb: e.activation(out=JUNK[:, :], in_=YB[yb][:, tb, :], func=AF.Square,
                                                              accum_out=SSQ[:, tb:tb + 1]),
                         reads=["YB%d_%d" % (yb, tb)], writes=["JUNK", "SSQ"])
                yield
        P.dma(lambda e: [e.dma_start(out=sqin[i], in_=SSQ[:, :])], "d_sq", 1, reads=["SSQ"], writes=["SQIN%d" % i])
        P.cc(lambda e: e.collective_compute("AllGather", ALU.bypass, replica_groups=[[0, 1, 2, 3], [4, 5, 6, 7]],
                                            ins=[sqin[i]], outs=[sqout[i]]),
             "cc_b", reads=["SQIN%d" % i], writes=["SQOUT%d" % i])

    def finalize1(i):
        P.dma(lambda e: [e.dma_start(out=SQG[:, :, :], in_=sqout[i].rearrange("(r p) b -> p r b", p=128))],
              "d_sqg", 1, reads=["SQOUT%d" % i], writes=["SQG"])
        P.op("dve", lambda e: e.tensor_tensor(out=TOT[:, :], in0=SQG[:, 0, :], in1=SQG[:, 1, :], op=ALU.add),
             reads=["SQG"], writes=["TOT"])
        P.op("dve", lambda e: e.tensor_tensor(out=TOT[:, :], in0=TOT[:, :], in1=SQG[:, 2, :], op=ALU.add),
             reads=["SQG", "TOT"], writes=["TOT"])
        P.op("dve", lambda e: e.tensor_tensor(out=TOT[:, :], in0=TOT[:, :], in1=SQG[:, 3, :], op=ALU.add),
             reads=["SQG", "TOT"], writes=["TOT"])

    def finalize2(i):
        yb = BSEQ[i] % 2
        P.op("act", lambda e: e.activation(out=RS[:, :], in_=TOT[:, :], func=AF.Ln, scale=1.0 / 2048.0, bias=1e-6),
             reads=["TOT"], writes=["RS"])
        P.op("act", lambda e: e.activation(out=RS[:, :], in_=RS[:, :], func=AF.Exp, scale=-0.5), reads=["RS"], writes=["RS"])
        for tb in range(4):
            P.op("dve", lambda e, tb=tb: e.scalar_tensor_tensor(
                out=YB[yb][:, tb, :], in0=YB[yb][:, tb, :], scalar=RS[:, tb:tb + 1], in1=FG[:, :], op0=ALU.mult, op1=ALU.mult),
                reads=["YB%d_%d" % (yb, tb), "RS", "FG"], writes=["YB%d_%d" % (yb, tb)])
        P.dma(lambda e: [e.dma_start(out=out[512 * i:512 * (i + 1), :].rearrange("(b p) n -> p b n", p=128), in_=YB[yb][:, :, :])],
              "d_out%d" % yb, 1, reads=["YB%d_%d" % (yb, tb) for tb in range(4)], writes=["OUT%d" % i])

    for ai in PROC:
        attention(ai)
    while pend_post:
        pend_post.pop(0)()
    while pend_B or SCH["inj"] is not None:
        if SCH["inj"] is None:
            k_, r_ = pend_B.pop(0)
            SCH["t"] = max(SCH["t"], r_)
            start_B(k_)
            SCH["t"] = max(SCH["t"], SCH["inj_ready"])
        if pend_B and pend_B[0][1] <= SCH["t"] + 16.0:
            prefetch_mixg(pend_B[0][0])
        run_units(64)
    while pend_F:
        k_, _r = pend_F.pop(0)
        if k_ not in FIN_DONE:
            do_finalize(k_, split=False)
    if DEBUG:
        P.dma(lambda e: [e.dma_start(out=dbg["ag0"], in_=agout[0])], "d_dbg2", 1, reads=["AGOUT0"], writes=["DBG2"])
    P.barrier()

    with ExitStack() as es:
        sems = {}
        for sname in P.cnt.keys():
            sems[sname] = es.enter_context(nc.semaphore("s_" + sname))
        block = es.enter_context(nc.Block())

        def run(e, eng):
            for waits, fn, sem, inc in P.ops[eng]:
                for (s_, v) in waits:
                    e.wait_ge(sems[s_], v)
                if fn is None:
                    continue
                r = fn(e)
                if inc == 0:
                    continue
                if isinstance(r, (list, tuple)):
                    for ins in r:
                        ins.then_inc(sems[sem], inc)
                else:
                    r.then_inc(sems[sem], inc)

        block.sync(lambda e: run(e, "sp"))
        block.tensor(lambda e: run(e, "pe"))
        block.scalar(lambda e: run(e, "act"))
        block.vector(lambda e: run(e, "dve"))
        block.gpsimd(lambda e: run(e, "pool"))
    return nc


def _bucket(dist):
    d = np.maximum(dist, 0).astype(np.int64)
    large = 16 + (np.log(np.maximum(d, 1).astype(np.float32) / np.float32(16.0)) / np.float32(math.log(128 / 16)) * np.float32(16)).astype(np.int32)
    large = np.minimum(large, 31)
    return np.where(d < 16, d, large).astype(np.int64)


def _bias_tiles(rel_bias, h):
    p = np.arange(128)[:, None]
    c = np.arange(256)[None, :]
    tiles = np.zeros((4, 128, 256), np.float32)
    d0 = 16 + c - p
    tiles[0] = rel_bias[_bucket(d0), h]
    d1 = c + 128 - p
    tiles[1] = rel_bias[_bucket(d1), h]
    d2 = c - p
    tiles[2] = np.where(d2 >= 0, rel_bias[_bucket(d2), h], np.float32(MASKV))
    d3 = c - 128 - p
    tiles[3] = np.where(d3 >= 0, rel_bias[_bucket(d3), h], np.float32(MASKV))
    return tiles


def _prep_inputs(x, meta_tokens, rel_bias, norm_g, w_in, conv_w, conv_b, w_a, b_a, w_x, b_x,
                 lru_lambda, lam_q1, lam_k1, lam_q2, lam_k2, subln_g, w_out, final_g):
    f = lambda a: np.asarray(a, dtype=np.float32)
    x, meta_tokens, rel_bias, norm_g, w_in = f(x), f(meta_tokens), f(rel_bias), f(norm_g), f(w_in)
    conv_w, conv_b, w_a, b_a, w_x, b_x = f(conv_w), f(conv_b), f(w_a), f(b_a), f(w_x), f(b_x)
    lru_lambda, subln_g, w_out, final_g = f(lru_lambda), f(subln_g), f(w_out), f(final_g)
    lamv = np.concatenate([f(lam_q1)[0], f(lam_k1)[0], f(lam_q2)[0], f(lam_k2)[0]])
    lamv = np.ascontiguousarray(np.broadcast_to(lamv[None, :], (128, 256)))
    ident = np.eye(128, dtype=np.float32)
    xTs = [np.ascontiguousarray(np.concatenate([meta_tokens, x[b]], axis=0).T) for b in range(2)]
    in_maps = []
    pidx = np.arange(128)
    for c in range(8):
        b, j = divmod(c, 4)
        hs = [2 * j, 2 * j + 1]
        cols = []
        for base in (0, 1024, 2048, 3072, 4096, 5120):
            for h in hs:
                cols.append(np.arange(base + h * 128, base + (h + 1) * 128))
        cols = np.concatenate(cols)
        w_in_c = np.ascontiguousarray(w_in[0][:, cols])
        w_out_c = np.ascontiguousarray(w_out[0][:, 512 * j:512 * (j + 1)])
        x_res = np.ascontiguousarray(x[b][:, 512 * j:512 * (j + 1)])
        fg = np.ascontiguousarray(np.broadcast_to(final_g[None, 512 * j:512 * (j + 1)], (128, 512)))
        sm = np.zeros((128, 40), np.float32)
        sm[:, 0:16] = norm_g[0].reshape(16, 128).T
        for bl in range(2):
            ch = hs[bl] * 128 + pidx
            for k in range(4):
                sm[:, 16 + 4 * bl + k] = conv_w[0][k, ch]
            sm[:, 24 + bl] = conv_b[0][ch]
            sm[:, 26 + bl] = b_a[0][ch]
            sm[:, 28 + bl] = b_x[0][ch]
            sm[:, 30 + bl] = lru_lambda[0][ch]
        sm[:, 32] = subln_g[0]
        sm[:, 33] = rel_bias[31, hs[0]]
        sm[:, 34] = rel_bias[31, hs[1]]
        wax = np.zeros((128, 4, 128), np.float32)
        for bl in range(2):
            wax[:, 2 * bl + 0, :] = w_a[0][hs[bl]]
            wax[:, 2 * bl + 1, :] = w_x[0][hs[bl]]
        bt = np.zeros((128, 2, 4, 256), np.float32)
        for hh in range(2):
            bt[:, hh] = np.transpose(_bias_tiles(rel_bias, hs[hh]), (1, 0, 2))
        in_maps.append({
            "xT": xTs[b], "w_in": w_in_c, "w_out": w_out_c, "x_res": x_res, "fg": fg, "smalls": sm,
            "lamv": lamv, "wax": np.ascontiguousarray(wax.reshape(128, 512)),
            "biast": np.ascontiguousarray(bt.reshape(128, 2048)), "ident": ident,
        })
    return in_maps


def kernel(**inputs):
    in_maps = _prep_inputs(**inputs)
    nc = build_program()
    res = run_bass_kernel_spmd(nc, in_maps, core_ids=list(range(8)))
    outp = np.zeros((2, S, D), np.float32)
    for c in range(8):
        b, j = divmod(c, 4)
        outp[b, :, 512 * j:512 * (j + 1)] = np.asarray(res.results[c]["out"], dtype=np.float32)
    return outp
```
